# Optimizing a Trainium2 kernel written in Bass

```python
import math
import jax, jax.numpy as jnp
from jax import lax
import numpy as np

D_MODEL = 1024
BATCH = 4
SEQ = 4096
DEPTH = 4

N_A = DEPTH // 2
N_B = DEPTH - N_A
TOK_WIDTH = 3 * D_MODEL // 4
MEM_HEADS = 4
MEM_HEAD_DIM = 64
MEM_WIDTH = MEM_HEADS * MEM_HEAD_DIM
MIX_WIDTH = TOK_WIDTH + MEM_WIDTH
N_MEM = 256
S5_GROUP = 16
S5_GROUPS = TOK_WIDTH // S5_GROUP
S5_STATE = 64
DT_MIN, DT_MAX = 1e-3, 1e-1
DIFF_HEAD_DIM = 64
DIFF_HEADS = TOK_WIDTH // (2 * DIFF_HEAD_DIM)
DIFF_V_DIM = 2 * DIFF_HEAD_DIM
QK_WIDTH = DIFF_HEADS * 2 * DIFF_HEAD_DIM
V_WIDTH = DIFF_HEADS * DIFF_V_DIM
ROT_DIM = DIFF_HEAD_DIM // 4
ROPE_THETA = 500000.0
Q_BLOCK = 128
D_FF = 2816
EPS = 1e-6

kernel_name = 'yoco_s5_diffattn_macaron_memory'


def rms_norm(x, g):
    xf = x.astype(jnp.float32)
    y = xf * lax.rsqrt(jnp.mean(xf * xf, axis=-1, keepdims=True) + EPS)
    return (y * g.astype(jnp.float32)).astype(x.dtype)


def swiglu(h, w_in, w_out):
    gu = h @ w_in
    return (jax.nn.silu(gu[..., :D_FF]) * gu[..., D_FF:]) @ w_out


def rope_tables(positions):
    inv = ROPE_THETA ** (-jnp.arange(0, ROT_DIM, 2, dtype=jnp.float32) / ROT_DIM)
    ang = positions.astype(jnp.float32)[..., None] * inv
    return jnp.cos(ang), jnp.sin(ang)


def partial_rope(t, cos, sin):
    half = ROT_DIM // 2
    c = cos[:, :, None, None, :]
    s = sin[:, :, None, None, :]
    t1 = t[..., :half].astype(jnp.float32)
    t2 = t[..., half:ROT_DIM].astype(jnp.float32)
    r = jnp.concatenate([t1 * c - t2 * s, t2 * c + t1 * s], axis=-1).astype(t.dtype)
    return jnp.concatenate([r, t[..., ROT_DIM:]], axis=-1)


def _ssm_combine(e1, e2):
    a1r, a1i, b1r, b1i = e1
    a2r, a2i, b2r, b2i = e2
    ar = a1r * a2r - a1i * a2i
    ai = a1r * a2i + a1i * a2r
    br = a2r * b1r - a2i * b1i + b2r
    bi = a2r * b1i + a2i * b1r + b2i
    return (ar, ai, br, bi)


def s5_mixer(u, a_re, a_im, log_dt, b_re, b_im, c_re, c_im, d, w_glu):
    f32 = jnp.float32
    bsz, seq, _ = u.shape
    ug = u.astype(f32).reshape(bsz, seq, S5_GROUPS, S5_GROUP)
    dt = jnp.exp(log_dt.astype(f32))[:, None]
    lr, li = a_re.astype(f32), a_im.astype(f32)
    mag = jnp.exp(lr * dt)
    abr, abi = mag * jnp.cos(li * dt), mag * jnp.sin(li * dt)
    den = lr * lr + li * li
    nr, ni = abr - 1.0, abi
    fr = (nr * lr + ni * li) / den
    fi = (ni * lr - nr * li) / den
    br, bi = b_re.astype(f32), b_im.astype(f32)
    bbr = fr[..., None] * br - fi[..., None] * bi
    bbi = fr[..., None] * bi + fi[..., None] * br
    bu_r = jnp.einsum('bsgc,gpc->bsgp', ug, bbr)
    bu_i = jnp.einsum('bsgc,gpc->bsgp', ug, bbi)
    a_r = jnp.broadcast_to(abr, (1, seq) + abr.shape)
    a_i = jnp.broadcast_to(abi, (1, seq) + abi.shape)
    _, _, hr, hi = lax.associative_scan(_ssm_combine, (a_r, a_i, bu_r, bu_i), axis=1)
    y = (jnp.einsum('bsgp,gcp->bsgc', hr, c_re.astype(f32))
         - jnp.einsum('bsgp,gcp->bsgc', hi, c_im.astype(f32))
         + d.astype(f32) * ug)
    y = jax.nn.gelu(y.reshape(bsz, seq, TOK_WIDTH))
    y = y * jax.nn.sigmoid(y @ w_glu.astype(f32))
    return y.astype(u.dtype)


def diff_attention(q, k, v, lam):
    bsz, seq = q.shape[0], q.shape[1]
    nblk = seq // Q_BLOCK
    scale = DIFF_HEAD_DIM ** -0.5
    qb = q.reshape(bsz, nblk, Q_BLOCK, DIFF_HEADS, 2, DIFF_HEAD_DIM).transpose(1, 0, 2, 3, 4, 5)
    kidx = jnp.arange(seq)

    def one_block(args):
        qblk, blk = args
        s = jnp.einsum('bqhcd,bkhcd->bhcqk', qblk, k, preferred_element_type=jnp.float32) * scale
        qidx = blk * Q_BLOCK + jnp.arange(Q_BLOCK)
        mask = kidx[None, :] <= qidx[:, None]
        s = jnp.where(mask, s, -jnp.inf)
        p = jax.nn.softmax(s, axis=-1)
        a = p[:, :, 0] - lam * p[:, :, 1]
        return jnp.einsum('bhqk,bkhe->bqhe', a.astype(v.dtype), v)

    o = lax.map(one_block, (qb, jnp.arange(nblk)))
    return o.transpose(1, 0, 2, 3, 4).reshape(bsz, seq, DIFF_HEADS, DIFF_V_DIM)


def memory_attention(q, mk, mv):
    s = jnp.einsum('bshd,bmhd->bhsm', q, mk, preferred_element_type=jnp.float32) * MEM_HEAD_DIM ** -0.5
    p = jax.nn.softmax(s, axis=-1)
    return jnp.einsum('bhsm,bmhd->bshd', p.astype(mv.dtype), mv)


def setup_inputs(seed: int = 0) -> dict:
    key = jax.random.key(seed)
    ks = iter(jax.random.split(key, 48))
    f32 = jnp.float32

    def nrm(shape, scale):
        return jax.random.normal(next(ks), shape, f32) * scale

    def gain(shape):
        return 1.0 + nrm(shape, 0.02)

    x = jax.random.normal(next(ks), (BATCH, SEQ, D_MODEL), f32)
    mem = jax.random.normal(next(ks), (BATCH, N_MEM, D_MODEL), f32)
    offset = jax.random.randint(next(ks), (BATCH, 1), 0, 1024, dtype=jnp.int32)
    positions = offset + jnp.arange(SEQ, dtype=jnp.int32)[None, :]
    n_idx = jnp.arange(S5_STATE, dtype=f32)
    return {
        'x': x,
        'mem': mem,
        'positions': positions,
        'ln_ffn1': gain((DEPTH, D_MODEL)),
        'ffn1_in': nrm((DEPTH, D_MODEL, 2 * D_FF), D_MODEL ** -0.5),
        'ffn1_out': nrm((DEPTH, D_FF, D_MODEL), D_FF ** -0.5),
        'ln_mix': gain((DEPTH, D_MODEL)),
        'w_mix_in': nrm((DEPTH, D_MODEL, MIX_WIDTH), D_MODEL ** -0.5),
        'w_mix_out': nrm((DEPTH, MIX_WIDTH, D_MODEL), MIX_WIDTH ** -0.5),
        'ln_mem': gain((D_MODEL,)),
        'w_mem_kv': nrm((DEPTH, D_MODEL, 2 * MEM_WIDTH), D_MODEL ** -0.5),
        'ln_ffn2': gain((DEPTH, D_MODEL)),
        'ffn2_in': nrm((DEPTH, D_MODEL, 2 * D_FF), D_MODEL ** -0.5),
        'ffn2_out': nrm((DEPTH, D_FF, D_MODEL), D_FF ** -0.5),
        's5_a_re': -0.5 + nrm((N_A, S5_GROUPS, S5_STATE), 0.01),
        's5_a_im': math.pi * n_idx + nrm((N_A, S5_GROUPS, S5_STATE), 0.01),
        's5_log_dt': jax.random.uniform(next(ks), (N_A, S5_GROUPS), f32, math.log(DT_MIN), math.log(DT_MAX)),
        's5_b_re': nrm((N_A, S5_GROUPS, S5_STATE, S5_GROUP), (2 * S5_GROUP) ** -0.5),
        's5_b_im': nrm((N_A, S5_GROUPS, S5_STATE, S5_GROUP), (2 * S5_GROUP) ** -0.5),
        's5_c_re': nrm((N_A, S5_GROUPS, S5_GROUP, S5_STATE), (2 * S5_STATE) ** -0.5),
        's5_c_im': nrm((N_A, S5_GROUPS, S5_GROUP, S5_STATE), (2 * S5_STATE) ** -0.5),
        's5_d': nrm((N_A, S5_GROUPS, S5_GROUP), 1.0),
        's5_w_glu': nrm((N_A, TOK_WIDTH, TOK_WIDTH), TOK_WIDTH ** -0.5),
        'ln_kv': gain((D_MODEL,)),
        'w_kv_shared': nrm((D_MODEL, QK_WIDTH + V_WIDTH), D_MODEL ** -0.5),
        'diff_lq1': nrm((N_B, DIFF_HEAD_DIM), 0.1),
        'diff_lk1': nrm((N_B, DIFF_HEAD_DIM), 0.1),
        'diff_lq2': nrm((N_B, DIFF_HEAD_DIM), 0.1),
        'diff_lk2': nrm((N_B, DIFF_HEAD_DIM), 0.1),
        'diff_subln': gain((N_B, DIFF_V_DIM)),
        'ln_final': gain((D_MODEL,)),
    }


def reference(x, mem, positions, ln_ffn1, ffn1_in, ffn1_out, ln_mix, w_mix_in, w_mix_out,
              ln_mem, w_mem_kv, ln_ffn2, ffn2_in, ffn2_out,
              s5_a_re, s5_a_im, s5_log_dt, s5_b_re, s5_b_im, s5_c_re, s5_c_im, s5_d, s5_w_glu,
              ln_kv, w_kv_shared, diff_lq1, diff_lk1, diff_lq2, diff_lk2, diff_subln, ln_final):
    bsz, seq, _ = x.shape
    n_mem = mem.shape[1]
    cos, sin = rope_tables(positions)
    mem_n = rms_norm(mem, ln_mem)
    k_sh = None
    v_sh = None
    for i in range(DEPTH):
        if i == N_A:
            hk = rms_norm(x, ln_kv)
            kv = hk @ w_kv_shared
            k_sh = partial_rope(kv[..., :QK_WIDTH].reshape(bsz, seq, DIFF_HEADS, 2, DIFF_HEAD_DIM), cos, sin)
            v_sh = kv[..., QK_WIDTH:].reshape(bsz, seq, DIFF_HEADS, DIFF_V_DIM)
        x = x + 0.5 * swiglu(rms_norm(x, ln_ffn1[i]), ffn1_in[i], ffn1_out[i])
        h = rms_norm(x, ln_mix[i])
        proj = h @ w_mix_in[i]
        tok_in = proj[..., :TOK_WIDTH]
        mq = proj[..., TOK_WIDTH:].reshape(bsz, seq, MEM_HEADS, MEM_HEAD_DIM)
        if i < N_A:
            j = i
            tok_out = s5_mixer(tok_in, s5_a_re[j], s5_a_im[j], s5_log_dt[j], s5_b_re[j], s5_b_im[j],
                               s5_c_re[j], s5_c_im[j], s5_d[j], s5_w_glu[j])
        else:
            j = i - N_A
            q = partial_rope(tok_in.reshape(bsz, seq, DIFF_HEADS, 2, DIFF_HEAD_DIM), cos, sin)
            lam_init = 0.8 - 0.6 * math.exp(-0.3 * i)
            lam = (jnp.exp(jnp.sum(diff_lq1[j].astype(jnp.float32) * diff_lk1[j].astype(jnp.float32)))
                   - jnp.exp(jnp.sum(diff_lq2[j].astype(jnp.float32) * diff_lk2[j].astype(jnp.float32)))
                   + lam_init)
            o = diff_attention(q, k_sh, v_sh, lam)
            o = rms_norm(o, diff_subln[j]) * (1.0 - lam_init)
            tok_out = o.reshape(bsz, seq, TOK_WIDTH)
        mkv = mem_n @ w_mem_kv[i]
        mk = mkv[..., :MEM_WIDTH].reshape(bsz, n_mem, MEM_HEADS, MEM_HEAD_DIM)
        mv = mkv[..., MEM_WIDTH:].reshape(bsz, n_mem, MEM_HEADS, MEM_HEAD_DIM)
        mo = memory_attention(mq, mk, mv).reshape(bsz, seq, MEM_WIDTH)
        x = x + jnp.concatenate([tok_out, mo], axis=-1) @ w_mix_out[i]
        x = x + 0.5 * swiglu(rms_norm(x, ln_ffn2[i]), ffn2_in[i], ffn2_out[i])
    return rms_norm(x, ln_final)
```

```python
import math
from contextlib import ExitStack

import numpy as np
import ml_dtypes

import concourse.bass as bass
import concourse.mybir as mybir
from concourse.bass_utils import run_bass_kernel_spmd

F32 = mybir.dt.float32
BF16 = mybir.dt.bfloat16
I32 = mybir.dt.int32
ALU = mybir.AluOpType
AF = mybir.ActivationFunctionType

D = 1024
S = 4096
NB = 4
DFF = 2816
NFC = DFF // 128
TT = 1024
NST = TT // 512
KCH = TT // 8
NG = 48
EPS = 1e-6
PI = math.pi
C1 = 6.28125
C2 = 2.0 * PI - C1
SLOT = 3072
NSLOT = 6
LAM_INIT = [0.8 - 0.6 * math.exp(-0.3 * i) for i in range(4)]


class Tok:
    __slots__ = ("sem", "val", "key")

    def __init__(self, sem, val, key):
        self.sem, self.val, self.key = sem, val, key


class Buf:
    __slots__ = ("w", "r", "name")

    def __init__(self, name=""):
        self.w = None
        self.r = {}
        self.name = name


class Eng:
    def __init__(self, h, sem, key):
        self.h, self.sem, self.key = h, sem, key
        self.count = 0
        self.waited = {}
        self.pending = []

    def wait(self, toks):
        for t in toks:
            if t is None:
                continue
            if t.key == self.key and self.key == "pe":
                continue
            assert t.val is not None, "waiting on unresolved token"
            if self.waited.get(t.key, 0) < t.val:
                self.h.wait_ge(t.sem, t.val)
                self.waited[t.key] = t.val


class Ctx:
    def __init__(self, nc, es):
        self.nc = nc
        self.es = es
        mk = lambda n: es.enter_context(nc.semaphore(n))
        self.pe = Eng(nc.tensor, mk("s_pe"), "pe")
        self.act = Eng(nc.scalar, mk("s_act"), "act")
        self.dve = Eng(nc.vector, mk("s_dve"), "dve")
        self.pool = Eng(nc.gpsimd, mk("s_pool"), "pool")
        self.sp = Eng(nc.sync, mk("s_sp"), "sp")
        self.dsems = [(mk(f"s_d{i}"), f"d{i}") for i in range(24)]
        self.dcnt = [0] * len(self.dsems)
        self.dlast = [None] * len(self.dsems)
        self.dnext = 0

    def deps(self, reads, writes):
        toks = []
        for b in reads:
            if b.w is not None:
                toks.append(b.w)
        for b in writes:
            if b.w is not None:
                toks.append(b.w)
            toks.extend(b.r.values())
        return toks

    def op(self, eng, fn, reads=(), writes=(), signal=True):
        eng.wait(self.deps(reads, writes))
        ins = fn()
        if signal:
            eng.count += 1
            ins.then_inc(eng.sem, 1)
            tok = Tok(eng.sem, eng.count, eng.key)
            for p in eng.pending:
                p.val = eng.count
            eng.pending = []
        else:
            tok = Tok(eng.sem, None, eng.key)
            eng.pending.append(tok)
        for b in reads:
            b.r[eng.key] = tok
        for b in writes:
            b.w = tok
            b.r = {}
        return tok

    def dma(self, q, out, in_, reads=(), writes=(), semidx=None):
        if semidx is None:
            semidx = self.dnext
            self.dnext = (self.dnext + 1) % 16
        sem, key = self.dsems[semidx]
        q.wait([self.dlast[semidx]])
        q.wait(self.deps(reads, writes))
        self.dcnt[semidx] += 1
        q.h.dma_start(out=out, in_=in_).then_inc(sem, 16)
        tok = Tok(sem, 16 * self.dcnt[semidx], key)
        self.dlast[semidx] = tok
        for b in reads:
            b.r[key] = tok
        for b in writes:
            b.w = tok
            b.r = {}
        return tok


def build_program(na_tiles=4, nb_tiles=2, dbg=None):
    nc = bass.Bass("TRN2", target_bir_lowering=False)
    es = ExitStack()
    with es:
        _build(nc, es, na_tiles, nb_tiles, dbg)
    return nc


def _build(nc, es, na_tiles, nb_tiles, dbg):
    cx = Ctx(nc, es)
    PE, ACT, DVE, POOL, SP = cx.pe, cx.act, cx.dve, cx.pool, cx.sp

    def din(name, shape, dt=F32):
        return nc.dram_tensor(name, list(shape), dt, kind="ExternalInput").ap()

    def dscr(name, shape, dt):
        return nc.dram_tensor(name, list(shape), dt, kind="Internal").ap()

    def sb(name, shape, dt):
        return es.enter_context(nc.sbuf_tensor("sb_" + name, list(shape), dt))

    xT_d = din("xT", [D, S])
    memT_d = din("memT", [D, 256])
    pos_all_d = din("pos_all", [128, S], I32)
    pos_mine_d = din("pos_mine", [128, S // 2], I32)
    masks_d = din("masks", [128, 8, 128], BF16)
    flags_d = din("flags", [128, 2])
    gains_d = din("gains", [128, 15, 8])
    ffn_in_d = [din("ffn1_in", [4, D, 2 * DFF]), din("ffn2_in", [4, D, 2 * DFF])]
    ffn_out_d = [din("ffn1_out", [4, DFF, D]), din("ffn2_out", [4, DFF, D])]
    wmi_d = din("w_mix_in", [4, D, D])
    wmo_d = din("w_mix_out", [4, D, D])
    wmkv_d = din("w_mem_kv", [4, D, 512])
    wglu_d = din("s5_w_glu", [2, 768, 768])
    wkv_d = din("w_kv_shared", [D, 1536])
    s5a_d = din("s5a", [128, 2, 3, NG])
    s5b_d = din("s5b", [128, 2, 2, NG, 16])
    s5c_d = din("s5c", [128, 2, 2, NG, 16])
    s5d_d = din("s5d", [128, 2, NG])
    lqk_d = din("lqk", [128, 2, 4, 64])
    subln_d = din("subln", [128, 2])
    cst_d = din("cst", [128, 16])
    ew_d = din("ew", [128, 8, 240], BF16)
    cm_d = din("cm", [128, 5, 128])
    cmb_d = din("cmb", [128, 2, 128], BF16)
    tabc_d = din("tabc", [128, 2, 64])
    outT_d = nc.dram_tensor("outT", [D, S // 2], F32, kind="ExternalOutput").ap()
    dbgs = {}
    for d_ in (dbg or []):
        dbgs[(d_[0], d_[1])] = nc.dram_tensor(f"dbg_{d_[0]}_{d_[1]}", [128, d_[2]], BF16 if d_[0].startswith("tokout") else F32, kind="ExternalOutput").ap()

    kT_scr = dscr("kT_scr", [768, S], BF16)
    v_scr = dscr("v_scr", [S, 768], BF16)
    x1_scr = dscr("x1_scr", [D, S // 2], F32)
    s5w_scr = dscr("s5w_scr", [2, 4, 128, NG, 128], BF16)
    rot_scr = dscr("rot_scr", [2, 2, 128, NG, KCH], F32)

    xT = sb("xT", [128, 8, TT], F32)
    hT = sb("hT", [128, 8, TT], BF16)
    act = sb("act", [128, NFC, TT], BF16)
    sq = sb("sq", [128, 8, 512], BF16)
    sq2 = sb("sq2", [128, 8, 512], BF16)
    rstd = sb("rstd", [128, 2, 512], F32)
    uT = act[:, 0:6, :]
    mqT = act[:, 6:8, :]
    cat = act[:, 8:16, :]
    tmpB = act[:, 16:22, :]
    ring = sb("ring", [128, NSLOT, SLOT], BF16)
    ew = sb("ew", [128, 8, 240], BF16)
    cm = sb("cm", [128, 5, 128], F32)
    cmb = sb("cmb", [128, 2, 128], BF16)
    cst = sb("cst", [128, 16], F32)
    tabc = sb("tabc", [128, 2, 64], F32)
    gains = sb("gains", [128, 15, 8], F32)
    flags = sb("flags", [128, 2], F32)
    masks = sb("masks", [128, 8, 128], BF16)
    ropeT = sb("ropeT", [128, 2, TT], F32)
    mkT = sb("mkT", [128, 4, 2, 256], BF16)
    mvv = sb("mvv", [128, 4, 2, 256], BF16)
    carry = sb("carry", [128, 2, NG], F32)
    r8 = sb("r8", [128, 2, NG], F32)
    lam = sb("lam", [128, 2, 4], F32)
    subln = sb("subln", [128, 2], F32)
    tmpA = sb("tmpA", [128, 4, 1024], F32)
    kvbuf = hT[:].rearrange("p k t -> p (k t)").rearrange("p (a n) -> p a n", a=2)
    kvbuf2 = sb("kvbuf2", [128, 2, 4096], BF16)

    ones_b = cmb[:, 0, :]
    perm_b = cmb[:, 1, :]
    ident_f = cm[:, 0, :]
    jmat_f = cm[:, 1, :]
    maskT_f = cm[:, 2, :]
    jv_f = cm[:, 3, :]
    ones_f = cm[:, 4, :]

    psum = [es.enter_context(nc.psum_tensor(f"ps{i}", [128, 512], F32)) for i in range(8)]
    ps_b = [Buf(f"ps{i}") for i in range(8)]
    ps_state = {"n": 0, "held": set()}

    def next_ps(hold=False):
        for _ in range(17):
            i = ps_state["n"]
            ps_state["n"] = (i + 1) % 8
            if i not in ps_state["held"] and not (ps_b[i].w is not None and not ps_b[i].r):
                break
        else:
            raise RuntimeError("no free PSUM bank")
        if hold:
            ps_state["held"].add(i)
        return psum[i], ps_b[i]

    def release_ps(pb):
        ps_state["held"].discard(ps_b.index(pb))

    B = {}

    def buf(name):
        if name not in B:
            B[name] = Buf(name)
        return B[name]

    slot_b = [Buf(f"slot{i}") for i in range(NSLOT)]
    wstate = {"n": 0}

    def wload(src_ap, shape, f32=False):
        n = wstate["n"]
        wstate["n"] = n + 1
        i = n % NSLOT
        nel = 1
        for s_ in shape[1:]:
            nel *= s_
        if f32:
            assert 2 * nel <= SLOT
            dst = ring[:, i, 0:2 * nel].bitcast(F32)
        else:
            assert nel <= SLOT, (shape, nel)
            dst = ring[:, i, 0:nel]
        if len(shape) == 3:
            dst = dst.rearrange("p (a b) -> p a b", b=shape[2])
        elif len(shape) == 4:
            dst = dst.rearrange("p (a b c) -> p a b c", b=shape[2], c=shape[3])
        cx.dma(POOL, dst, src_ap, writes=[slot_b[i]], semidx=16 + i)
        return dst, slot_b[i]

    def mm(out, lhsT, rhs, start, stop, reads, writes):
        return cx.op(PE, lambda: nc.tensor.matmul(out, lhsT, rhs, start=start, stop=stop),
                     reads=reads, writes=writes, signal=stop)

    def act_op(out, in_, func, reads, writes, **kw):
        return cx.op(ACT, lambda: nc.scalar.activation(out=out, in_=in_, func=func, **kw),
                     reads=reads, writes=writes)

    def dve(fn, reads, writes):
        return cx.op(DVE, fn, reads=reads, writes=writes)

    def tt(out, a, b, op, reads, writes):
        return dve(lambda: nc.vector.tensor_tensor(out=out, in0=a, in1=b, op=op), reads, writes)

    def ts(out, a, s1, s2, op0, op1, reads, writes):
        if s2 is None:
            return dve(lambda: nc.vector.tensor_scalar(out=out, in0=a, scalar1=s1, scalar2=None, op0=op0), reads, writes)
        return dve(lambda: nc.vector.tensor_scalar(out=out, in0=a, scalar1=s1, scalar2=s2, op0=op0, op1=op1), reads, writes)

    def stt(out, a, sc, b, op0, op1, reads, writes):
        return dve(lambda: nc.vector.scalar_tensor_tensor(out=out, in0=a, scalar=sc, in1=b, op0=op0, op1=op1), reads, writes)

    def cpy(out, a, reads, writes):
        return dve(lambda: nc.vector.tensor_copy(out=out, in_=a), reads, writes)

    def evac(i, out, a, reads, writes):
        if i % 2 == 0:
            return act_op(out, a, AF.Identity, reads, writes)
        return cpy(out, a, reads, writes)

    cb = buf("consts")
    for dst, src in [(ew, ew_d), (cm, cm_d), (cmb, cmb_d), (cst, cst_d), (gains, gains_d),
                     (flags, flags_d), (masks, masks_d), (subln, subln_d), (tabc, tabc_d)]:
        cx.dma(SP, dst[:], src, writes=[cb])
    CC = {n: cst[:, i:i + 1] for i, n in enumerate(
        ["psiQ1", "psiQ2", "psiP1", "psiP2", "invc", "invs", "halfpi", "zero"])}
    tA = [buf(f"tmpA{i}") for i in range(4)]
    tBb = [buf(f"tmpB{i}") for i in range(6)]

    def sin_rr(out, ang, tmpf, tmpi, rb, wb):
        ts(tmpi, ang, 1.0 / (2 * PI), None, ALU.mult, None, rb, wb)
        cpy(tmpf, tmpi, rb, wb)
        stt(ang, tmpf, -C1, ang, ALU.mult, ALU.add, rb, wb)
        stt(ang, tmpf, -C2, ang, ALU.mult, ALU.add, rb, wb)
        ts(ang, ang, PI, -PI, ALU.min, ALU.max, rb, wb)
        act_op(out, ang, AF.Sin, rb, wb)

    def rmsnorm(src, src_b, gcol, dst, dst_b, n, par=0):
        sqb = buf("sq")
        act_op(sq[:, :, 0:n], src, AF.Square, [src_b], [sqb])
        ps, pb = next_ps()
        for k in range(8):
            mm(ps[:, 0:n], ones_b, sq[:, k, 0:n], k == 0, k == 7, [sqb, cb], [pb])
        rb = buf(f"rstd{par}")
        r = rstd[:, par, 0:n]
        act_op(r, ps[:, 0:n], AF.Sqrt, [pb], [rb], scale=1.0 / D, bias=EPS)
        dve(lambda: nc.vector.reciprocal(out=r, in_=r), [rb], [rb])
        for k in range(8):
            stt(dst[:, k, :], src[:, k, :], gcol[:, k:k + 1], r, ALU.mult, ALU.mult, [src_b, rb, cb],
                dst_b if isinstance(dst_b, list) else [dst_b])

    xb = [buf(f"x{st}") for st in range(NST)]
    hb = [buf(f"h{st}") for st in range(NST)]
    SL = [slice(st * 512, (st + 1) * 512) for st in range(NST)]

    def norm_x(gidx):
        gcol = gains[:, gidx, :]
        sqbs = [buf("sq"), buf("sq2")]
        sqs = [sq, sq2]
        pss = []
        for st in range(NST):
            act_op(sqs[st][:, :, :], xT[:, :, SL[st]], AF.Square, [xb[st]], [sqbs[st]])
        for st in range(NST):
            ps, pb = next_ps()
            for k in range(8):
                mm(ps[:], ones_b, sqs[st][:, k, :], k == 0, k == 7, [sqbs[st], cb], [pb])
            pss.append((ps, pb))
        for st in range(NST):
            act_op(rstd[:, st, :], pss[st][0][:], AF.Sqrt, [pss[st][1]], [buf(f"rstd{st}")], scale=1.0 / D, bias=EPS)
        for st in range(NST):
            r = rstd[:, st, :]
            dve(lambda: nc.vector.reciprocal(out=r, in_=r), [buf(f"rstd{st}")], [buf(f"rstd{st}")])
        for st in range(NST):
            for k in range(8):
                stt(hT[:, k, SL[st]], xT[:, k, SL[st]], gcol[:, k:k + 1], rstd[:, st, :], ALU.mult, ALU.mult,
                    [xb[st], buf(f"rstd{st}"), cb], [hb[st]])

    act_guard = []

    def ffn(layer, which):
        norm_x((0 if which == 0 else 8) + layer)
        w_in = ffn_in_d[which][layer].rearrange("(k p) n -> p k n", p=128)
        w_out = ffn_out_d[which][layer].rearrange("(f p) n -> p f n", p=128)
        actb = [[buf(f"act{j}_{st}") for st in range(NST)] for j in range(NFC)]
        if act_guard:
            DVE.wait(act_guard)
            del act_guard[:]
        for j in range(NFC):
            wg, wgb = wload(w_in[:, :, j * 128:(j + 1) * 128], [128, 8, 128])
            wu, wub = wload(w_in[:, :, DFF + j * 128: DFF + (j + 1) * 128], [128, 8, 128])
            for st in range(NST):
                sl = SL[st]
                pg, pgb = next_ps()
                for k in range(8):
                    mm(pg[:], wg[:, k, :], hT[:, k, sl], k == 0, k == 7, [wgb, hb[st]], [pgb])
                pu, pub = next_ps()
                for k in range(8):
                    mm(pu[:], wu[:, k, :], hT[:, k, sl], k == 0, k == 7, [wub, hb[st]], [pub])
                sg = tmpA[:, st, 0:512]
                act_op(sg, pg[:], AF.Silu, [pgb], [tA[st]])
                tt(act[:, j, sl], sg, pu[:], ALU.mult, [tA[st], pub], [actb[j][st]])
        for dch in range(8):
            wo, wob = wload(w_out[:, :, dch * 128:(dch + 1) * 128], [128, NFC, 128])
            for st in range(NST):
                sl = SL[st]
                po, pob = next_ps()
                for j in range(NFC):
                    mm(po[:], wo[:, j, :], act[:, j, sl], j == 0, j == NFC - 1, [wob, actb[j][st]], [pob])
                stt(xT[:, dch, sl], po[:], 0.5, xT[:, dch, sl], ALU.mult, ALU.add, [pob, xb[st]], [xb[st]])

    ub = [[buf(f"u{k}_{st}") for st in range(NST)] for k in range(6)]
    mqb = [[buf(f"mq{k}_{st}") for st in range(NST)] for k in range(2)]
    catb = [[buf(f"cat{k}_{st}") for st in range(NST)] for k in range(8)]

    def in_proj(layer):
        norm_x(4 + layer)
        w = wmi_d[layer].rearrange("(k p) n -> p k n", p=128)
        n = 0
        for cc in range(4):
            wc, wcb = wload(w[:, :, cc * 256:(cc + 1) * 256], [128, 8, 256])
            for half in range(2):
                oc = 2 * cc + half
                for st in range(NST):
                    ps, pb = next_ps()
                    for k in range(8):
                        mm(ps[:], wc[:, k, half * 128:(half + 1) * 128], hT[:, k, SL[st]], k == 0, k == 7, [wcb, hb[st]], [pb])
                    if oc < 6:
                        evac(n, uT[:, oc, SL[st]], ps[:], [pb], [ub[oc][st]])
                    else:
                        evac(n, mqT[:, oc - 6, SL[st]], ps[:], [pb], [mqb[oc - 6][st]])
                    n += 1

    def mem_attn(layer):
        steps = [(st, c, hd2, mb) for st in range(NST) for c in range(2) for hd2 in range(2) for mb in range(2)]
        PRE = 3
        sps = {}

        def issue_s(n_):
            st_, c_, hd2_, mb_ = steps[n_]
            base = hd2_ * 64
            pss, psb = next_ps()
            mm(pss[:], mkT[base:base + 64, layer, c_, mb_ * 128:(mb_ + 1) * 128], mqT[base:base + 64, c_, SL[st_]],
               True, True, [buf("mkv"), mqb[c_][st_]], [psb])
            sps[n_] = (pss, psb)

        for n_ in range(PRE):
            issue_s(n_)
        po = pl = None
        for n_, (st, c, hd2, mb) in enumerate(steps):
            if hd2 == 0 and mb == 0:
                po, pob = next_ps(hold=True)
                pl, plb = next_ps(hold=True)
            if n_ + PRE < len(steps):
                issue_s(n_ + PRE)
            base = hd2 * 64
            hd = 2 * c + hd2
            pss, psb = sps.pop(n_)
            e = tmpB[:, n_ % 6, 0:512]
            eb = tBb[n_ % 6]
            act_op(e, pss[:], AF.Exp, [psb], [eb], scale=0.125)
            mm(po[base:base + 64, :], mvv[:, layer, mb, hd * 64:(hd + 1) * 64], e, mb == 0, mb == 1, [buf("mkv"), eb], [pob])
            mm(pl[base:base + 64, :], ones_b[:, 0:64], e, mb == 0, mb == 1, [cb, eb], [plb])
            if hd2 == 1 and mb == 1:
                rl = tmpA[:, 2, 0:512]
                dve(lambda: nc.vector.reciprocal(out=rl, in_=pl[:]), [plb], [tA[2]])
                tt(cat[:, 6 + c, SL[st]], po[:], rl, ALU.mult, [pob, tA[2]], [catb[6 + c][st]])
                release_ps(pob)
                release_ps(plb)

    def out_proj(layer, srcs):
        w = wmo_d[layer].rearrange("(k p) n -> p k n", p=128)
        for cc in range(4):
            wc, wcb = wload(w[:, :, cc * 256:(cc + 1) * 256], [128, 8, 256])
            for half in range(2):
                dch = 2 * cc + half
                for st in range(NST):
                    ps, pb = next_ps()
                    for k in range(8):
                        mm(ps[:], wc[:, k, half * 128:(half + 1) * 128], srcs[k][0][:, SL[st]], k == 0, k == 7,
                           [wcb, srcs[k][1][st]], [pb])
                    tt(xT[:, dch, SL[st]], ps[:], xT[:, dch, SL[st]], ALU.add, [pb, xb[st]], [xb[st]])

    def rope_tables(pos_src):
        rb_ = buf("rope")
        allA = tA
        posi = tmpA[:, 0, :].bitcast(I32)
        cx.dma(SP, posi, pos_src, writes=[tA[0]])
        cpy(tmpA[:, 1, :], posi, [tA[0]], [tA[1]])
        for which, icol, pcol in ((0, "invc", "halfpi"), (1, "invs", "zero")):
            ts(tmpA[:, 2, :], tmpA[:, 1, :], CC[icol], CC[pcol], ALU.mult, ALU.add, [tA[1], cb], [tA[2]])
            sin_rr(ropeT[:, which, :], tmpA[:, 2, :], tmpA[:, 3, :], tmpA[:, 0, :].bitcast(I32), [tA[2], tA[3], tA[0]], [tA[2], tA[3], tA[0], rb_])

    def rope_inplace(T6, bufs6):
        rb_ = buf("rope")
        n = 0
        for ch in range(6):
            for st in range(NST):
                ps, pb = next_ps()
                mm(ps[:], perm_b, T6[:, ch, SL[st]], True, True, [cb, bufs6[ch][st]], [pb])
                i0, i1 = (n % 2) * 2, (n % 2) * 2 + 1
                n += 1
                tt(tmpA[:, i0, 0:512], ps[:], ropeT[:, 1, SL[st]], ALU.mult, [pb, rb_], [tA[i0]])
                tt(tmpA[:, i1, 0:512], T6[:, ch, SL[st]], ropeT[:, 0, SL[st]], ALU.mult, [bufs6[ch][st], rb_], [tA[i1]])
                tt(T6[:, ch, SL[st]], tmpA[:, i0, 0:512], tmpA[:, i1, 0:512], ALU.add, [tA[i0], tA[i1]], [bufs6[ch][st]])

    def s5_mixer(l):
        carb = buf(f"carry{l}")

        def stage_a(gb):
            g0 = 8 * gb
            par = gb % 2
            wP, wPb = wload(s5w_scr[l, 0:2, :, g0:g0 + 8, :].rearrange("k p g m -> p k g m"), [128, 2, 8, 128])
            Ug = tmpB[:, 3 * par, :].rearrange("p (g k) -> p g k", k=128)
            ugb = tBb[3 * par]
            uv = uT[:, gb, :].rearrange("p (k s) -> p s k", s=8)
            for half in range(2):
                ps, pb = next_ps()
                for g4 in range(4):
                    gi = half * 4 + g4
                    for s in range(8):
                        mm(ps[:, g4 * 128:(g4 + 1) * 128], ew[:, gi, 112 - 16 * s:240 - 16 * s], uv[:, s, :], s == 0, s == 7,
                           [cb, ub[gb][0], ub[gb][1]], [pb])
                evac(half, tmpB[:, 3 * par, half * 512:(half + 1) * 512], ps[:], [pb], [ugb])
            pS = [next_ps(hold=True) for _ in range(2)]
            pSp = [next_ps(hold=True) for _ in range(2)]
            for half in range(2):
                for g4 in range(4):
                    gi = half * 4 + g4
                    mm(pS[half][0][:, g4 * 128:(g4 + 1) * 128], wP[:, 0, gi, :], Ug[:, gi, :], True, True, [wPb, ugb], [pS[half][1]])
                    mm(pSp[half][0][:, g4 * 128:(g4 + 1) * 128], wP[:, 1, gi, :], Ug[:, gi, :], True, True, [wPb, ugb], [pSp[half][1]])
            return Ug, ugb, pS, pSp

        nxt = stage_a(0)
        for gb in range(6):
            g0 = 8 * gb
            Ug, ugb, pS, pSp = nxt
            wB, wBb = wload(s5w_scr[l, 2:4, :, g0:g0 + 8, :].rearrange("k p g m -> p k g m"), [128, 2, 8, 128])
            cosr, cosb = wload(rot_scr[l, 0, :, g0:g0 + 8, :], [128, 8, 128], f32=True)
            sinr, sinb = wload(rot_scr[l, 1, :, g0:g0 + 8, :], [128, 8, 128], f32=True)
            cosr2 = cosr.rearrange("p g k -> p (g k)")
            sinr2 = sinr.rearrange("p g k -> p (g k)")
            Hp = tmpB[:, 1, :].rearrange("p (g k) -> p g k", k=128)
            Yg = tmpB[:, 2, :].rearrange("p (g k) -> p g k", k=128)
            for half in range(2):
                hs = slice(half * 512, (half + 1) * 512)
                tt(tmpA[:, 0, hs], pS[half][0][:], cosr2[:, hs], ALU.mult, [pS[half][1], cosb], [tA[0]])
                tt(tmpA[:, 1, hs], pSp[half][0][:], sinr2[:, hs], ALU.mult, [pSp[half][1], sinb], [tA[1]])
                release_ps(pS[half][1])
                release_ps(pSp[half][1])
            tt(tmpA[:, 2, :], tmpA[:, 0, :], tmpA[:, 1, :], ALU.add, [tA[0], tA[1]], [tA[2]])
            if gb + 1 < 6:
                nxt = stage_a(gb + 1)
            G = tmpA[:, 3, :].rearrange("p (g k) -> p g k", k=128)
            Sr = tmpA[:, 2, :].rearrange("p (g k) -> p g k", k=128)
            for gi in range(8):
                g = g0 + gi
                dve(lambda: nc.vector.tensor_tensor_scan(out=G[:, gi, :], data0=r8[:, l, g:g + 1].to_broadcast([128, 128]),
                                                          data1=Sr[:, gi, :], initial=carry[:, l, g:g + 1],
                                                          op0=ALU.mult, op1=ALU.add),
                    [tA[2], carb, buf("r8")], [tA[3]])
            for half in range(2):
                hs = slice(half * 512, (half + 1) * 512)
                ps, pb = next_ps()
                mm(ps[:], jmat_f, tmpA[:, 3, hs], True, True, [cb, tA[3]], [pb])
                tt(tmpA[:, 0, hs], tmpA[:, 3, hs], cosr2[:, hs], ALU.mult, [tA[3], cosb], [tA[0]])
                tt(tmpA[:, 1, hs], ps[:], sinr2[:, hs], ALU.mult, [pb, sinb], [tA[1]])
            t1 = tmpA[:, 0, :].rearrange("p (g k) -> p g k", k=128)
            t2 = tmpA[:, 1, :].rearrange("p (g k) -> p g k", k=128)
            cpy(Hp[:, :, 0:1], carry[:, l, g0:g0 + 8].unsqueeze(2), [carb], [tBb[1]])
            tt(Hp[:, :, 1:128], t1[:, :, 0:127], t2[:, :, 0:127], ALU.subtract, [tA[0], tA[1]], [tBb[1]])
            tt(carry[:, l, g0:g0 + 8].unsqueeze(2), t1[:, :, 127:128], t2[:, :, 127:128], ALU.subtract, [tA[0], tA[1]], [carb])
            pY = [next_ps(hold=True) for _ in range(2)]
            for half in range(2):
                for g4 in range(4):
                    gi = half * 4 + g4
                    mm(pY[half][0][:, g4 * 128:(g4 + 1) * 128], wB[:, 0, gi, :], Ug[:, gi, :], True, False, [wBb, ugb], [pY[half][1]])
                    mm(pY[half][0][:, g4 * 128:(g4 + 1) * 128], wB[:, 1, gi, :], Hp[:, gi, :], False, True, [wBb, tBb[1]], [pY[half][1]])
            for half in range(2):
                act_op(tmpB[:, 2, half * 512:(half + 1) * 512], pY[half][0][:], AF.Gelu_apprx_tanh, [pY[half][1]], [tBb[2]])
                release_ps(pY[half][1])
            cv = cat[:, gb, :].rearrange("p (k s) -> p s k", s=8)
            for half in range(2):
                ps, pb = next_ps()
                for s4 in range(4):
                    s = half * 4 + s4
                    for gi in range(8):
                        mm(ps[:, s4 * 128:(s4 + 1) * 128], ew[:, s, 112 - 16 * gi:240 - 16 * gi], Yg[:, gi, :], gi == 0, gi == 7,
                           [cb, tBb[2]], [pb])
                evac(half, cv[:, half * 4:(half + 1) * 4, :], ps[:].rearrange("p (s k) -> p s k", k=128), [pb], [catb[gb][0], catb[gb][1]])
        w = wglu_d[l].rearrange("(k p) n -> p k n", p=128)
        for cc in range(2):
            wc, wcb = wload(w[:, :, cc * 384:(cc + 1) * 384], [128, 6, 384])
            for j3 in range(3):
                j = cc * 3 + j3
                for st in range(NST):
                    ps, pb = next_ps()
                    for k in range(6):
                        mm(ps[:], wc[:, k, j3 * 128:(j3 + 1) * 128], cat[:, k, SL[st]], k == 0, k == 5, [wcb, catb[k][st]], [pb])
                    sg = tmpA[:, st, 0:512]
                    act_op(sg, ps[:], AF.Sigmoid, [pb], [tA[st]])
                    tt(uT[:, j, SL[st]], cat[:, j, SL[st]], sg, ALU.mult, [catb[j][st], tA[st]], [ub[j][st]])
        return [(uT[:, k, :], ub[k]) for k in range(6)] + [(cat[:, 6 + k, :], catb[6 + k]) for k in range(2)]

    def diff_attn(l, tb):
        j = l - 2
        rope_inplace(uT, ub)
        nkeys = (16 * tb + 16) * 128
        kvs = [(kvbuf, [hb[0], hb[1]]), (kvbuf2, [buf("kv2")])]

        def kv_views(hh):
            kb_, bb = kvs[hh % 2]
            return kb_[:, 0, :], kb_[:, 1, :].rearrange("p (b e) -> p b e", e=128), bb

        def load_kv(hh):
            KT, V, bb = kv_views(hh)
            cx.dma(SP, KT[:, 0:nkeys], kT_scr[hh * 128:(hh + 1) * 128, 0:nkeys], reads=[buf("kvscr")], writes=bb)
            cx.dma(SP, V[:, 0:nkeys // 128, :], v_scr[0:nkeys, hh * 128:(hh + 1) * 128].rearrange("(b p) e -> p b e", p=128),
                   reads=[buf("kvscr")], writes=bb)

        steps = []
        for hh in range(6):
            for st in range(NST):
                i = 2 * tb + st
                nkb = 8 * i + 8
                for kb in range(nkb):
                    for c in range(2):
                        steps.append((hh, st, i, nkb, kb, c))
        eaccs = [[tmpA[:, 3, 0:512], tmpA[:, 3, 512:1024]], [tmpA[:, 0, 512:1024], tmpA[:, 1, 512:1024]]]
        eabs = [[buf("eacc00"), buf("eacc01")], [buf("eacc10"), buf("eacc11")]]
        PRE = 3
        sps = {}
        nE = 0
        load_kv(0)

        def col0(i_, kb_):
            r_ = kb_ - 8 * i_
            return (r_ // 2) * 128 if r_ >= 0 else 0

        def issue_s(n_):
            hh_, st_, i_, nkb_, kb_, c_ = steps[n_]
            KT, V, bb = kv_views(hh_)
            c0 = col0(i_, kb_)
            pss, psb = next_ps()
            mm(pss[:, c0:512], KT[c_ * 64:(c_ + 1) * 64, kb_ * 128:(kb_ + 1) * 128],
               uT[c_ * 64:(c_ + 1) * 64, hh_, st_ * 512 + c0:(st_ + 1) * 512],
               True, True, bb + [ub[hh_][st_]], [psb])
            sps[n_] = (pss, psb)

        for n_ in range(min(PRE, len(steps))):
            issue_s(n_)
        pO = None
        pending_ep = []
        for n_, (hh, st, i, nkb, kb, c) in enumerate(steps):
            blk = (hh * NST + st) % 2
            eacc, eab = eaccs[blk], eabs[blk]
            KT, V, kvb = kv_views(hh)
            if kb == 0 and c == 0:
                pO = [next_ps(hold=True) for _ in range(2)]
                if st == 0 and hh + 1 < 6:
                    load_kv(hh + 1)
            if n_ + PRE < len(steps):
                issue_s(n_ + PRE)
            pss, psb = sps.pop(n_)
            c0 = col0(i, kb)
            e = tmpB[:, nE % 6, 0:512]
            eb = tBb[nE % 6]
            nE += 1
            act_op(e[:, c0:512], pss[:, c0:512], AF.Exp, [psb], [eb], scale=0.125)
            if kb >= 8 * i:
                tt(e[:, c0:c0 + 128], e[:, c0:c0 + 128], masks[:, kb - 8 * i, :], ALU.mult, [eb, cb], [eb])
            mm(pO[c][0][:, c0:512], V[:, kb, :], e[:, c0:512], kb == 0, kb == nkb - 1, kvb + [eb], [pO[c][1]])
            if c == 0:
                if kb == 0:
                    cpy(eacc[c], e, [eb], [eab[c]])
                else:
                    tt(eacc[c][:, c0:512], eacc[c][:, c0:512], e[:, c0:512], ALU.add, [eb, eab[c]], [eab[c]])
            else:
                ec_, e_ = eacc[c][:, c0:512], e[:, c0:512]
                if kb == 0:
                    cx.op(POOL, lambda: nc.gpsimd.tensor_copy(out=ec_, in_=e_), reads=[eb], writes=[eab[c]])
                else:
                    cx.op(POOL, lambda: nc.gpsimd.tensor_tensor(out=ec_, in0=ec_, in1=e_, op=ALU.add), reads=[eb, eab[c]], writes=[eab[c]])
            if pending_ep:
                pending_ep.pop(0)()
            if not (kb == nkb - 1 and c == 1):
                continue
            o0, o1, r_ = tmpA[:, 0, 0:512], tmpA[:, 1, 0:512], tmpA[:, 2, 0:512]
            for c2 in range(2):
                act_op((o0, o1)[c2], pO[c2][0][:], AF.Identity, [pO[c2][1]], [buf(f"o{c2}")])
                release_ps(pO[c2][1])

            def stage1(c2, eacc=eacc, eab=eab):
                pl, plb = next_ps()
                mm(pl[:], ones_f, eacc[c2], True, True, [cb, eab[c2]], [plb])
                dve(lambda: nc.vector.reciprocal(out=r_, in_=pl[:]), [plb], [tA[2]])
                tt((o0, o1)[c2], (o0, o1)[c2], r_, ALU.mult, [buf(f"o{c2}"), tA[2]], [buf(f"o{c2}")])

            def stage2():
                stt(o0, o1, lam[:, j, 1:2], o0, ALU.mult, ALU.add, [buf("o0"), buf("o1"), buf("lam")], [buf("o0")])
                act_op(sq[:, 0, :], o0, AF.Square, [buf("o0")], [buf("sq")])

            def stage3():
                ps_, pb_ = next_ps()
                mm(ps_[:], ones_b, sq[:, 0, :], True, True, [buf("sq"), cb], [pb_])
                act_op(r_, ps_[:], AF.Sqrt, [pb_], [tA[2]], scale=1.0 / 128, bias=EPS)

            def stage4(hh=hh, st=st):
                dve(lambda: nc.vector.reciprocal(out=r_, in_=r_), [tA[2]], [tA[2]])
                stt(cat[:, hh, SL[st]], o0, lam[:, j, 2:3], r_, ALU.mult, ALU.mult, [buf("o0"), tA[2], buf("lam")], [catb[hh][st]])

            nop = lambda: None
            pending_ep.extend([nop, lambda f=stage1: f(0), nop, lambda f=stage1: f(1), nop, stage2, nop, stage3, nop, stage4])
        while pending_ep:
            pending_ep.pop(0)()
        return [(cat[:, k, :], catb[k]) for k in range(8)]

    def kv_tile(ta):
        t0 = ta * TT
        norm_x(13)
        w = wkv_d.rearrange("(k p) n -> p k n", p=128)
        n = 0
        for cc in range(3):
            wc, wcb = wload(w[:, :, cc * 256:(cc + 1) * 256], [128, 8, 256])
            for half in range(2):
                kc = 2 * cc + half
                for st in range(NST):
                    ps, pb = next_ps()
                    for k in range(8):
                        mm(ps[:], wc[:, k, half * 128:(half + 1) * 128], hT[:, k, SL[st]], k == 0, k == 7, [wcb, hb[st]], [pb])
                    evac(n, uT[:, kc, SL[st]], ps[:], [pb], [ub[kc][st]])
                    n += 1
        rope_inplace(uT, ub)
        allub = [ub[k][st] for k in range(6) for st in range(NST)]
        kv_toks.append(cx.dma(SP, kT_scr[:, t0:t0 + TT].rearrange("(k p) t -> p k t", p=128), uT, reads=allub, writes=[buf("kvscr")]))
        act_guard.append(kv_toks[-1])
        vt = cat.rearrange("p k t -> p (k t)")[:, 0:6144].rearrange("p (b e) -> p b e", e=768)
        allcat = [catb[k][st] for k in range(8) for st in range(NST)]
        for cc in range(2):
            wc, wcb = wload(w[:, :, 768 + cc * 384:768 + (cc + 1) * 384], [128, 8, 384])
            for tbk in range(8):
                ps, pb = next_ps()
                st = tbk // 4
                for k in range(8):
                    mm(ps[:, 0:384], hT[:, k, tbk * 128:(tbk + 1) * 128], wc[:, k, :], k == 0, k == 7, [wcb, hb[st]], [pb])
                evac(n, vt[:, tbk, cc * 384:(cc + 1) * 384], ps[:, 0:384], [pb], [catb[tbk][cc]])
                n += 1
        kv_toks.append(cx.dma(SP, v_scr[t0:t0 + TT, :].rearrange("(b p) e -> p b e", p=128), vt, reads=allcat, writes=[buf("kvscr")]))
        act_guard.append(kv_toks[-1])

    kv_toks = []
    scr_toks = []

    def select_store(ta):
        xv = xT[:].rearrange("p k (m two j) -> p k m two j", two=2, j=128)
        ov = tmpA[:].rearrange("p a (k2 m j) -> p (a k2) m j", k2=2, j=128)
        ts(ov, xv[:, :, :, 0, :], flags[:, 0:1], None, ALU.mult, None, xb + [cb], tA)
        stt(ov, xv[:, :, :, 1, :], flags[:, 1:2], ov, ALU.mult, ALU.add, xb + [cb] + tA, tA)
        kv_toks.append(cx.dma(SP, x1_scr[:, ta * 512:(ta + 1) * 512].rearrange("(k p) t -> p k t", p=128),
                              tmpA[:].rearrange("p a (k2 t) -> p (a k2) t", k2=2), reads=tA, writes=[buf("x1scr")]))

    memf = tmpA[:].rearrange("p a t -> p (a t)")[:, 0:2048].rearrange("p (k m) -> p k m", m=256)
    memn = tmpB[:, 0:2, :].rearrange("p a t -> p (a t)").rearrange("p (k m) -> p k m", m=256)
    mfb, mnb = buf("memf"), buf("memn")
    cx.dma(SP, memf, memT_d.rearrange("(k p) m -> p k m", p=128), writes=[mfb])
    rmsnorm(memf, mfb, gains[:, 12, :], memn, mnb, 256, 0)
    n = 0
    for l4 in range(4):
        w = wmkv_d[l4].rearrange("(k p) n -> p k n", p=128)
        wk, wkb = wload(w[:, :, 0:256], [128, 8, 256])
        wv, wvb = wload(w[:, :, 256:512], [128, 8, 256])
        for c in range(2):
            ps, pb = next_ps()
            for k in range(8):
                mm(ps[:, 0:256], wk[:, k, c * 128:(c + 1) * 128], memn[:, k, :], k == 0, k == 7, [wkb, mnb], [pb])
            evac(n, mkT[:, l4, c, :], ps[:, 0:256], [pb], [buf("mkv")])
            n += 1
        for mb in range(2):
            ps, pb = next_ps()
            for k in range(8):
                mm(ps[:, 0:256], memn[:, k, mb * 128:(mb + 1) * 128], wv[:, k, :], k == 0, k == 7, [wvb, mnb], [pb])
            evac(n, mvv[:, l4, mb, :], ps[:, 0:256], [pb], [buf("mkv")])
            n += 1
    lq = tmpA[:, 2, 0:512].rearrange("p (j a d) -> p j a d", j=2, a=4)
    lb = buf("lam")
    cx.dma(SP, lq, lqk_d, writes=[tA[2]])
    for j in range(2):
        for a_ in range(2):
            tt(tmpA[:, 3, 0:64], lq[:, j, 2 * a_, :], lq[:, j, 2 * a_ + 1, :], ALU.mult, [tA[2]], [tA[3]])
            dve(lambda: nc.vector.reduce_sum(out=lam[:, j, 2 + a_:3 + a_], in_=tmpA[:, 3, 0:64], axis=mybir.AxisListType.X), [tA[3]], [lb])
            act_op(lam[:, j, 2 + a_:3 + a_], lam[:, j, 2 + a_:3 + a_], AF.Exp, [lb], [lb])
        tt(lam[:, j, 0:1], lam[:, j, 2:3], lam[:, j, 3:4], ALU.subtract, [lb], [lb])
        ts(lam[:, j, 1:2], lam[:, j, 0:1], -1.0, -LAM_INIT[2 + j], ALU.mult, ALU.add, [lb], [lb])
        ts(lam[:, j, 2:3], subln[:, j:j + 1], 1.0 - LAM_INIT[2 + j], None, ALU.mult, None, [lb, cb], [lb])

    xflat = xT[:].rearrange("p k t -> p (k t)")
    pp = xflat[:, 0:1152].rearrange("p (a g) -> p a g", g=NG)
    BRI = xflat[:, 1152:2688].rearrange("p (a g c) -> p a g c", a=2, c=16)
    CRI = xflat[:, 2688:4224].rearrange("p (a g c) -> p a g c", a=2, c=16)
    trtf = xflat[:, 4224:7296]
    trt = trtf.rearrange("p (a g) -> p a g", g=NG)
    DCOL = xflat[:, 7296:7344]
    tmpAf = tmpA[:].rearrange("p a t -> p (a t)")
    BRAW = tmpAf[:, 1024:2560].rearrange("p (a g c) -> p a g c", a=2, c=16)
    actf = act[:, 0:16, :].rearrange("p k t -> p (k t)").bitcast(F32)
    hTf = hT[:].rearrange("p k t -> p (k t)").bitcast(F32)
    sqf = sq[:].rearrange("p k t -> p (k t)").bitcast(F32)
    ppb = buf("pp")
    hsb = buf("hTscr")
    acb = buf("actscr")
    sqb_ = buf("sq")
    for l in range(2):
        R, W = [ppb], [ppb]
        cx.dma(SP, pp[:, 0:3, :], s5a_d[:, l, :, :], writes=W)
        cx.dma(SP, BRAW, s5b_d[:, l], writes=tA)
        cx.dma(SP, CRI, s5c_d[:, l], writes=W)
        cx.dma(SP, DCOL, s5d_d[:, l, :], writes=W)
        LR, LI, LDT, DT, LRDT, TH, ANG, TF, SN, MG, LAMR, LAMI, NR, DEN, FR, FI, T1, T2, TH8 = [pp[:, i, :] for i in range(19)]
        TI = pp[:, 19, :].bitcast(I32)
        act_op(DT, LDT, AF.Exp, R, W)
        tt(LRDT, LR, DT, ALU.mult, R, W)
        tt(TH, LI, DT, ALU.mult, R, W)
        ts(TH8, TH, 8.0, None, ALU.mult, None, R, W)

        def trig(out, n_, psi):
            ts(ANG, TH, float(n_), psi, ALU.mult, ALU.add, R + [cb], W)
            sin_rr(SN, ANG, TF, TI, R, W)
            act_op(MG, LRDT, AF.Exp, R, W, scale=float(n_))
            tt(out, SN, MG, ALU.mult, R, W)

        trig(LAMR, 1, CC["halfpi"])
        trig(LAMI, 1, CC["zero"])
        ts(NR, LAMR, -1.0, None, ALU.add, None, R, W)
        tt(T1, LR, LR, ALU.mult, R, W)
        tt(T2, LI, LI, ALU.mult, R, W)
        tt(DEN, T1, T2, ALU.add, R, W)
        dve(lambda: nc.vector.reciprocal(out=DEN, in_=DEN), R, W)
        tt(T1, NR, LR, ALU.mult, R, W)
        tt(T2, LAMI, LI, ALU.mult, R, W)
        tt(FR, T1, T2, ALU.add, R, W)
        tt(FR, FR, DEN, ALU.mult, R, W)
        tt(T1, LAMI, LR, ALU.mult, R, W)
        tt(T2, NR, LI, ALU.mult, R, W)
        tt(FI, T1, T2, ALU.subtract, R, W)
        tt(FI, FI, DEN, ALU.mult, R, W)
        bc = lambda a_: a_.unsqueeze(2).to_broadcast([128, NG, 16])
        tb_ = tmpAf[:, 0:768].rearrange("p (g c) -> p g c", c=16)
        RA, WA = R + tA, W + tA
        tt(BRI[:, 0], BRAW[:, 0], bc(FR), ALU.mult, RA, WA)
        tt(tb_, BRAW[:, 1], bc(FI), ALU.mult, RA, WA)
        tt(BRI[:, 0], BRI[:, 0], tb_, ALU.subtract, RA, WA)
        tt(BRI[:, 1], BRAW[:, 1], bc(FR), ALU.mult, RA, WA)
        tt(tb_, BRAW[:, 0], bc(FI), ALU.mult, RA, WA)
        tt(BRI[:, 1], BRI[:, 1], tb_, ALU.add, RA, WA)
        NV = tabc[:, 0, :]
        PSV = tabc[:, 1, :]
        b3 = lambda a_: a_.unsqueeze(2).to_broadcast([128, 64, NG])
        g3 = lambda a_: a_.unsqueeze(1).to_broadcast([128, 64, NG])
        ANGA = actf[:, 0:3072]
        TFA = actf[:, 3072:6144]
        MGA = hTf[:, 0:3072]
        TIA = tmpAf[:, 0:3072].bitcast(I32)
        a3 = lambda a_: a_.rearrange("p (a g) -> p a g", g=NG)
        RT, WT = R + [acb, hsb, cb] + tA, W + [acb, hsb] + tA
        tt(a3(ANGA), g3(TH), b3(NV), ALU.mult, RT, WT)
        tt(a3(ANGA), a3(ANGA), b3(PSV), ALU.add, RT, WT)
        sin_rr(trtf, ANGA, TFA, TIA, RT, WT)
        tt(a3(MGA), g3(LRDT), b3(NV), ALU.mult, RT, WT)
        act_op(MGA, MGA, AF.Exp, RT, WT)
        tt(trtf, trtf, MGA, ALU.mult, RT, WT)
        act_op(r8[:, l, :], LRDT, AF.Exp, R, [buf("r8")], scale=8.0)
        dve(lambda: nc.vector.memset(carry[:, l, :], 0.0), [], [buf(f"carry{l}")])
        for gb in range(6):
            g0 = 8 * gb
            gs = slice(g0, g0 + 8)
            mats = [tmpA[:, i, :].rearrange("p (g s c) -> p g s c", s=8, c=16) for i in range(4)]
            scr = sqf[:, 1024:2048].rearrange("p (g s c) -> p g s c", s=8, c=16)
            srcs_ = [(BRI, 0), (BRI, 2), (CRI, 4), (CRI, 6)]
            for mi in range(4):
                dat, st0 = srcs_[mi]
                A0 = dat[:, 0, gs, :].unsqueeze(2).to_broadcast([128, 8, 8, 16])
                A1 = dat[:, 1, gs, :].unsqueeze(2).to_broadcast([128, 8, 8, 16])
                Q0 = trt[:, st0 * 8:(st0 + 1) * 8, gs].rearrange("p s g -> p g s").unsqueeze(3).to_broadcast([128, 8, 8, 16])
                Q1 = trt[:, (st0 + 1) * 8:(st0 + 2) * 8, gs].rearrange("p s g -> p g s").unsqueeze(3).to_broadcast([128, 8, 8, 16])
                tt(mats[mi], A0, Q0, ALU.mult, R, [tA[mi]])
                tt(scr, A1, Q1, ALU.mult, R, [sqb_])
                tt(mats[mi], mats[mi], scr, ALU.add, [tA[mi], sqb_], [tA[mi]])
            W4 = tmpB[:, 2:6, :].rearrange("p a (g m) -> p a g m", m=128)
            w4b = tBb[2:6]
            Zf, Z7f, Xf, X1f = [tmpA[:, i, :].rearrange("p (g m) -> p g m", m=128) for i in range(4)]
            Tm = sqf[:, 0:1024].rearrange("p (g m) -> p g m", m=128)
            for half in range(2):
                ps, pb = next_ps()
                for g4 in range(4):
                    gi = half * 4 + g4
                    mm(ps[:, g4 * 128:(g4 + 1) * 128], Zf[:, gi, :], Xf[:, gi, :], True, True, [tA[0], tA[2]], [pb])
                tt(Tm[:, half * 4:(half + 1) * 4, :], ps[:].rearrange("p (g m) -> p g m", m=128),
                   maskT_f.unsqueeze(1).to_broadcast([128, 4, 128]), ALU.mult, [pb, cb], [sqb_])
                for g4 in range(4):
                    gi = half * 4 + g4
                    stt(W4[:, 2, gi, :], ident_f, DCOL[:, g0 + gi:g0 + gi + 1], Tm[:, gi, :], ALU.mult, ALU.add, [cb, sqb_] + R, w4b)
                for kind, rhs_ in ((0, ident_f), (1, jmat_f)):
                    ps, pb = next_ps()
                    for g4 in range(4):
                        gi = half * 4 + g4
                        mm(ps[:, g4 * 128:(g4 + 1) * 128], Z7f[:, gi, :], rhs_, True, True, [tA[1], cb], [pb])
                    evac(kind, W4[:, kind, half * 4:(half + 1) * 4, :], ps[:].rearrange("p (g m) -> p g m", m=128), [pb], w4b)
            cpy(W4[:, 3], X1f, [tA[3]], w4b)
            scr_toks.append(cx.dma(SP, s5w_scr[l, :, :, g0:g0 + 8, :].rearrange("k p g m -> p k g m"), W4, reads=w4b, writes=[buf("s5scr")]))
            ang2 = tmpAf[:, 0:2048].rearrange("p (w g k) -> p w g k", w=2, k=128)
            tt(ang2[:, 1], jv_f.unsqueeze(1).to_broadcast([128, 8, 128]), TH8[:, gs].unsqueeze(2).to_broadcast([128, 8, 128]),
               ALU.mult, R + [cb] + tA, tA)
            ts(ang2[:, 0], ang2[:, 1], PI / 2, None, ALU.add, None, tA, tA)
            sin_rr(tmpAf[:, 2048:4096], tmpAf[:, 0:2048], hTf[:, 0:2048], hTf[:, 2048:4096].bitcast(I32), tA + [hsb], tA + [hsb])
            scr_toks.append(cx.dma(SP, rot_scr[l, :, :, g0:g0 + 8, :].rearrange("w p g k -> p w g k"),
                                   tmpAf[:, 2048:4096].rearrange("p (w g k) -> p w g k", w=2, k=128), reads=tA, writes=[buf("s5scr")]))
    POOL.wait(scr_toks)
    bar = [Tok(e.sem, e.count, e.key) for e in (PE, ACT, DVE) if e.count > 0]
    SP.wait(bar)

    out_toks = []
    xsrc = xT_d.rearrange("(k p) t -> p k t", p=128)
    for ta in range(na_tiles):
        for st in range(NST):
            cx.dma(SP, xT[:, :, SL[st]], xsrc[:, :, ta * TT + st * 512: ta * TT + (st + 1) * 512], writes=[xb[st]])
        rope_tables(pos_all_d[:, ta * TT:(ta + 1) * TT])
        for l in range(2):
            ffn(l, 0)
            if (f"ffn1_{l}", ta) in dbgs:
                out_toks.append(cx.dma(SP, dbgs[(f"ffn1_{l}", ta)], xT[:].rearrange("p k t -> p (k t)"), reads=xb))
            in_proj(l)
            mem_attn(l)
            srcs = s5_mixer(l)
            if (f"tokout_{l}", ta) in dbgs:
                out_toks.append(cx.dma(SP, dbgs[(f"tokout_{l}", ta)], act[:, 0:16, :].rearrange("p k t -> p (k t)"),
                                       reads=[b_ for s_ in srcs for b_ in s_[1]]))
            out_proj(l, srcs)
            ffn(l, 1)
            if (f"x_{l}", ta) in dbgs:
                out_toks.append(cx.dma(SP, dbgs[(f"x_{l}", ta)], xT[:].rearrange("p k t -> p (k t)"), reads=xb))
        kv_tile(ta)
        select_store(ta)

    SP.wait(kv_toks)
    x1src = x1_scr.rearrange("(k p) t -> p k t", p=128)
    for tb in range(nb_tiles):
        for st in range(NST):
            cx.dma(SP, xT[:, :, SL[st]], x1src[:, :, tb * TT + st * 512: tb * TT + (st + 1) * 512],
                   reads=[buf("x1scr")], writes=[xb[st]])
        rope_tables(pos_mine_d[:, tb * TT:(tb + 1) * TT])
        for l in range(2, 4):
            ffn(l, 0)
            in_proj(l)
            mem_attn(l)
            srcs = diff_attn(l, tb)
            if (f"tokout_{l}", tb) in dbgs:
                out_toks.append(cx.dma(SP, dbgs[(f"tokout_{l}", tb)], act[:, 0:16, :].rearrange("p k t -> p (k t)"),
                                       reads=[b_ for s_ in srcs for b_ in s_[1]]))
            out_proj(l, srcs)
            ffn(l, 1)
        for st in range(NST):
            of = tmpA[:].rearrange("p a (k2 t) -> p (a k2) t", k2=2)
            rmsnorm(xT[:, :, SL[st]], xb[st], gains[:, 14, :], of, list(tA), 512, st)
            out_toks.append(cx.dma(SP, outT_d[:, tb * TT + st * 512: tb * TT + (st + 1) * 512].rearrange("(k p) t -> p k t", p=128),
                                   of, reads=list(tA)))

    SP.wait(out_toks)


def _consts():
    cst = np.zeros((128, 16), np.float32)
    hp = PI / 2
    lo, hi = slice(0, 64), slice(64, 128)
    cst[lo, 0], cst[hi, 0] = 0 + hp, -hp + hp
    cst[lo, 1], cst[hi, 1] = hp + hp, 0 + hp
    cst[lo, 2], cst[hi, 2] = 0 + hp, hp + hp
    cst[lo, 3], cst[hi, 3] = hp + hp, PI + hp
    inv = (500000.0 ** (-np.arange(0, 16, 2, dtype=np.float32) / 16)).astype(np.float32)
    for base in (0, 64):
        cst[base:base + 8, 4] = inv
        cst[base + 8:base + 16, 4] = inv
        cst[base:base + 8, 5] = -inv
        cst[base + 8:base + 16, 5] = inv
    cst[:, 6] = hp
    ew = np.zeros((128, 8, 240), np.float32)
    for gi in range(8):
        for c in range(16):
            ew[16 * gi + c, gi, 112 + c] = 1.0
    cm = np.zeros((128, 5, 128), np.float32)
    cm[:, 0, :] = np.eye(128)
    for p in range(64):
        cm[64 + p, 1, p] = 1.0
        cm[p, 1, 64 + p] = -1.0
    sidx = np.arange(128) // 16
    cm[:, 2, :] = (sidx[None, :] >= sidx[:, None]).astype(np.float32)
    cm[:, 3, :] = (np.arange(128, dtype=np.float32) + 1.0)[None, :]
    cm[:, 4, :] = 1.0
    cmb = np.zeros((128, 2, 128), np.float32)
    cmb[:, 0, :] = 1.0
    for base in (0, 64):
        for j in range(8):
            cmb[base + 8 + j, 1, base + j] = 1.0
            cmb[base + j, 1, base + 8 + j] = 1.0
    tabc = np.zeros((128, 2, 64), np.float32)
    sets = [(lambda s_: -s_, 0), (lambda s_: -s_, 1), (lambda s_: 7 - s_, 0), (lambda s_: 7 - s_, 1),
            (lambda s_: s_, 2), (lambda s_: s_, 3), (lambda s_: s_ + 1, 2), (lambda s_: s_ + 1, 3)]
    for k_, (fn, pc) in enumerate(sets):
        for s_ in range(8):
            tabc[:, 0, k_ * 8 + s_] = fn(s_)
            tabc[:, 1, k_ * 8 + s_] = cst[:, pc]
    return cst, ew.astype(ml_dtypes.bfloat16), cm, cmb.astype(ml_dtypes.bfloat16), tabc


def _masks(h):
    m = np.zeros((128, 8, 128), np.float32)
    s = np.arange(128)[:, None]
    for r in range(8):
        jj = r // 2
        key = r * 128 + s
        qry = (2 * jj + h) * 128 + np.arange(128)[None, :]
        m[:, r, :] = (key <= qry)
    return m.astype(ml_dtypes.bfloat16)


def make_in_maps(inputs):
    f = lambda a: np.ascontiguousarray(np.asarray(a))
    cst, ew, cm, cmb, tabc = _consts()
    gains = np.stack([*inputs["ln_ffn1"], *inputs["ln_mix"], *inputs["ln_ffn2"],
                      inputs["ln_mem"], inputs["ln_kv"], inputs["ln_final"]], 0)
    gains = f(gains.reshape(15, 8, 128).transpose(2, 0, 1))
    rep = lambda a: np.concatenate([a, a], 0)
    s5a = np.stack([np.stack([inputs["s5_a_re"][l].T, inputs["s5_a_im"][l].T,
                              np.broadcast_to(inputs["s5_log_dt"][l][None, :], (64, NG))], 0) for l in range(2)], 0)
    s5a = f(rep(s5a.transpose(2, 0, 1, 3)))
    s5b = np.stack([np.stack([inputs["s5_b_re"][l], inputs["s5_b_im"][l]], 0) for l in range(2)], 0)
    s5b = f(rep(s5b.transpose(3, 0, 1, 2, 4)))
    s5c = np.stack([np.stack([inputs["s5_c_re"][l], inputs["s5_c_im"][l]], 0) for l in range(2)], 0)
    s5c = f(rep(s5c.transpose(4, 0, 1, 2, 3)))
    s5d = f(np.tile(inputs["s5_d"].transpose(2, 0, 1), (8, 1, 1)))
    lqk = np.stack([np.stack([inputs["diff_lq1"][j], inputs["diff_lk1"][j], inputs["diff_lq2"][j],
                              inputs["diff_lk2"][j]], 0) for j in range(2)], 0)
    lqk = f(np.broadcast_to(lqk[None], (128, 2, 4, 64)))
    subln = f(inputs["diff_subln"].T)
    shared = dict(gains=gains, s5a=s5a, s5b=s5b, s5c=s5c, s5d=s5d, lqk=lqk, subln=subln,
                  cst=cst, ew=ew, cm=cm, cmb=cmb, tabc=tabc)
    for k in ["ffn1_in", "ffn2_in", "ffn1_out", "ffn2_out", "w_mix_in", "w_mix_out", "w_mem_kv",
              "s5_w_glu", "w_kv_shared"]:
        shared[k] = f(inputs[k])
    in_maps = []
    for c in range(8):
        b, h = c // 2, c % 2
        m = dict(shared)
        m["xT"] = f(inputs["x"][b].T)
        m["memT"] = f(inputs["mem"][b].T)
        pos = np.asarray(inputs["positions"][b]).astype(np.int32)
        m["pos_all"] = f(np.broadcast_to(pos[None, :], (128, S)))
        mine = pos.reshape(16, 2, 128)[:, h, :].reshape(-1)
        m["pos_mine"] = f(np.broadcast_to(mine[None, :], (128, S // 2)))
        m["masks"] = _masks(h)
        fl = np.zeros((128, 2), np.float32)
        fl[:, 0] = 1 - h
        fl[:, 1] = h
        m["flags"] = fl
        in_maps.append(m)
    return in_maps


def kernel(**inputs):
    in_maps = make_in_maps(inputs)
    nc = build_program()
    res = run_bass_kernel_spmd(nc, in_maps, core_ids=list(range(8)))
    out = np.zeros((NB, S, D), np.float32)
    for c in range(8):
        b, h = c // 2, c % 2
        o = np.asarray(res.results[c]["outT"]).T.reshape(16, 128, D)
        out[b].reshape(16, 2, 128, D)[:, h] = o
    return out
```

```python
import math
from contextlib import ExitStack

import numpy as np
import ml_dtypes

import concourse.bass as bass
import concourse.mybir as mybir
from concourse.bass_utils import run_bass_kernel_spmd

F32 = mybir.dt.float32
BF16 = mybir.dt.bfloat16
I32 = mybir.dt.int32
ALU = mybir.AluOpType
AF = mybir.ActivationFunctionType

D = 1024
S = 4096
NB = 4
DFF = 2816
NFC = DFF // 128
TT = 1024
NST = TT // 512
KCH = TT // 8
NG = 48
EPS = 1e-6
PI = math.pi
C1 = 6.28125
C2 = 2.0 * PI - C1
SLOT = 3072
NSLOT = 6
LAM_INIT = [0.8 - 0.6 * math.exp(-0.3 * i) for i in range(4)]


class Tok:
    __slots__ = ("sem", "val", "key")

    def __init__(self, sem, val, key):
        self.sem, self.val, self.key = sem, val, key


class Buf:
    __slots__ = ("w", "r", "name")

    def __init__(self, name=""):
        self.w = None
        self.r = {}
        self.name = name


class Eng:
    def __init__(self, h, sem, key):
        self.h, self.sem, self.key = h, sem, key
        self.count = 0
        self.waited = {}
        self.pending = []

    def wait(self, toks):
        for t in toks:
            if t is None:
                continue
            if t.key == self.key and self.key == "pe":
                continue
            assert t.val is not None, "waiting on unresolved token"
            if self.waited.get(t.key, 0) < t.val:
                self.h.wait_ge(t.sem, t.val)
                self.waited[t.key] = t.val


class Ctx:
    def __init__(self, nc, es):
        self.nc = nc
        self.es = es
        mk = lambda n: es.enter_context(nc.semaphore(n))
        self.pe = Eng(nc.tensor, mk("s_pe"), "pe")
        self.act = Eng(nc.scalar, mk("s_act"), "act")
        self.dve = Eng(nc.vector, mk("s_dve"), "dve")
        self.pool = Eng(nc.gpsimd, mk("s_pool"), "pool")
        self.sp = Eng(nc.sync, mk("s_sp"), "sp")
        self.dsems = [(mk(f"s_d{i}"), f"d{i}") for i in range(24)]
        self.dcnt = [0] * len(self.dsems)
        self.dlast = [None] * len(self.dsems)
        self.dnext = 0

    def deps(self, reads, writes):
        toks = []
        for b in reads:
            if b.w is not None:
                toks.append(b.w)
        for b in writes:
            if b.w is not None:
                toks.append(b.w)
            toks.extend(b.r.values())
        return toks

    def op(self, eng, fn, reads=(), writes=(), signal=True):
        eng.wait(self.deps(reads, writes))
        ins = fn()
        if signal:
            eng.count += 1
            ins.then_inc(eng.sem, 1)
            tok = Tok(eng.sem, eng.count, eng.key)
            for p in eng.pending:
                p.val = eng.count
            eng.pending = []
        else:
            tok = Tok(eng.sem, None, eng.key)
            eng.pending.append(tok)
        for b in reads:
            b.r[eng.key] = tok
        for b in writes:
            b.w = tok
            b.r = {}
        return tok

    def dma(self, q, out, in_, reads=(), writes=(), semidx=None):
        if semidx is None:
            semidx = self.dnext
            self.dnext = (self.dnext + 1) % 16
        sem, key = self.dsems[semidx]
        q.wait([self.dlast[semidx]])
        q.wait(self.deps(reads, writes))
        self.dcnt[semidx] += 1
        q.h.dma_start(out=out, in_=in_).then_inc(sem, 16)
        tok = Tok(sem, 16 * self.dcnt[semidx], key)
        self.dlast[semidx] = tok
        for b in reads:
            b.r[key] = tok
        for b in writes:
            b.w = tok
            b.r = {}
        return tok


def build_program(na_tiles=4, nb_tiles=2, dbg=None):
    nc = bass.Bass("TRN2", target_bir_lowering=False)
    es = ExitStack()
    with es:
        _build(nc, es, na_tiles, nb_tiles, dbg)
    return nc


def _build(nc, es, na_tiles, nb_tiles, dbg):
    cx = Ctx(nc, es)
    PE, ACT, DVE, POOL, SP = cx.pe, cx.act, cx.dve, cx.pool, cx.sp

    def din(name, shape, dt=F32):
        return nc.dram_tensor(name, list(shape), dt, kind="ExternalInput").ap()

    def dscr(name, shape, dt):
        return nc.dram_tensor(name, list(shape), dt, kind="Internal").ap()

    def sb(name, shape, dt):
        return es.enter_context(nc.sbuf_tensor("sb_" + name, list(shape), dt))

    xT_d = din("xT", [D, S])
    memT_d = din("memT", [D, 256])
    pos_all_d = din("pos_all", [128, S], I32)
    pos_mine_d = din("pos_mine", [128, S // 2], I32)
    masks_d = din("masks", [128, 8, 128], BF16)
    flags_d = din("flags", [128, 2])
    gains_d = din("gains", [128, 15, 8])
    ffn_in_d = [din("ffn1_in", [4, D, 2 * DFF]), din("ffn2_in", [4, D, 2 * DFF])]
    ffn_out_d = [din("ffn1_out", [4, DFF, D]), din("ffn2_out", [4, DFF, D])]
    wmi_d = din("w_mix_in", [4, D, D])
    wmo_d = din("w_mix_out", [4, D, D])
    wmkv_d = din("w_mem_kv", [4, D, 512])
    wglu_d = din("s5_w_glu", [2, 768, 768])
    wkv_d = din("w_kv_shared", [D, 1536])
    s5a_d = din("s5a", [128, 2, 3, NG])
    s5b_d = din("s5b", [128, 2, 2, NG, 16])
    s5c_d = din("s5c", [128, 2, 2, NG, 16])
    s5d_d = din("s5d", [128, 2, NG])
    lqk_d = din("lqk", [128, 2, 4, 64])
    subln_d = din("subln", [128, 2])
    cst_d = din("cst", [128, 16])
    ew_d = din("ew", [128, 8, 240], BF16)
    cm_d = din("cm", [128, 5, 128])
    cmb_d = din("cmb", [128, 2, 128], BF16)
    tabc_d = din("tabc", [128, 2, 64])
    outT_d = nc.dram_tensor("outT", [D, S // 2], F32, kind="ExternalOutput").ap()
    dbgs = {}
    for d_ in (dbg or []):
        dbgs[(d_[0], d_[1])] = nc.dram_tensor(f"dbg_{d_[0]}_{d_[1]}", [128, d_[2]], BF16 if d_[0].startswith("tokout") else F32, kind="ExternalOutput").ap()

    kT_scr = dscr("kT_scr", [768, S], BF16)
    v_scr = dscr("v_scr", [S, 768], BF16)
    x1_scr = dscr("x1_scr", [D, S // 2], F32)
    s5w_scr = dscr("s5w_scr", [2, 4, 128, NG, 128], BF16)
    rot_scr = dscr("rot_scr", [2, 2, 128, NG, KCH], F32)

    xT = sb("xT", [128, 8, TT], F32)
    hT = sb("hT", [128, 8, TT], BF16)
    act = sb("act", [128, NFC, TT], BF16)
    sq = sb("sq", [128, 8, 512], BF16)
    sq2 = sb("sq2", [128, 8, 512], BF16)
    rstd = sb("rstd", [128, 2, 512], F32)
    uT = act[:, 0:6, :]
    mqT = act[:, 6:8, :]
    cat = act[:, 8:16, :]
    tmpB = act[:, 16:22, :]
    ring = sb("ring", [128, NSLOT, SLOT], BF16)
    ew = sb("ew", [128, 8, 240], BF16)
    cm = sb("cm", [128, 5, 128], F32)
    cmb = sb("cmb", [128, 2, 128], BF16)
    cst = sb("cst", [128, 16], F32)
    tabc = sb("tabc", [128, 2, 64], F32)
    gains = sb("gains", [128, 15, 8], F32)
    flags = sb("flags", [128, 2], F32)
    masks = sb("masks", [128, 8, 128], BF16)
    ropeT = sb("ropeT", [128, 2, TT], F32)
    mkT = sb("mkT", [128, 4, 2, 256], BF16)
    mvv = sb("mvv", [128, 4, 2, 256], BF16)
    carry = sb("carry", [128, 2, NG], F32)
    r8 = sb("r8", [128, 2, NG], F32)
    lam = sb("lam", [128, 2, 4], F32)
    subln = sb("subln", [128, 2], F32)
    tmpA = sb("tmpA", [128, 4, 1024], F32)
    kvbuf = hT[:].rearrange("p k t -> p (k t)").rearrange("p (a n) -> p a n", a=2)
    kvbuf2 = sb("kvbuf2", [128, 2, 4096], BF16)

    ones_b = cmb[:, 0, :]
    perm_b = cmb[:, 1, :]
    ident_f = cm[:, 0, :]
    jmat_f = cm[:, 1, :]
    maskT_f = cm[:, 2, :]
    jv_f = cm[:, 3, :]
    ones_f = cm[:, 4, :]

    psum = [es.enter_context(nc.psum_tensor(f"ps{i}", [128, 512], F32)) for i in range(8)]
    ps_b = [Buf(f"ps{i}") for i in range(8)]
    ps_state = {"n": 0, "held": set()}

    def next_ps(hold=False):
        for _ in range(17):
            i = ps_state["n"]
            ps_state["n"] = (i + 1) % 8
            if i not in ps_state["held"] and not (ps_b[i].w is not None and not ps_b[i].r):
                break
        else:
            raise RuntimeError("no free PSUM bank")
        if hold:
            ps_state["held"].add(i)
        return psum[i], ps_b[i]

    def release_ps(pb):
        ps_state["held"].discard(ps_b.index(pb))

    B = {}

    def buf(name):
        if name not in B:
            B[name] = Buf(name)
        return B[name]

    slot_b = [Buf(f"slot{i}") for i in range(NSLOT)]
    wstate = {"n": 0}

    def wload(src_ap, shape, f32=False):
        n = wstate["n"]
        wstate["n"] = n + 1
        i = n % NSLOT
        nel = 1
        for s_ in shape[1:]:
            nel *= s_
        if f32:
            assert 2 * nel <= SLOT
            dst = ring[:, i, 0:2 * nel].bitcast(F32)
        else:
            assert nel <= SLOT, (shape, nel)
            dst = ring[:, i, 0:nel]
        if len(shape) == 3:
            dst = dst.rearrange("p (a b) -> p a b", b=shape[2])
        elif len(shape) == 4:
            dst = dst.rearrange("p (a b c) -> p a b c", b=shape[2], c=shape[3])
        cx.dma(POOL, dst, src_ap, writes=[slot_b[i]], semidx=16 + i)
        return dst, slot_b[i]

    def mm(out, lhsT, rhs, start, stop, reads, writes):
        return cx.op(PE, lambda: nc.tensor.matmul(out, lhsT, rhs, start=start, stop=stop),
                     reads=reads, writes=writes, signal=stop)

    def act_op(out, in_, func, reads, writes, **kw):
        return cx.op(ACT, lambda: nc.scalar.activation(out=out, in_=in_, func=func, **kw),
                     reads=reads, writes=writes)

    def dve(fn, reads, writes):
        return cx.op(DVE, fn, reads=reads, writes=writes)

    def tt(out, a, b, op, reads, writes):
        return dve(lambda: nc.vector.tensor_tensor(out=out, in0=a, in1=b, op=op), reads, writes)

    def ts(out, a, s1, s2, op0, op1, reads, writes):
        if s2 is None:
            return dve(lambda: nc.vector.tensor_scalar(out=out, in0=a, scalar1=s1, scalar2=None, op0=op0), reads, writes)
        return dve(lambda: nc.vector.tensor_scalar(out=out, in0=a, scalar1=s1, scalar2=s2, op0=op0, op1=op1), reads, writes)

    def stt(out, a, sc, b, op0, op1, reads, writes):
        return dve(lambda: nc.vector.scalar_tensor_tensor(out=out, in0=a, scalar=sc, in1=b, op0=op0, op1=op1), reads, writes)

    def cpy(out, a, reads, writes):
        return dve(lambda: nc.vector.tensor_copy(out=out, in_=a), reads, writes)

    def evac(i, out, a, reads, writes):
        if i % 2 == 0:
            return act_op(out, a, AF.Identity, reads, writes)
        return cpy(out, a, reads, writes)

    cb = buf("consts")
    for dst, src in [(ew, ew_d), (cm, cm_d), (cmb, cmb_d), (cst, cst_d), (gains, gains_d),
                     (flags, flags_d), (masks, masks_d), (subln, subln_d), (tabc, tabc_d)]:
        cx.dma(SP, dst[:], src, writes=[cb])
    CC = {n: cst[:, i:i + 1] for i, n in enumerate(
        ["psiQ1", "psiQ2", "psiP1", "psiP2", "invc", "invs", "halfpi", "zero"])}
    tA = [buf(f"tmpA{i}") for i in range(4)]
    tBb = [buf(f"tmpB{i}") for i in range(6)]

    def sin_rr(out, ang, tmpf, tmpi, rb, wb):
        ts(tmpi, ang, 1.0 / (2 * PI), None, ALU.mult, None, rb, wb)
        cpy(tmpf, tmpi, rb, wb)
        stt(ang, tmpf, -C1, ang, ALU.mult, ALU.add, rb, wb)
        stt(ang, tmpf, -C2, ang, ALU.mult, ALU.add, rb, wb)
        ts(ang, ang, PI, -PI, ALU.min, ALU.max, rb, wb)
        act_op(out, ang, AF.Sin, rb, wb)

    def rmsnorm(src, src_b, gcol, dst, dst_b, n, par=0):
        sqb = buf("sq")
        act_op(sq[:, :, 0:n], src, AF.Square, [src_b], [sqb])
        ps, pb = next_ps()
        for k in range(8):
            mm(ps[:, 0:n], ones_b, sq[:, k, 0:n], k == 0, k == 7, [sqb, cb], [pb])
        rb = buf(f"rstd{par}")
        r = rstd[:, par, 0:n]
        act_op(r, ps[:, 0:n], AF.Sqrt, [pb], [rb], scale=1.0 / D, bias=EPS)
        dve(lambda: nc.vector.reciprocal(out=r, in_=r), [rb], [rb])
        for k in range(8):
            stt(dst[:, k, :], src[:, k, :], gcol[:, k:k + 1], r, ALU.mult, ALU.mult, [src_b, rb, cb],
                dst_b if isinstance(dst_b, list) else [dst_b])

    xb = [buf(f"x{st}") for st in range(NST)]
    hb = [buf(f"h{st}") for st in range(NST)]
    SL = [slice(st * 512, (st + 1) * 512) for st in range(NST)]

    def norm_x(gidx):
        gcol = gains[:, gidx, :]
        sqbs = [buf("sq"), buf("sq2")]
        sqs = [sq, sq2]
        pss = []
        for st in range(NST):
            act_op(sqs[st][:, :, :], xT[:, :, SL[st]], AF.Square, [xb[st]], [sqbs[st]])
        for st in range(NST):
            ps, pb = next_ps()
            for k in range(8):
                mm(ps[:], ones_b, sqs[st][:, k, :], k == 0, k == 7, [sqbs[st], cb], [pb])
            pss.append((ps, pb))
        for st in range(NST):
            act_op(rstd[:, st, :], pss[st][0][:], AF.Sqrt, [pss[st][1]], [buf(f"rstd{st}")], scale=1.0 / D, bias=EPS)
        for st in range(NST):
            r = rstd[:, st, :]
            dve(lambda: nc.vector.reciprocal(out=r, in_=r), [buf(f"rstd{st}")], [buf(f"rstd{st}")])
        for st in range(NST):
            for k in range(8):
                stt(hT[:, k, SL[st]], xT[:, k, SL[st]], gcol[:, k:k + 1], rstd[:, st, :], ALU.mult, ALU.mult,
                    [xb[st], buf(f"rstd{st}"), cb], [hb[st]])

    act_guard = []

    def ffn(layer, which):
        norm_x((0 if which == 0 else 8) + layer)
        w_in = ffn_in_d[which][layer].rearrange("(k p) n -> p k n", p=128)
        w_out = ffn_out_d[which][layer].rearrange("(f p) n -> p f n", p=128)
        actb = [[buf(f"act{j}_{st}") for st in range(NST)] for j in range(NFC)]
        if act_guard:
            DVE.wait(act_guard)
            del act_guard[:]
        for j in range(NFC):
            wg, wgb = wload(w_in[:, :, j * 128:(j + 1) * 128], [128, 8, 128])
            wu, wub = wload(w_in[:, :, DFF + j * 128: DFF + (j + 1) * 128], [128, 8, 128])
            for st in range(NST):
                sl = SL[st]
                pg, pgb = next_ps()
                for k in range(8):
                    mm(pg[:], wg[:, k, :], hT[:, k, sl], k == 0, k == 7, [wgb, hb[st]], [pgb])
                pu, pub = next_ps()
                for k in range(8):
                    mm(pu[:], wu[:, k, :], hT[:, k, sl], k == 0, k == 7, [wub, hb[st]], [pub])
                sg = tmpA[:, st, 0:512]
                act_op(sg, pg[:], AF.Silu, [pgb], [tA[st]])
                tt(act[:, j, sl], sg, pu[:], ALU.mult, [tA[st], pub], [actb[j][st]])
        for dch in range(8):
            wo, wob = wload(w_out[:, :, dch * 128:(dch + 1) * 128], [128, NFC, 128])
            for st in range(NST):
                sl = SL[st]
                po, pob = next_ps()
                for j in range(NFC):
                    mm(po[:], wo[:, j, :], act[:, j, sl], j == 0, j == NFC - 1, [wob, actb[j][st]], [pob])
                stt(xT[:, dch, sl], po[:], 0.5, xT[:, dch, sl], ALU.mult, ALU.add, [pob, xb[st]], [xb[st]])

    ub = [[buf(f"u{k}_{st}") for st in range(NST)] for k in range(6)]
    mqb = [[buf(f"mq{k}_{st}") for st in range(NST)] for k in range(2)]
    catb = [[buf(f"cat{k}_{st}") for st in range(NST)] for k in range(8)]

    def in_proj(layer):
        norm_x(4 + layer)
        w = wmi_d[layer].rearrange("(k p) n -> p k n", p=128)
        n = 0
        for cc in range(4):
            wc, wcb = wload(w[:, :, cc * 256:(cc + 1) * 256], [128, 8, 256])
            for half in range(2):
                oc = 2 * cc + half
                for st in range(NST):
                    ps, pb = next_ps()
                    for k in range(8):
                        mm(ps[:], wc[:, k, half * 128:(half + 1) * 128], hT[:, k, SL[st]], k == 0, k == 7, [wcb, hb[st]], [pb])
                    if oc < 6:
                        evac(n, uT[:, oc, SL[st]], ps[:], [pb], [ub[oc][st]])
                    else:
                        evac(n, mqT[:, oc - 6, SL[st]], ps[:], [pb], [mqb[oc - 6][st]])
                    n += 1

    def mem_attn(layer):
        steps = [(st, c, hd2, mb) for st in range(NST) for c in range(2) for hd2 in range(2) for mb in range(2)]
        PRE = 3
        sps = {}

        def issue_s(n_):
            st_, c_, hd2_, mb_ = steps[n_]
            base = hd2_ * 64
            pss, psb = next_ps()
            mm(pss[:], mkT[base:base + 64, layer, c_, mb_ * 128:(mb_ + 1) * 128], mqT[base:base + 64, c_, SL[st_]],
               True, True, [buf("mkv"), mqb[c_][st_]], [psb])
            sps[n_] = (pss, psb)

        for n_ in range(PRE):
            issue_s(n_)
        po = pl = None
        for n_, (st, c, hd2, mb) in enumerate(steps):
            if hd2 == 0 and mb == 0:
                po, pob = next_ps(hold=True)
                pl, plb = next_ps(hold=True)
            if n_ + PRE < len(steps):
                issue_s(n_ + PRE)
            base = hd2 * 64
            hd = 2 * c + hd2
            pss, psb = sps.pop(n_)
            e = tmpB[:, n_ % 6, 0:512]
            eb = tBb[n_ % 6]
            act_op(e, pss[:], AF.Exp, [psb], [eb], scale=0.125)
            mm(po[base:base + 64, :], mvv[:, layer, mb, hd * 64:(hd + 1) * 64], e, mb == 0, mb == 1, [buf("mkv"), eb], [pob])
            mm(pl[base:base + 64, :], ones_b[:, 0:64], e, mb == 0, mb == 1, [cb, eb], [plb])
            if hd2 == 1 and mb == 1:
                rl = tmpA[:, 2, 0:512]
                dve(lambda: nc.vector.reciprocal(out=rl, in_=pl[:]), [plb], [tA[2]])
                tt(cat[:, 6 + c, SL[st]], po[:], rl, ALU.mult, [pob, tA[2]], [catb[6 + c][st]])
                release_ps(pob)
                release_ps(plb)

    def out_proj(layer, srcs):
        w = wmo_d[layer].rearrange("(k p) n -> p k n", p=128)
        for cc in range(4):
            wc, wcb = wload(w[:, :, cc * 256:(cc + 1) * 256], [128, 8, 256])
            for half in range(2):
                dch = 2 * cc + half
                for st in range(NST):
                    ps, pb = next_ps()
                    for k in range(8):
                        mm(ps[:], wc[:, k, half * 128:(half + 1) * 128], srcs[k][0][:, SL[st]], k == 0, k == 7,
                           [wcb, srcs[k][1][st]], [pb])
                    tt(xT[:, dch, SL[st]], ps[:], xT[:, dch, SL[st]], ALU.add, [pb, xb[st]], [xb[st]])

    def rope_tables(pos_src):
        rb_ = buf("rope")
        allA = tA
        posi = tmpA[:, 0, :].bitcast(I32)
        cx.dma(SP, posi, pos_src, writes=[tA[0]])
        cpy(tmpA[:, 1, :], posi, [tA[0]], [tA[1]])
        for which, icol, pcol in ((0, "invc", "halfpi"), (1, "invs", "zero")):
            ts(tmpA[:, 2, :], tmpA[:, 1, :], CC[icol], CC[pcol], ALU.mult, ALU.add, [tA[1], cb], [tA[2]])
            sin_rr(ropeT[:, which, :], tmpA[:, 2, :], tmpA[:, 3, :], tmpA[:, 0, :].bitcast(I32), [tA[2], tA[3], tA[0]], [tA[2], tA[3], tA[0], rb_])

    def rope_inplace(T6, bufs6):
        rb_ = buf("rope")
        n = 0
        for ch in range(6):
            for st in range(NST):
                ps, pb = next_ps()
                mm(ps[:], perm_b, T6[:, ch, SL[st]], True, True, [cb, bufs6[ch][st]], [pb])
                i0, i1 = (n % 2) * 2, (n % 2) * 2 + 1
                n += 1
                tt(tmpA[:, i0, 0:512], ps[:], ropeT[:, 1, SL[st]], ALU.mult, [pb, rb_], [tA[i0]])
                tt(tmpA[:, i1, 0:512], T6[:, ch, SL[st]], ropeT[:, 0, SL[st]], ALU.mult, [bufs6[ch][st], rb_], [tA[i1]])
                tt(T6[:, ch, SL[st]], tmpA[:, i0, 0:512], tmpA[:, i1, 0:512], ALU.add, [tA[i0], tA[i1]], [bufs6[ch][st]])

    def s5_mixer(l):
        carb = buf(f"carry{l}")

        def stage_a(gb):
            g0 = 8 * gb
            par = gb % 2
            wP, wPb = wload(s5w_scr[l, 0:2, :, g0:g0 + 8, :].rearrange("k p g m -> p k g m"), [128, 2, 8, 128])
            Ug = tmpB[:, 3 * par, :].rearrange("p (g k) -> p g k", k=128)
            ugb = tBb[3 * par]
            uv = uT[:, gb, :].rearrange("p (k s) -> p s k", s=8)
            for half in range(2):
                ps, pb = next_ps()
                for g4 in range(4):
                    gi = half * 4 + g4
                    for s in range(8):
                        mm(ps[:, g4 * 128:(g4 + 1) * 128], ew[:, gi, 112 - 16 * s:240 - 16 * s], uv[:, s, :], s == 0, s == 7,
                           [cb, ub[gb][0], ub[gb][1]], [pb])
                evac(half, tmpB[:, 3 * par, half * 512:(half + 1) * 512], ps[:], [pb], [ugb])
            pS = [next_ps(hold=True) for _ in range(2)]
            pSp = [next_ps(hold=True) for _ in range(2)]
            for half in range(2):
                for g4 in range(4):
                    gi = half * 4 + g4
                    mm(pS[half][0][:, g4 * 128:(g4 + 1) * 128], wP[:, 0, gi, :], Ug[:, gi, :], True, True, [wPb, ugb], [pS[half][1]])
                    mm(pSp[half][0][:, g4 * 128:(g4 + 1) * 128], wP[:, 1, gi, :], Ug[:, gi, :], True, True, [wPb, ugb], [pSp[half][1]])
            return Ug, ugb, pS, pSp

        nxt = stage_a(0)
        for gb in range(6):
            g0 = 8 * gb
            Ug, ugb, pS, pSp = nxt
            wB, wBb = wload(s5w_scr[l, 2:4, :, g0:g0 + 8, :].rearrange("k p g m -> p k g m"), [128, 2, 8, 128])
            cosr, cosb = wload(rot_scr[l, 0, :, g0:g0 + 8, :], [128, 8, 128], f32=True)
            sinr, sinb = wload(rot_scr[l, 1, :, g0:g0 + 8, :], [128, 8, 128], f32=True)
            cosr2 = cosr.rearrange("p g k -> p (g k)")
            sinr2 = sinr.rearrange("p g k -> p (g k)")
            Hp = tmpB[:, 1, :].rearrange("p (g k) -> p g k", k=128)
            Yg = tmpB[:, 2, :].rearrange("p (g k) -> p g k", k=128)
            for half in range(2):
                hs = slice(half * 512, (half + 1) * 512)
                tt(tmpA[:, 0, hs], pS[half][0][:], cosr2[:, hs], ALU.mult, [pS[half][1], cosb], [tA[0]])
                tt(tmpA[:, 1, hs], pSp[half][0][:], sinr2[:, hs], ALU.mult, [pSp[half][1], sinb], [tA[1]])
                release_ps(pS[half][1])
                release_ps(pSp[half][1])
            tt(tmpA[:, 2, :], tmpA[:, 0, :], tmpA[:, 1, :], ALU.add, [tA[0], tA[1]], [tA[2]])
            if gb + 1 < 6:
                nxt = stage_a(gb + 1)
            G = tmpA[:, 3, :].rearrange("p (g k) -> p g k", k=128)
            Sr = tmpA[:, 2, :].rearrange("p (g k) -> p g k", k=128)
            for gi in range(8):
                g = g0 + gi
                dve(lambda: nc.vector.tensor_tensor_scan(out=G[:, gi, :], data0=r8[:, l, g:g + 1].to_broadcast([128, 128]),
                                                          data1=Sr[:, gi, :], initial=carry[:, l, g:g + 1],
                                                          op0=ALU.mult, op1=ALU.add),
                    [tA[2], carb, buf("r8")], [tA[3]])
            for half in range(2):
                hs = slice(half * 512, (half + 1) * 512)
                ps, pb = next_ps()
                mm(ps[:], jmat_f, tmpA[:, 3, hs], True, True, [cb, tA[3]], [pb])
                tt(tmpA[:, 0, hs], tmpA[:, 3, hs], cosr2[:, hs], ALU.mult, [tA[3], cosb], [tA[0]])
                tt(tmpA[:, 1, hs], ps[:], sinr2[:, hs], ALU.mult, [pb, sinb], [tA[1]])
            t1 = tmpA[:, 0, :].rearrange("p (g k) -> p g k", k=128)
            t2 = tmpA[:, 1, :].rearrange("p (g k) -> p g k", k=128)
            cpy(Hp[:, :, 0:1], carry[:, l, g0:g0 + 8].unsqueeze(2), [carb], [tBb[1]])
            tt(Hp[:, :, 1:128], t1[:, :, 0:127], t2[:, :, 0:127], ALU.subtract, [tA[0], tA[1]], [tBb[1]])
            tt(carry[:, l, g0:g0 + 8].unsqueeze(2), t1[:, :, 127:128], t2[:, :, 127:128], ALU.subtract, [tA[0], tA[1]], [carb])
            pY = [next_ps(hold=True) for _ in range(2)]
            for half in range(2):
                for g4 in range(4):
                    gi = half * 4 + g4
                    mm(pY[half][0][:, g4 * 128:(g4 + 1) * 128], wB[:, 0, gi, :], Ug[:, gi, :], True, False, [wBb, ugb], [pY[half][1]])
                    mm(pY[half][0][:, g4 * 128:(g4 + 1) * 128], wB[:, 1, gi, :], Hp[:, gi, :], False, True, [wBb, tBb[1]], [pY[half][1]])
            for half in range(2):
                act_op(tmpB[:, 2, half * 512:(half + 1) * 512], pY[half][0][:], AF.Gelu_apprx_tanh, [pY[half][1]], [tBb[2]])
                release_ps(pY[half][1])
            cv = cat[:, gb, :].rearrange("p (k s) -> p s k", s=8)
            for half in range(2):
                ps, pb = next_ps()
                for s4 in range(4):
                    s = half * 4 + s4
                    for gi in range(8):
                        mm(ps[:, s4 * 128:(s4 + 1) * 128], ew[:, s, 112 - 16 * gi:240 - 16 * gi], Yg[:, gi, :], gi == 0, gi == 7,
                           [cb, tBb[2]], [pb])
                evac(half, cv[:, half * 4:(half + 1) * 4, :], ps[:].rearrange("p (s k) -> p s k", k=128), [pb], [catb[gb][0], catb[gb][1]])
        w = wglu_d[l].rearrange("(k p) n -> p k n", p=128)
        for cc in range(2):
            wc, wcb = wload(w[:, :, cc * 384:(cc + 1) * 384], [128, 6, 384])
            for j3 in range(3):
                j = cc * 3 + j3
                for st in range(NST):
                    ps, pb = next_ps()
                    for k in range(6):
                        mm(ps[:], wc[:, k, j3 * 128:(j3 + 1) * 128], cat[:, k, SL[st]], k == 0, k == 5, [wcb, catb[k][st]], [pb])
                    sg = tmpA[:, st, 0:512]
                    act_op(sg, ps[:], AF.Sigmoid, [pb], [tA[st]])
                    tt(uT[:, j, SL[st]], cat[:, j, SL[st]], sg, ALU.mult, [catb[j][st], tA[st]], [ub[j][st]])
        return [(uT[:, k, :], ub[k]) for k in range(6)] + [(cat[:, 6 + k, :], catb[6 + k]) for k in range(2)]

    def diff_attn(l, tb):
        j = l - 2
        rope_inplace(uT, ub)
        nkeys = (16 * tb + 16) * 128
        kvs = [(kvbuf, [hb[0], hb[1]]), (kvbuf2, [buf("kv2")])]

        def kv_views(hh):
            kb_, bb = kvs[hh % 2]
            return kb_[:, 0, :], kb_[:, 1, :].rearrange("p (b e) -> p b e", e=128), bb

        def load_kv(hh):
            KT, V, bb = kv_views(hh)
            cx.dma(SP, KT[:, 0:nkeys], kT_scr[hh * 128:(hh + 1) * 128, 0:nkeys], reads=[buf("kvscr")], writes=bb)
            cx.dma(SP, V[:, 0:nkeys // 128, :], v_scr[0:nkeys, hh * 128:(hh + 1) * 128].rearrange("(b p) e -> p b e", p=128),
                   reads=[buf("kvscr")], writes=bb)

        steps = []
        for hh in range(6):
            for st in range(NST):
                i = 2 * tb + st
                nkb = 8 * i + 8
                for kb in range(nkb):
                    for c in range(2):
                        steps.append((hh, st, i, nkb, kb, c))
        eaccs = [[tmpA[:, 3, 0:512], tmpA[:, 3, 512:1024]], [tmpA[:, 0, 512:1024], tmpA[:, 1, 512:1024]]]
        eabs = [[buf("eacc00"), buf("eacc01")], [buf("eacc10"), buf("eacc11")]]
        PRE = 3
        sps = {}
        nE = 0
        load_kv(0)

        def col0(i_, kb_):
            r_ = kb_ - 8 * i_
            return (r_ // 2) * 128 if r_ >= 0 else 0

        def issue_s_pair(n0):
            todo = []
            toks = []
            for n_ in (n0, n0 + 1):
                hh_, st_, i_, nkb_, kb_, c_ = steps[n_]
                KT, V, bb = kv_views(hh_)
                pss, psb = next_ps()
                toks += cx.deps(bb + [ub[hh_][st_]], [psb])
                todo.append((n_, pss, psb, KT, bb, hh_, st_, col0(i_, kb_), kb_, c_))
            PE.wait(toks)
            for n_, pss, psb, KT, bb, hh_, st_, c0, kb_, c_ in todo:
                mm(pss[:, c0:512], KT[c_ * 64:(c_ + 1) * 64, kb_ * 128:(kb_ + 1) * 128],
                   uT[c_ * 64:(c_ + 1) * 64, hh_, st_ * 512 + c0:(st_ + 1) * 512],
                   True, True, bb + [ub[hh_][st_]], [psb])
                sps[n_] = (pss, psb)

        issue_s_pair(0)
        pO = None
        pending_ep = []
        for n_, (hh, st, i, nkb, kb, c) in enumerate(steps):
            blk = (hh * NST + st) % 2
            eacc, eab = eaccs[blk], eabs[blk]
            KT, V, kvb = kv_views(hh)
            if kb == 0 and c == 0:
                pO = [next_ps(hold=True) for _ in range(2)]
                if st == 0 and hh + 1 < 6:
                    load_kv(hh + 1)
            if c == 0 and n_ + 2 < len(steps):
                issue_s_pair(n_ + 2)
            pss, psb = sps.pop(n_)
            c0 = col0(i, kb)
            e = tmpB[:, nE % 6, 0:512]
            eb = tBb[nE % 6]
            nE += 1
            act_op(e[:, c0:512], pss[:, c0:512], AF.Exp, [psb], [eb], scale=0.125)
            if kb >= 8 * i:
                tt(e[:, c0:c0 + 128], e[:, c0:c0 + 128], masks[:, kb - 8 * i, :], ALU.mult, [eb, cb], [eb])
            if c == 0:
                pv0 = (e, eb)
            else:
                e0_, eb0_ = pv0
                PE.wait(cx.deps(kvb + [eb0_], [pO[0][1]]) + cx.deps(kvb + [eb], [pO[1][1]]))
                mm(pO[0][0][:, c0:512], V[:, kb, :], e0_[:, c0:512], kb == 0, kb == nkb - 1, kvb + [eb0_], [pO[0][1]])
                mm(pO[1][0][:, c0:512], V[:, kb, :], e[:, c0:512], kb == 0, kb == nkb - 1, kvb + [eb], [pO[1][1]])
            if c == 0:
                if kb == 0:
                    cpy(eacc[c], e, [eb], [eab[c]])
                else:
                    tt(eacc[c][:, c0:512], eacc[c][:, c0:512], e[:, c0:512], ALU.add, [eb, eab[c]], [eab[c]])
            else:
                ec_, e_ = eacc[c][:, c0:512], e[:, c0:512]
                if kb == 0:
                    cx.op(POOL, lambda: nc.gpsimd.tensor_copy(out=ec_, in_=e_), reads=[eb], writes=[eab[c]])
                else:
                    cx.op(POOL, lambda: nc.gpsimd.tensor_tensor(out=ec_, in0=ec_, in1=e_, op=ALU.add), reads=[eb, eab[c]], writes=[eab[c]])
            if pending_ep:
                pending_ep.pop(0)()
            if not (kb == nkb - 1 and c == 1):
                continue
            o0, o1, r_ = tmpA[:, 0, 0:512], tmpA[:, 1, 0:512], tmpA[:, 2, 0:512]
            for c2 in range(2):
                act_op((o0, o1)[c2], pO[c2][0][:], AF.Identity, [pO[c2][1]], [buf(f"o{c2}")])
                release_ps(pO[c2][1])

            def stage1(c2, eacc=eacc, eab=eab):
                pl, plb = next_ps()
                mm(pl[:], ones_f, eacc[c2], True, True, [cb, eab[c2]], [plb])
                dve(lambda: nc.vector.reciprocal(out=r_, in_=pl[:]), [plb], [tA[2]])
                tt((o0, o1)[c2], (o0, o1)[c2], r_, ALU.mult, [buf(f"o{c2}"), tA[2]], [buf(f"o{c2}")])

            def stage2():
                stt(o0, o1, lam[:, j, 1:2], o0, ALU.mult, ALU.add, [buf("o0"), buf("o1"), buf("lam")], [buf("o0")])
                act_op(sq[:, 0, :], o0, AF.Square, [buf("o0")], [buf("sq")])

            def stage3():
                ps_, pb_ = next_ps()
                mm(ps_[:], ones_b, sq[:, 0, :], True, True, [buf("sq"), cb], [pb_])
                act_op(r_, ps_[:], AF.Sqrt, [pb_], [tA[2]], scale=1.0 / 128, bias=EPS)

            def stage4(hh=hh, st=st):
                dve(lambda: nc.vector.reciprocal(out=r_, in_=r_), [tA[2]], [tA[2]])
                stt(cat[:, hh, SL[st]], o0, lam[:, j, 2:3], r_, ALU.mult, ALU.mult, [buf("o0"), tA[2], buf("lam")], [catb[hh][st]])

            nop = lambda: None
            pending_ep.extend([nop, lambda f=stage1: f(0), nop, lambda f=stage1: f(1), nop, stage2, nop, stage3, nop, stage4])
        while pending_ep:
            pending_ep.pop(0)()
        return [(cat[:, k, :], catb[k]) for k in range(8)]

    def kv_tile(ta):
        t0 = ta * TT
        norm_x(13)
        w = wkv_d.rearrange("(k p) n -> p k n", p=128)
        n = 0
        for cc in range(3):
            wc, wcb = wload(w[:, :, cc * 256:(cc + 1) * 256], [128, 8, 256])
            for half in range(2):
                kc = 2 * cc + half
                for st in range(NST):
                    ps, pb = next_ps()
                    for k in range(8):
                        mm(ps[:], wc[:, k, half * 128:(half + 1) * 128], hT[:, k, SL[st]], k == 0, k == 7, [wcb, hb[st]], [pb])
                    evac(n, uT[:, kc, SL[st]], ps[:], [pb], [ub[kc][st]])
                    n += 1
        rope_inplace(uT, ub)
        allub = [ub[k][st] for k in range(6) for st in range(NST)]
        kv_toks.append(cx.dma(SP, kT_scr[:, t0:t0 + TT].rearrange("(k p) t -> p k t", p=128), uT, reads=allub, writes=[buf("kvscr")]))
        act_guard.append(kv_toks[-1])
        vt = cat.rearrange("p k t -> p (k t)")[:, 0:6144].rearrange("p (b e) -> p b e", e=768)
        allcat = [catb[k][st] for k in range(8) for st in range(NST)]
        for cc in range(2):
            wc, wcb = wload(w[:, :, 768 + cc * 384:768 + (cc + 1) * 384], [128, 8, 384])
            for tbk in range(8):
                ps, pb = next_ps()
                st = tbk // 4
                for k in range(8):
                    mm(ps[:, 0:384], hT[:, k, tbk * 128:(tbk + 1) * 128], wc[:, k, :], k == 0, k == 7, [wcb, hb[st]], [pb])
                evac(n, vt[:, tbk, cc * 384:(cc + 1) * 384], ps[:, 0:384], [pb], [catb[tbk][cc]])
                n += 1
        kv_toks.append(cx.dma(SP, v_scr[t0:t0 + TT, :].rearrange("(b p) e -> p b e", p=128), vt, reads=allcat, writes=[buf("kvscr")]))
        act_guard.append(kv_toks[-1])

    kv_toks = []
    scr_toks = []

    def select_store(ta):
        xv = xT[:].rearrange("p k (m two j) -> p k m two j", two=2, j=128)
        ov = tmpA[:].rearrange("p a (k2 m j) -> p (a k2) m j", k2=2, j=128)
        ts(ov, xv[:, :, :, 0, :], flags[:, 0:1], None, ALU.mult, None, xb + [cb], tA)
        stt(ov, xv[:, :, :, 1, :], flags[:, 1:2], ov, ALU.mult, ALU.add, xb + [cb] + tA, tA)
        kv_toks.append(cx.dma(SP, x1_scr[:, ta * 512:(ta + 1) * 512].rearrange("(k p) t -> p k t", p=128),
                              tmpA[:].rearrange("p a (k2 t) -> p (a k2) t", k2=2), reads=tA, writes=[buf("x1scr")]))

    memf = tmpA[:].rearrange("p a t -> p (a t)")[:, 0:2048].rearrange("p (k m) -> p k m", m=256)
    memn = tmpB[:, 0:2, :].rearrange("p a t -> p (a t)").rearrange("p (k m) -> p k m", m=256)
    mfb, mnb = buf("memf"), buf("memn")
    cx.dma(SP, memf, memT_d.rearrange("(k p) m -> p k m", p=128), writes=[mfb])
    rmsnorm(memf, mfb, gains[:, 12, :], memn, mnb, 256, 0)
    n = 0
    for l4 in range(4):
        w = wmkv_d[l4].rearrange("(k p) n -> p k n", p=128)
        wk, wkb = wload(w[:, :, 0:256], [128, 8, 256])
        wv, wvb = wload(w[:, :, 256:512], [128, 8, 256])
        for c in range(2):
            ps, pb = next_ps()
            for k in range(8):
                mm(ps[:, 0:256], wk[:, k, c * 128:(c + 1) * 128], memn[:, k, :], k == 0, k == 7, [wkb, mnb], [pb])
            evac(n, mkT[:, l4, c, :], ps[:, 0:256], [pb], [buf("mkv")])
            n += 1
        for mb in range(2):
            ps, pb = next_ps()
            for k in range(8):
                mm(ps[:, 0:256], memn[:, k, mb * 128:(mb + 1) * 128], wv[:, k, :], k == 0, k == 7, [wvb, mnb], [pb])
            evac(n, mvv[:, l4, mb, :], ps[:, 0:256], [pb], [buf("mkv")])
            n += 1
    lq = tmpA[:, 2, 0:512].rearrange("p (j a d) -> p j a d", j=2, a=4)
    lb = buf("lam")
    cx.dma(SP, lq, lqk_d, writes=[tA[2]])
    for j in range(2):
        for a_ in range(2):
            tt(tmpA[:, 3, 0:64], lq[:, j, 2 * a_, :], lq[:, j, 2 * a_ + 1, :], ALU.mult, [tA[2]], [tA[3]])
            dve(lambda: nc.vector.reduce_sum(out=lam[:, j, 2 + a_:3 + a_], in_=tmpA[:, 3, 0:64], axis=mybir.AxisListType.X), [tA[3]], [lb])
            act_op(lam[:, j, 2 + a_:3 + a_], lam[:, j, 2 + a_:3 + a_], AF.Exp, [lb], [lb])
        tt(lam[:, j, 0:1], lam[:, j, 2:3], lam[:, j, 3:4], ALU.subtract, [lb], [lb])
        ts(lam[:, j, 1:2], lam[:, j, 0:1], -1.0, -LAM_INIT[2 + j], ALU.mult, ALU.add, [lb], [lb])
        ts(lam[:, j, 2:3], subln[:, j:j + 1], 1.0 - LAM_INIT[2 + j], None, ALU.mult, None, [lb, cb], [lb])

    xflat = xT[:].rearrange("p k t -> p (k t)")
    pp = xflat[:, 0:1152].rearrange("p (a g) -> p a g", g=NG)
    BRI = xflat[:, 1152:2688].rearrange("p (a g c) -> p a g c", a=2, c=16)
    CRI = xflat[:, 2688:4224].rearrange("p (a g c) -> p a g c", a=2, c=16)
    trtf = xflat[:, 4224:7296]
    trt = trtf.rearrange("p (a g) -> p a g", g=NG)
    DCOL = xflat[:, 7296:7344]
    tmpAf = tmpA[:].rearrange("p a t -> p (a t)")
    BRAW = tmpAf[:, 1024:2560].rearrange("p (a g c) -> p a g c", a=2, c=16)
    actf = act[:, 0:16, :].rearrange("p k t -> p (k t)").bitcast(F32)
    hTf = hT[:].rearrange("p k t -> p (k t)").bitcast(F32)
    sqf = sq[:].rearrange("p k t -> p (k t)").bitcast(F32)
    ppb = buf("pp")
    hsb = buf("hTscr")
    acb = buf("actscr")
    sqb_ = buf("sq")
    for l in range(2):
        R, W = [ppb], [ppb]
        cx.dma(SP, pp[:, 0:3, :], s5a_d[:, l, :, :], writes=W)
        cx.dma(SP, BRAW, s5b_d[:, l], writes=tA)
        cx.dma(SP, CRI, s5c_d[:, l], writes=W)
        cx.dma(SP, DCOL, s5d_d[:, l, :], writes=W)
        LR, LI, LDT, DT, LRDT, TH, ANG, TF, SN, MG, LAMR, LAMI, NR, DEN, FR, FI, T1, T2, TH8 = [pp[:, i, :] for i in range(19)]
        TI = pp[:, 19, :].bitcast(I32)
        act_op(DT, LDT, AF.Exp, R, W)
        tt(LRDT, LR, DT, ALU.mult, R, W)
        tt(TH, LI, DT, ALU.mult, R, W)
        ts(TH8, TH, 8.0, None, ALU.mult, None, R, W)

        def trig(out, n_, psi):
            ts(ANG, TH, float(n_), psi, ALU.mult, ALU.add, R + [cb], W)
            sin_rr(SN, ANG, TF, TI, R, W)
            act_op(MG, LRDT, AF.Exp, R, W, scale=float(n_))
            tt(out, SN, MG, ALU.mult, R, W)

        trig(LAMR, 1, CC["halfpi"])
        trig(LAMI, 1, CC["zero"])
        ts(NR, LAMR, -1.0, None, ALU.add, None, R, W)
        tt(T1, LR, LR, ALU.mult, R, W)
        tt(T2, LI, LI, ALU.mult, R, W)
        tt(DEN, T1, T2, ALU.add, R, W)
        dve(lambda: nc.vector.reciprocal(out=DEN, in_=DEN), R, W)
        tt(T1, NR, LR, ALU.mult, R, W)
        tt(T2, LAMI, LI, ALU.mult, R, W)
        tt(FR, T1, T2, ALU.add, R, W)
        tt(FR, FR, DEN, ALU.mult, R, W)
        tt(T1, LAMI, LR, ALU.mult, R, W)
        tt(T2, NR, LI, ALU.mult, R, W)
        tt(FI, T1, T2, ALU.subtract, R, W)
        tt(FI, FI, DEN, ALU.mult, R, W)
        bc = lambda a_: a_.unsqueeze(2).to_broadcast([128, NG, 16])
        tb_ = tmpAf[:, 0:768].rearrange("p (g c) -> p g c", c=16)
        RA, WA = R + tA, W + tA
        tt(BRI[:, 0], BRAW[:, 0], bc(FR), ALU.mult, RA, WA)
        tt(tb_, BRAW[:, 1], bc(FI), ALU.mult, RA, WA)
        tt(BRI[:, 0], BRI[:, 0], tb_, ALU.subtract, RA, WA)
        tt(BRI[:, 1], BRAW[:, 1], bc(FR), ALU.mult, RA, WA)
        tt(tb_, BRAW[:, 0], bc(FI), ALU.mult, RA, WA)
        tt(BRI[:, 1], BRI[:, 1], tb_, ALU.add, RA, WA)
        NV = tabc[:, 0, :]
        PSV = tabc[:, 1, :]
        b3 = lambda a_: a_.unsqueeze(2).to_broadcast([128, 64, NG])
        g3 = lambda a_: a_.unsqueeze(1).to_broadcast([128, 64, NG])
        ANGA = actf[:, 0:3072]
        TFA = actf[:, 3072:6144]
        MGA = hTf[:, 0:3072]
        TIA = tmpAf[:, 0:3072].bitcast(I32)
        a3 = lambda a_: a_.rearrange("p (a g) -> p a g", g=NG)
        RT, WT = R + [acb, hsb, cb] + tA, W + [acb, hsb] + tA
        tt(a3(ANGA), g3(TH), b3(NV), ALU.mult, RT, WT)
        tt(a3(ANGA), a3(ANGA), b3(PSV), ALU.add, RT, WT)
        sin_rr(trtf, ANGA, TFA, TIA, RT, WT)
        tt(a3(MGA), g3(LRDT), b3(NV), ALU.mult, RT, WT)
        act_op(MGA, MGA, AF.Exp, RT, WT)
        tt(trtf, trtf, MGA, ALU.mult, RT, WT)
        act_op(r8[:, l, :], LRDT, AF.Exp, R, [buf("r8")], scale=8.0)
        dve(lambda: nc.vector.memset(carry[:, l, :], 0.0), [], [buf(f"carry{l}")])
        for gb in range(6):
            g0 = 8 * gb
            gs = slice(g0, g0 + 8)
            mats = [tmpA[:, i, :].rearrange("p (g s c) -> p g s c", s=8, c=16) for i in range(4)]
            scr = sqf[:, 1024:2048].rearrange("p (g s c) -> p g s c", s=8, c=16)
            srcs_ = [(BRI, 0), (BRI, 2), (CRI, 4), (CRI, 6)]
            for mi in range(4):
                dat, st0 = srcs_[mi]
                A0 = dat[:, 0, gs, :].unsqueeze(2).to_broadcast([128, 8, 8, 16])
                A1 = dat[:, 1, gs, :].unsqueeze(2).to_broadcast([128, 8, 8, 16])
                Q0 = trt[:, st0 * 8:(st0 + 1) * 8, gs].rearrange("p s g -> p g s").unsqueeze(3).to_broadcast([128, 8, 8, 16])
                Q1 = trt[:, (st0 + 1) * 8:(st0 + 2) * 8, gs].rearrange("p s g -> p g s").unsqueeze(3).to_broadcast([128, 8, 8, 16])
                tt(mats[mi], A0, Q0, ALU.mult, R, [tA[mi]])
                tt(scr, A1, Q1, ALU.mult, R, [sqb_])
                tt(mats[mi], mats[mi], scr, ALU.add, [tA[mi], sqb_], [tA[mi]])
            W4 = tmpB[:, 2:6, :].rearrange("p a (g m) -> p a g m", m=128)
            w4b = tBb[2:6]
            Zf, Z7f, Xf, X1f = [tmpA[:, i, :].rearrange("p (g m) -> p g m", m=128) for i in range(4)]
            Tm = sqf[:, 0:1024].rearrange("p (g m) -> p g m", m=128)
            for half in range(2):
                ps, pb = next_ps()
                for g4 in range(4):
                    gi = half * 4 + g4
                    mm(ps[:, g4 * 128:(g4 + 1) * 128], Zf[:, gi, :], Xf[:, gi, :], True, True, [tA[0], tA[2]], [pb])
                tt(Tm[:, half * 4:(half + 1) * 4, :], ps[:].rearrange("p (g m) -> p g m", m=128),
                   maskT_f.unsqueeze(1).to_broadcast([128, 4, 128]), ALU.mult, [pb, cb], [sqb_])
                for g4 in range(4):
                    gi = half * 4 + g4
                    stt(W4[:, 2, gi, :], ident_f, DCOL[:, g0 + gi:g0 + gi + 1], Tm[:, gi, :], ALU.mult, ALU.add, [cb, sqb_] + R, w4b)
                for kind, rhs_ in ((0, ident_f), (1, jmat_f)):
                    ps, pb = next_ps()
                    for g4 in range(4):
                        gi = half * 4 + g4
                        mm(ps[:, g4 * 128:(g4 + 1) * 128], Z7f[:, gi, :], rhs_, True, True, [tA[1], cb], [pb])
                    evac(kind, W4[:, kind, half * 4:(half + 1) * 4, :], ps[:].rearrange("p (g m) -> p g m", m=128), [pb], w4b)
            cpy(W4[:, 3], X1f, [tA[3]], w4b)
            scr_toks.append(cx.dma(SP, s5w_scr[l, :, :, g0:g0 + 8, :].rearrange("k p g m -> p k g m"), W4, reads=w4b, writes=[buf("s5scr")]))
            ang2 = tmpAf[:, 0:2048].rearrange("p (w g k) -> p w g k", w=2, k=128)
            tt(ang2[:, 1], jv_f.unsqueeze(1).to_broadcast([128, 8, 128]), TH8[:, gs].unsqueeze(2).to_broadcast([128, 8, 128]),
               ALU.mult, R + [cb] + tA, tA)
            ts(ang2[:, 0], ang2[:, 1], PI / 2, None, ALU.add, None, tA, tA)
            sin_rr(tmpAf[:, 2048:4096], tmpAf[:, 0:2048], hTf[:, 0:2048], hTf[:, 2048:4096].bitcast(I32), tA + [hsb], tA + [hsb])
            scr_toks.append(cx.dma(SP, rot_scr[l, :, :, g0:g0 + 8, :].rearrange("w p g k -> p w g k"),
                                   tmpAf[:, 2048:4096].rearrange("p (w g k) -> p w g k", w=2, k=128), reads=tA, writes=[buf("s5scr")]))
    POOL.wait(scr_toks)
    bar = [Tok(e.sem, e.count, e.key) for e in (PE, ACT, DVE) if e.count > 0]
    SP.wait(bar)

    out_toks = []
    xsrc = xT_d.rearrange("(k p) t -> p k t", p=128)
    for ta in range(na_tiles):
        for st in range(NST):
            cx.dma(SP, xT[:, :, SL[st]], xsrc[:, :, ta * TT + st * 512: ta * TT + (st + 1) * 512], writes=[xb[st]])
        rope_tables(pos_all_d[:, ta * TT:(ta + 1) * TT])
        for l in range(2):
            ffn(l, 0)
            if (f"ffn1_{l}", ta) in dbgs:
                out_toks.append(cx.dma(SP, dbgs[(f"ffn1_{l}", ta)], xT[:].rearrange("p k t -> p (k t)"), reads=xb))
            in_proj(l)
            mem_attn(l)
            srcs = s5_mixer(l)
            if (f"tokout_{l}", ta) in dbgs:
                out_toks.append(cx.dma(SP, dbgs[(f"tokout_{l}", ta)], act[:, 0:16, :].rearrange("p k t -> p (k t)"),
                                       reads=[b_ for s_ in srcs for b_ in s_[1]]))
            out_proj(l, srcs)
            ffn(l, 1)
            if (f"x_{l}", ta) in dbgs:
                out_toks.append(cx.dma(SP, dbgs[(f"x_{l}", ta)], xT[:].rearrange("p k t -> p (k t)"), reads=xb))
        kv_tile(ta)
        select_store(ta)

    SP.wait(kv_toks)
    x1src = x1_scr.rearrange("(k p) t -> p k t", p=128)
    for tb in range(nb_tiles):
        for st in range(NST):
            cx.dma(SP, xT[:, :, SL[st]], x1src[:, :, tb * TT + st * 512: tb * TT + (st + 1) * 512],
                   reads=[buf("x1scr")], writes=[xb[st]])
        rope_tables(pos_mine_d[:, tb * TT:(tb + 1) * TT])
        for l in range(2, 4):
            ffn(l, 0)
            in_proj(l)
            mem_attn(l)
            srcs = diff_attn(l, tb)
            if (f"tokout_{l}", tb) in dbgs:
                out_toks.append(cx.dma(SP, dbgs[(f"tokout_{l}", tb)], act[:, 0:16, :].rearrange("p k t -> p (k t)"),
                                       reads=[b_ for s_ in srcs for b_ in s_[1]]))
            out_proj(l, srcs)
            ffn(l, 1)
        for st in range(NST):
            of = tmpA[:].rearrange("p a (k2 t) -> p (a k2) t", k2=2)
            rmsnorm(xT[:, :, SL[st]], xb[st], gains[:, 14, :], of, list(tA), 512, st)
            out_toks.append(cx.dma(SP, outT_d[:, tb * TT + st * 512: tb * TT + (st + 1) * 512].rearrange("(k p) t -> p k t", p=128),
                                   of, reads=list(tA)))

    SP.wait(out_toks)


def _consts():
    cst = np.zeros((128, 16), np.float32)
    hp = PI / 2
    lo, hi = slice(0, 64), slice(64, 128)
    cst[lo, 0], cst[hi, 0] = 0 + hp, -hp + hp
    cst[lo, 1], cst[hi, 1] = hp + hp, 0 + hp
    cst[lo, 2], cst[hi, 2] = 0 + hp, hp + hp
    cst[lo, 3], cst[hi, 3] = hp + hp, PI + hp
    inv = (500000.0 ** (-np.arange(0, 16, 2, dtype=np.float32) / 16)).astype(np.float32)
    for base in (0, 64):
        cst[base:base + 8, 4] = inv
        cst[base + 8:base + 16, 4] = inv
        cst[base:base + 8, 5] = -inv
        cst[base + 8:base + 16, 5] = inv
    cst[:, 6] = hp
    ew = np.zeros((128, 8, 240), np.float32)
    for gi in range(8):
        for c in range(16):
            ew[16 * gi + c, gi, 112 + c] = 1.0
    cm = np.zeros((128, 5, 128), np.float32)
    cm[:, 0, :] = np.eye(128)
    for p in range(64):
        cm[64 + p, 1, p] = 1.0
        cm[p, 1, 64 + p] = -1.0
    sidx = np.arange(128) // 16
    cm[:, 2, :] = (sidx[None, :] >= sidx[:, None]).astype(np.float32)
    cm[:, 3, :] = (np.arange(128, dtype=np.float32) + 1.0)[None, :]
    cm[:, 4, :] = 1.0
    cmb = np.zeros((128, 2, 128), np.float32)
    cmb[:, 0, :] = 1.0
    for base in (0, 64):
        for j in range(8):
            cmb[base + 8 + j, 1, base + j] = 1.0
            cmb[base + j, 1, base + 8 + j] = 1.0
    tabc = np.zeros((128, 2, 64), np.float32)
    sets = [(lambda s_: -s_, 0), (lambda s_: -s_, 1), (lambda s_: 7 - s_, 0), (lambda s_: 7 - s_, 1),
            (lambda s_: s_, 2), (lambda s_: s_, 3), (lambda s_: s_ + 1, 2), (lambda s_: s_ + 1, 3)]
    for k_, (fn, pc) in enumerate(sets):
        for s_ in range(8):
            tabc[:, 0, k_ * 8 + s_] = fn(s_)
            tabc[:, 1, k_ * 8 + s_] = cst[:, pc]
    return cst, ew.astype(ml_dtypes.bfloat16), cm, cmb.astype(ml_dtypes.bfloat16), tabc


def _masks(h):
    m = np.zeros((128, 8, 128), np.float32)
    s = np.arange(128)[:, None]
    for r in range(8):
        jj = r // 2
        key = r * 128 + s
        qry = (2 * jj + h) * 128 + np.arange(128)[None, :]
        m[:, r, :] = (key <= qry)
    return m.astype(ml_dtypes.bfloat16)


def make_in_maps(inputs):
    f = lambda a: np.ascontiguousarray(np.asarray(a))
    cst, ew, cm, cmb, tabc = _consts()
    gains = np.stack([*inputs["ln_ffn1"], *inputs["ln_mix"], *inputs["ln_ffn2"],
                      inputs["ln_mem"], inputs["ln_kv"], inputs["ln_final"]], 0)
    gains = f(gains.reshape(15, 8, 128).transpose(2, 0, 1))
    rep = lambda a: np.concatenate([a, a], 0)
    s5a = np.stack([np.stack([inputs["s5_a_re"][l].T, inputs["s5_a_im"][l].T,
                              np.broadcast_to(inputs["s5_log_dt"][l][None, :], (64, NG))], 0) for l in range(2)], 0)
    s5a = f(rep(s5a.transpose(2, 0, 1, 3)))
    s5b = np.stack([np.stack([inputs["s5_b_re"][l], inputs["s5_b_im"][l]], 0) for l in range(2)], 0)
    s5b = f(rep(s5b.transpose(3, 0, 1, 2, 4)))
    s5c = np.stack([np.stack([inputs["s5_c_re"][l], inputs["s5_c_im"][l]], 0) for l in range(2)], 0)
    s5c = f(rep(s5c.transpose(4, 0, 1, 2, 3)))
    s5d = f(np.tile(inputs["s5_d"].transpose(2, 0, 1), (8, 1, 1)))
    lqk = np.stack([np.stack([inputs["diff_lq1"][j], inputs["diff_lk1"][j], inputs["diff_lq2"][j],
                              inputs["diff_lk2"][j]], 0) for j in range(2)], 0)
    lqk = f(np.broadcast_to(lqk[None], (128, 2, 4, 64)))
    subln = f(inputs["diff_subln"].T)
    shared = dict(gains=gains, s5a=s5a, s5b=s5b, s5c=s5c, s5d=s5d, lqk=lqk, subln=subln,
                  cst=cst, ew=ew, cm=cm, cmb=cmb, tabc=tabc)
    for k in ["ffn1_in", "ffn2_in", "ffn1_out", "ffn2_out", "w_mix_in", "w_mix_out", "w_mem_kv",
              "s5_w_glu", "w_kv_shared"]:
        shared[k] = f(inputs[k])
    in_maps = []
    for c in range(8):
        b, h = c // 2, c % 2
        m = dict(shared)
        m["xT"] = f(inputs["x"][b].T)
        m["memT"] = f(inputs["mem"][b].T)
        pos = np.asarray(inputs["positions"][b]).astype(np.int32)
        m["pos_all"] = f(np.broadcast_to(pos[None, :], (128, S)))
        mine = pos.reshape(16, 2, 128)[:, h, :].reshape(-1)
        m["pos_mine"] = f(np.broadcast_to(mine[None, :], (128, S // 2)))
        m["masks"] = _masks(h)
        fl = np.zeros((128, 2), np.float32)
        fl[:, 0] = 1 - h
        fl[:, 1] = h
        m["flags"] = fl
        in_maps.append(m)
    return in_maps


def kernel(**inputs):
    in_maps = make_in_maps(inputs)
    nc = build_program()
    res = run_bass_kernel_spmd(nc, in_maps, core_ids=list(range(8)))
    out = np.zeros((NB, S, D), np.float32)
    for c in range(8):
        b, h = c // 2, c % 2
        o = np.asarray(res.results[c]["outT"]).T.reshape(16, 128, D)
        out[b].reshape(16, 2, 128, D)[:, h] = o
    return out
```

```python
import math
from contextlib import ExitStack

import numpy as np
import ml_dtypes

import concourse.bass as bass
import concourse.mybir as mybir
from concourse.bass_utils import run_bass_kernel_spmd

F32 = mybir.dt.float32
BF16 = mybir.dt.bfloat16
I32 = mybir.dt.int32
ALU = mybir.AluOpType
AF = mybir.ActivationFunctionType

D = 1024
S = 4096
NB = 4
DFF = 2816
NFC = DFF // 128
TT = 1024
NST = TT // 512
KCH = TT // 8
NG = 48
EPS = 1e-6
PI = math.pi
C1 = 6.28125
C2 = 2.0 * PI - C1
SLOT = 3072
NSLOT = 6
LAM_INIT = [0.8 - 0.6 * math.exp(-0.3 * i) for i in range(4)]


class Tok:
    __slots__ = ("sem", "val", "key")

    def __init__(self, sem, val, key):
        self.sem, self.val, self.key = sem, val, key


class Buf:
    __slots__ = ("w", "r", "name")

    def __init__(self, name=""):
        self.w = None
        self.r = {}
        self.name = name


class Eng:
    def __init__(self, h, sem, key):
        self.h, self.sem, self.key = h, sem, key
        self.count = 0
        self.waited = {}
        self.pending = []

    def wait(self, toks):
        for t in toks:
            if t is None:
                continue
            if t.key == self.key and self.key == "pe":
                continue
            assert t.val is not None, "waiting on unresolved token"
            if self.waited.get(t.key, 0) < t.val:
                self.h.wait_ge(t.sem, t.val)
                self.waited[t.key] = t.val


class Ctx:
    def __init__(self, nc, es):
        self.nc = nc
        self.es = es
        mk = lambda n: es.enter_context(nc.semaphore(n))
        self.pe = Eng(nc.tensor, mk("s_pe"), "pe")
        self.act = Eng(nc.scalar, mk("s_act"), "act")
        self.dve = Eng(nc.vector, mk("s_dve"), "dve")
        self.pool = Eng(nc.gpsimd, mk("s_pool"), "pool")
        self.sp = Eng(nc.sync, mk("s_sp"), "sp")
        self.dsems = [(mk(f"s_d{i}"), f"d{i}") for i in range(24)]
        self.dcnt = [0] * len(self.dsems)
        self.dlast = [None] * len(self.dsems)
        self.dnext = 0

    def deps(self, reads, writes):
        toks = []
        for b in reads:
            if b.w is not None:
                toks.append(b.w)
        for b in writes:
            if b.w is not None:
                toks.append(b.w)
            toks.extend(b.r.values())
        return toks

    def op(self, eng, fn, reads=(), writes=(), signal=True):
        eng.wait(self.deps(reads, writes))
        ins = fn()
        if signal:
            eng.count += 1
            ins.then_inc(eng.sem, 1)
            tok = Tok(eng.sem, eng.count, eng.key)
            for p in eng.pending:
                p.val = eng.count
            eng.pending = []
        else:
            tok = Tok(eng.sem, None, eng.key)
            eng.pending.append(tok)
        for b in reads:
            b.r[eng.key] = tok
        for b in writes:
            b.w = tok
            b.r = {}
        return tok

    def dma(self, q, out, in_, reads=(), writes=(), semidx=None):
        if semidx is None:
            semidx = self.dnext
            self.dnext = (self.dnext + 1) % 16
        sem, key = self.dsems[semidx]
        q.wait([self.dlast[semidx]])
        q.wait(self.deps(reads, writes))
        self.dcnt[semidx] += 1
        q.h.dma_start(out=out, in_=in_).then_inc(sem, 16)
        tok = Tok(sem, 16 * self.dcnt[semidx], key)
        self.dlast[semidx] = tok
        for b in reads:
            b.r[key] = tok
        for b in writes:
            b.w = tok
            b.r = {}
        return tok


def build_program(na_tiles=4, nb_tiles=2, dbg=None):
    nc = bass.Bass("TRN2", target_bir_lowering=False)
    es = ExitStack()
    with es:
        _build(nc, es, na_tiles, nb_tiles, dbg)
    return nc


def _build(nc, es, na_tiles, nb_tiles, dbg):
    cx = Ctx(nc, es)
    PE, ACT, DVE, POOL, SP = cx.pe, cx.act, cx.dve, cx.pool, cx.sp

    def din(name, shape, dt=F32):
        return nc.dram_tensor(name, list(shape), dt, kind="ExternalInput").ap()

    def dscr(name, shape, dt):
        return nc.dram_tensor(name, list(shape), dt, kind="Internal").ap()

    def sb(name, shape, dt):
        return es.enter_context(nc.sbuf_tensor("sb_" + name, list(shape), dt))

    xT_d = din("xT", [D, S])
    memT_d = din("memT", [D, 256])
    pos_all_d = din("pos_all", [128, S], I32)
    pos_mine_d = din("pos_mine", [128, S // 2], I32)
    masks_d = din("masks", [128, 8, 128], BF16)
    flags_d = din("flags", [128, 2])
    gains_d = din("gains", [128, 15, 8])
    ffn_in_d = [din("ffn1_in", [4, D, 2 * DFF]), din("ffn2_in", [4, D, 2 * DFF])]
    ffn_out_d = [din("ffn1_out", [4, DFF, D]), din("ffn2_out", [4, DFF, D])]
    wmi_d = din("w_mix_in", [4, D, D])
    wmo_d = din("w_mix_out", [4, D, D])
    wmkv_d = din("w_mem_kv", [4, D, 512])
    wglu_d = din("s5_w_glu", [2, 768, 768])
    wkv_d = din("w_kv_shared", [D, 1536])
    s5a_d = din("s5a", [128, 2, 3, NG])
    s5b_d = din("s5b", [128, 2, 2, NG, 16])
    s5c_d = din("s5c", [128, 2, 2, NG, 16])
    s5d_d = din("s5d", [128, 2, NG])
    lqk_d = din("lqk", [128, 2, 4, 64])
    subln_d = din("subln", [128, 2])
    cst_d = din("cst", [128, 16])
    ew_d = din("ew", [128, 8, 240], BF16)
    cm_d = din("cm", [128, 5, 128])
    cmb_d = din("cmb", [128, 2, 128], BF16)
    tabc_d = din("tabc", [128, 2, 64])
    outT_d = nc.dram_tensor("outT", [D, S // 2], F32, kind="ExternalOutput").ap()
    dbgs = {}
    for d_ in (dbg or []):
        dbgs[(d_[0], d_[1])] = nc.dram_tensor(f"dbg_{d_[0]}_{d_[1]}", [128, d_[2]], BF16 if d_[0].startswith("tokout") else F32, kind="ExternalOutput").ap()

    kT_scr = dscr("kT_scr", [768, S], BF16)
    v_scr = dscr("v_scr", [S, 768], BF16)
    x1_scr = dscr("x1_scr", [D, S // 2], F32)
    s5w_scr = dscr("s5w_scr", [2, 4, 128, NG, 128], BF16)
    rot_scr = dscr("rot_scr", [2, 2, 128, NG, KCH], F32)

    xT = sb("xT", [128, 8, TT], F32)
    hT = sb("hT", [128, 8, TT], BF16)
    act = sb("act", [128, NFC, TT], BF16)
    sq = sb("sq", [128, 8, 512], BF16)
    sq2 = sb("sq2", [128, 8, 512], BF16)
    rstd = sb("rstd", [128, 2, 512], F32)
    uT = act[:, 0:6, :]
    mqT = act[:, 6:8, :]
    cat = act[:, 8:16, :]
    tmpB = act[:, 16:22, :]
    ring = sb("ring", [128, NSLOT, SLOT], BF16)
    ew = sb("ew", [128, 8, 240], BF16)
    cm = sb("cm", [128, 5, 128], F32)
    cmb = sb("cmb", [128, 2, 128], BF16)
    cst = sb("cst", [128, 16], F32)
    tabc = sb("tabc", [128, 2, 64], F32)
    gains = sb("gains", [128, 15, 8], F32)
    flags = sb("flags", [128, 2], F32)
    masks = sb("masks", [128, 8, 128], BF16)
    ropeT = sb("ropeT", [128, 2, TT], F32)
    mkT = sb("mkT", [128, 4, 2, 256], BF16)
    mvv = sb("mvv", [128, 4, 2, 256], BF16)
    carry = sb("carry", [128, 2, NG], F32)
    r8 = sb("r8", [128, 2, NG], F32)
    lam = sb("lam", [128, 2, 4], F32)
    subln = sb("subln", [128, 2], F32)
    tmpA = sb("tmpA", [128, 4, 1024], F32)
    kvbuf = hT[:].rearrange("p k t -> p (k t)").rearrange("p (a n) -> p a n", a=2)
    kvbuf2 = sb("kvbuf2", [128, 2, 4096], BF16)

    ones_b = cmb[:, 0, :]
    perm_b = cmb[:, 1, :]
    ident_f = cm[:, 0, :]
    jmat_f = cm[:, 1, :]
    maskT_f = cm[:, 2, :]
    jv_f = cm[:, 3, :]
    ones_f = cm[:, 4, :]

    psum = [es.enter_context(nc.psum_tensor(f"ps{i}", [128, 512], F32)) for i in range(8)]
    ps_b = [Buf(f"ps{i}") for i in range(8)]
    ps_state = {"n": 0, "held": set()}

    def next_ps(hold=False):
        for _ in range(17):
            i = ps_state["n"]
            ps_state["n"] = (i + 1) % 8
            if i not in ps_state["held"] and not (ps_b[i].w is not None and not ps_b[i].r):
                break
        else:
            raise RuntimeError("no free PSUM bank")
        if hold:
            ps_state["held"].add(i)
        return psum[i], ps_b[i]

    def release_ps(pb):
        ps_state["held"].discard(ps_b.index(pb))

    B = {}

    def buf(name):
        if name not in B:
            B[name] = Buf(name)
        return B[name]

    slot_b = [Buf(f"slot{i}") for i in range(NSLOT)]
    wstate = {"n": 0}

    def wload(src_ap, shape, f32=False):
        n = wstate["n"]
        wstate["n"] = n + 1
        i = n % NSLOT
        nel = 1
        for s_ in shape[1:]:
            nel *= s_
        if f32:
            assert 2 * nel <= SLOT
            dst = ring[:, i, 0:2 * nel].bitcast(F32)
        else:
            assert nel <= SLOT, (shape, nel)
            dst = ring[:, i, 0:nel]
        if len(shape) == 3:
            dst = dst.rearrange("p (a b) -> p a b", b=shape[2])
        elif len(shape) == 4:
            dst = dst.rearrange("p (a b c) -> p a b c", b=shape[2], c=shape[3])
        cx.dma(POOL, dst, src_ap, writes=[slot_b[i]], semidx=16 + i)
        return dst, slot_b[i]

    def mm(out, lhsT, rhs, start, stop, reads, writes):
        return cx.op(PE, lambda: nc.tensor.matmul(out, lhsT, rhs, start=start, stop=stop),
                     reads=reads, writes=writes, signal=stop)

    def act_op(out, in_, func, reads, writes, **kw):
        return cx.op(ACT, lambda: nc.scalar.activation(out=out, in_=in_, func=func, **kw),
                     reads=reads, writes=writes)

    def dve(fn, reads, writes):
        return cx.op(DVE, fn, reads=reads, writes=writes)

    def tt(out, a, b, op, reads, writes):
        return dve(lambda: nc.vector.tensor_tensor(out=out, in0=a, in1=b, op=op), reads, writes)

    def ts(out, a, s1, s2, op0, op1, reads, writes):
        if s2 is None:
            return dve(lambda: nc.vector.tensor_scalar(out=out, in0=a, scalar1=s1, scalar2=None, op0=op0), reads, writes)
        return dve(lambda: nc.vector.tensor_scalar(out=out, in0=a, scalar1=s1, scalar2=s2, op0=op0, op1=op1), reads, writes)

    def stt(out, a, sc, b, op0, op1, reads, writes):
        return dve(lambda: nc.vector.scalar_tensor_tensor(out=out, in0=a, scalar=sc, in1=b, op0=op0, op1=op1), reads, writes)

    def cpy(out, a, reads, writes):
        return dve(lambda: nc.vector.tensor_copy(out=out, in_=a), reads, writes)

    def evac(i, out, a, reads, writes):
        if i % 2 == 0:
            return act_op(out, a, AF.Identity, reads, writes)
        return cpy(out, a, reads, writes)

    cb = buf("consts")
    for dst, src in [(ew, ew_d), (cm, cm_d), (cmb, cmb_d), (cst, cst_d), (gains, gains_d),
                     (flags, flags_d), (masks, masks_d), (subln, subln_d), (tabc, tabc_d)]:
        cx.dma(SP, dst[:], src, writes=[cb])
    CC = {n: cst[:, i:i + 1] for i, n in enumerate(
        ["psiQ1", "psiQ2", "psiP1", "psiP2", "invc", "invs", "halfpi", "zero"])}
    tA = [buf(f"tmpA{i}") for i in range(4)]
    tBb = [buf(f"tmpB{i}") for i in range(6)]

    def sin_rr(out, ang, tmpf, tmpi, rb, wb):
        ts(tmpi, ang, 1.0 / (2 * PI), None, ALU.mult, None, rb, wb)
        cpy(tmpf, tmpi, rb, wb)
        stt(ang, tmpf, -C1, ang, ALU.mult, ALU.add, rb, wb)
        stt(ang, tmpf, -C2, ang, ALU.mult, ALU.add, rb, wb)
        ts(ang, ang, PI, -PI, ALU.min, ALU.max, rb, wb)
        act_op(out, ang, AF.Sin, rb, wb)

    def rmsnorm(src, src_b, gcol, dst, dst_b, n, par=0):
        sqb = buf("sq")
        act_op(sq[:, :, 0:n], src, AF.Square, [src_b], [sqb])
        ps, pb = next_ps()
        for k in range(8):
            mm(ps[:, 0:n], ones_b, sq[:, k, 0:n], k == 0, k == 7, [sqb, cb], [pb])
        rb = buf(f"rstd{par}")
        r = rstd[:, par, 0:n]
        act_op(r, ps[:, 0:n], AF.Sqrt, [pb], [rb], scale=1.0 / D, bias=EPS)
        dve(lambda: nc.vector.reciprocal(out=r, in_=r), [rb], [rb])
        for k in range(8):
            stt(dst[:, k, :], src[:, k, :], gcol[:, k:k + 1], r, ALU.mult, ALU.mult, [src_b, rb, cb],
                dst_b if isinstance(dst_b, list) else [dst_b])

    xb = [buf(f"x{st}") for st in range(NST)]
    hb = [buf(f"h{st}") for st in range(NST)]
    SL = [slice(st * 512, (st + 1) * 512) for st in range(NST)]

    def norm_x(gidx):
        gcol = gains[:, gidx, :]
        sqbs = [buf("sq"), buf("sq2")]
        sqs = [sq, sq2]
        pss = []
        for st in range(NST):
            act_op(sqs[st][:, :, :], xT[:, :, SL[st]], AF.Square, [xb[st]], [sqbs[st]])
        for st in range(NST):
            ps, pb = next_ps()
            for k in range(8):
                mm(ps[:], ones_b, sqs[st][:, k, :], k == 0, k == 7, [sqbs[st], cb], [pb])
            pss.append((ps, pb))
        for st in range(NST):
            act_op(rstd[:, st, :], pss[st][0][:], AF.Sqrt, [pss[st][1]], [buf(f"rstd{st}")], scale=1.0 / D, bias=EPS)
        for st in range(NST):
            r = rstd[:, st, :]
            dve(lambda: nc.vector.reciprocal(out=r, in_=r), [buf(f"rstd{st}")], [buf(f"rstd{st}")])
        for st in range(NST):
            for k in range(8):
                stt(hT[:, k, SL[st]], xT[:, k, SL[st]], gcol[:, k:k + 1], rstd[:, st, :], ALU.mult, ALU.mult,
                    [xb[st], buf(f"rstd{st}"), cb], [hb[st]])

    act_guard = []

    def ffn(layer, which):
        norm_x((0 if which == 0 else 8) + layer)
        w_in = ffn_in_d[which][layer].rearrange("(k p) n -> p k n", p=128)
        w_out = ffn_out_d[which][layer].rearrange("(f p) n -> p f n", p=128)
        actb = [[buf(f"act{j}_{st}") for st in range(NST)] for j in range(NFC)]
        if act_guard:
            DVE.wait(act_guard)
            del act_guard[:]
        for j in range(NFC):
            wg, wgb = wload(w_in[:, :, j * 128:(j + 1) * 128], [128, 8, 128])
            wu, wub = wload(w_in[:, :, DFF + j * 128: DFF + (j + 1) * 128], [128, 8, 128])
            for st in range(NST):
                sl = SL[st]
                pg, pgb = next_ps()
                for k in range(8):
                    mm(pg[:], wg[:, k, :], hT[:, k, sl], k == 0, k == 7, [wgb, hb[st]], [pgb])
                pu, pub = next_ps()
                for k in range(8):
                    mm(pu[:], wu[:, k, :], hT[:, k, sl], k == 0, k == 7, [wub, hb[st]], [pub])
                sg = tmpA[:, st, 0:512]
                act_op(sg, pg[:], AF.Silu, [pgb], [tA[st]])
                tt(act[:, j, sl], sg, pu[:], ALU.mult, [tA[st], pub], [actb[j][st]])
        for dch in range(8):
            wo, wob = wload(w_out[:, :, dch * 128:(dch + 1) * 128], [128, NFC, 128])
            for st in range(NST):
                sl = SL[st]
                po, pob = next_ps()
                for j in range(NFC):
                    mm(po[:], wo[:, j, :], act[:, j, sl], j == 0, j == NFC - 1, [wob, actb[j][st]], [pob])
                stt(xT[:, dch, sl], po[:], 0.5, xT[:, dch, sl], ALU.mult, ALU.add, [pob, xb[st]], [xb[st]])

    ub = [[buf(f"u{k}_{st}") for st in range(NST)] for k in range(6)]
    mqb = [[buf(f"mq{k}_{st}") for st in range(NST)] for k in range(2)]
    catb = [[buf(f"cat{k}_{st}") for st in range(NST)] for k in range(8)]

    def in_proj(layer):
        norm_x(4 + layer)
        w = wmi_d[layer].rearrange("(k p) n -> p k n", p=128)
        n = 0
        for cc in range(4):
            wc, wcb = wload(w[:, :, cc * 256:(cc + 1) * 256], [128, 8, 256])
            for half in range(2):
                oc = 2 * cc + half
                for st in range(NST):
                    ps, pb = next_ps()
                    for k in range(8):
                        mm(ps[:], wc[:, k, half * 128:(half + 1) * 128], hT[:, k, SL[st]], k == 0, k == 7, [wcb, hb[st]], [pb])
                    if oc < 6:
                        evac(n, uT[:, oc, SL[st]], ps[:], [pb], [ub[oc][st]])
                    else:
                        evac(n, mqT[:, oc - 6, SL[st]], ps[:], [pb], [mqb[oc - 6][st]])
                    n += 1

    def mem_attn(layer):
        steps = [(st, c, hd2, mb) for st in range(NST) for c in range(2) for hd2 in range(2) for mb in range(2)]
        PRE = 3
        sps = {}

        def issue_s(n_):
            st_, c_, hd2_, mb_ = steps[n_]
            base = hd2_ * 64
            pss, psb = next_ps()
            mm(pss[:], mkT[base:base + 64, layer, c_, mb_ * 128:(mb_ + 1) * 128], mqT[base:base + 64, c_, SL[st_]],
               True, True, [buf("mkv"), mqb[c_][st_]], [psb])
            sps[n_] = (pss, psb)

        for n_ in range(PRE):
            issue_s(n_)
        po = pl = None
        for n_, (st, c, hd2, mb) in enumerate(steps):
            if hd2 == 0 and mb == 0:
                po, pob = next_ps(hold=True)
                pl, plb = next_ps(hold=True)
            if n_ + PRE < len(steps):
                issue_s(n_ + PRE)
            base = hd2 * 64
            hd = 2 * c + hd2
            pss, psb = sps.pop(n_)
            e = tmpB[:, n_ % 6, 0:512]
            eb = tBb[n_ % 6]
            act_op(e, pss[:], AF.Exp, [psb], [eb], scale=0.125)
            mm(po[base:base + 64, :], mvv[:, layer, mb, hd * 64:(hd + 1) * 64], e, mb == 0, mb == 1, [buf("mkv"), eb], [pob])
            mm(pl[base:base + 64, :], ones_b[:, 0:64], e, mb == 0, mb == 1, [cb, eb], [plb])
            if hd2 == 1 and mb == 1:
                rl = tmpA[:, 2, 0:512]
                dve(lambda: nc.vector.reciprocal(out=rl, in_=pl[:]), [plb], [tA[2]])
                tt(cat[:, 6 + c, SL[st]], po[:], rl, ALU.mult, [pob, tA[2]], [catb[6 + c][st]])
                release_ps(pob)
                release_ps(plb)

    def out_proj(layer, srcs):
        w = wmo_d[layer].rearrange("(k p) n -> p k n", p=128)
        for cc in range(4):
            wc, wcb = wload(w[:, :, cc * 256:(cc + 1) * 256], [128, 8, 256])
            for half in range(2):
                dch = 2 * cc + half
                for st in range(NST):
                    ps, pb = next_ps()
                    for k in range(8):
                        mm(ps[:], wc[:, k, half * 128:(half + 1) * 128], srcs[k][0][:, SL[st]], k == 0, k == 7,
                           [wcb, srcs[k][1][st]], [pb])
                    tt(xT[:, dch, SL[st]], ps[:], xT[:, dch, SL[st]], ALU.add, [pb, xb[st]], [xb[st]])

    def rope_tables(pos_src):
        rb_ = buf("rope")
        allA = tA
        posi = tmpA[:, 0, :].bitcast(I32)
        cx.dma(SP, posi, pos_src, writes=[tA[0]])
        cpy(tmpA[:, 1, :], posi, [tA[0]], [tA[1]])
        for which, icol, pcol in ((0, "invc", "halfpi"), (1, "invs", "zero")):
            ts(tmpA[:, 2, :], tmpA[:, 1, :], CC[icol], CC[pcol], ALU.mult, ALU.add, [tA[1], cb], [tA[2]])
            sin_rr(ropeT[:, which, :], tmpA[:, 2, :], tmpA[:, 3, :], tmpA[:, 0, :].bitcast(I32), [tA[2], tA[3], tA[0]], [tA[2], tA[3], tA[0], rb_])

    def rope_inplace(T6, bufs6):
        rb_ = buf("rope")
        n = 0
        for ch in range(6):
            for st in range(NST):
                ps, pb = next_ps()
                mm(ps[:], perm_b, T6[:, ch, SL[st]], True, True, [cb, bufs6[ch][st]], [pb])
                i0, i1 = (n % 2) * 2, (n % 2) * 2 + 1
                n += 1
                tt(tmpA[:, i0, 0:512], ps[:], ropeT[:, 1, SL[st]], ALU.mult, [pb, rb_], [tA[i0]])
                tt(tmpA[:, i1, 0:512], T6[:, ch, SL[st]], ropeT[:, 0, SL[st]], ALU.mult, [bufs6[ch][st], rb_], [tA[i1]])
                tt(T6[:, ch, SL[st]], tmpA[:, i0, 0:512], tmpA[:, i1, 0:512], ALU.add, [tA[i0], tA[i1]], [bufs6[ch][st]])

    def s5_mixer(l):
        carb = buf(f"carry{l}")

        def stage_a(gb):
            g0 = 8 * gb
            par = gb % 2
            wP, wPb = wload(s5w_scr[l, 0:2, :, g0:g0 + 8, :].rearrange("k p g m -> p k g m"), [128, 2, 8, 128])
            Ug = tmpB[:, 3 * par, :].rearrange("p (g k) -> p g k", k=128)
            ugb = tBb[3 * par]
            uv = uT[:, gb, :].rearrange("p (k s) -> p s k", s=8)
            for half in range(2):
                ps, pb = next_ps()
                for g4 in range(4):
                    gi = half * 4 + g4
                    for s in range(8):
                        mm(ps[:, g4 * 128:(g4 + 1) * 128], ew[:, gi, 112 - 16 * s:240 - 16 * s], uv[:, s, :], s == 0, s == 7,
                           [cb, ub[gb][0], ub[gb][1]], [pb])
                evac(half, tmpB[:, 3 * par, half * 512:(half + 1) * 512], ps[:], [pb], [ugb])
            pS = [next_ps(hold=True) for _ in range(2)]
            pSp = [next_ps(hold=True) for _ in range(2)]
            for half in range(2):
                for g4 in range(4):
                    gi = half * 4 + g4
                    mm(pS[half][0][:, g4 * 128:(g4 + 1) * 128], wP[:, 0, gi, :], Ug[:, gi, :], True, True, [wPb, ugb], [pS[half][1]])
                    mm(pSp[half][0][:, g4 * 128:(g4 + 1) * 128], wP[:, 1, gi, :], Ug[:, gi, :], True, True, [wPb, ugb], [pSp[half][1]])
            return Ug, ugb, pS, pSp

        nxt = stage_a(0)
        for gb in range(6):
            g0 = 8 * gb
            Ug, ugb, pS, pSp = nxt
            wB, wBb = wload(s5w_scr[l, 2:4, :, g0:g0 + 8, :].rearrange("k p g m -> p k g m"), [128, 2, 8, 128])
            cosr, cosb = wload(rot_scr[l, 0, :, g0:g0 + 8, :], [128, 8, 128], f32=True)
            sinr, sinb = wload(rot_scr[l, 1, :, g0:g0 + 8, :], [128, 8, 128], f32=True)
            cosr2 = cosr.rearrange("p g k -> p (g k)")
            sinr2 = sinr.rearrange("p g k -> p (g k)")
            Hp = tmpB[:, 1, :].rearrange("p (g k) -> p g k", k=128)
            Yg = tmpB[:, 2, :].rearrange("p (g k) -> p g k", k=128)
            for half in range(2):
                hs = slice(half * 512, (half + 1) * 512)
                tt(tmpA[:, 0, hs], pS[half][0][:], cosr2[:, hs], ALU.mult, [pS[half][1], cosb], [tA[0]])
                tt(tmpA[:, 1, hs], pSp[half][0][:], sinr2[:, hs], ALU.mult, [pSp[half][1], sinb], [tA[1]])
                release_ps(pS[half][1])
                release_ps(pSp[half][1])
            tt(tmpA[:, 2, :], tmpA[:, 0, :], tmpA[:, 1, :], ALU.add, [tA[0], tA[1]], [tA[2]])
            if gb + 1 < 6:
                nxt = stage_a(gb + 1)
            G = tmpA[:, 3, :].rearrange("p (g k) -> p g k", k=128)
            Sr = tmpA[:, 2, :].rearrange("p (g k) -> p g k", k=128)
            for gi in range(8):
                g = g0 + gi
                dve(lambda: nc.vector.tensor_tensor_scan(out=G[:, gi, :], data0=r8[:, l, g:g + 1].to_broadcast([128, 128]),
                                                          data1=Sr[:, gi, :], initial=carry[:, l, g:g + 1],
                                                          op0=ALU.mult, op1=ALU.add),
                    [tA[2], carb, buf("r8")], [tA[3]])
            for half in range(2):
                hs = slice(half * 512, (half + 1) * 512)
                ps, pb = next_ps()
                mm(ps[:], jmat_f, tmpA[:, 3, hs], True, True, [cb, tA[3]], [pb])
                tt(tmpA[:, 0, hs], tmpA[:, 3, hs], cosr2[:, hs], ALU.mult, [tA[3], cosb], [tA[0]])
                tt(tmpA[:, 1, hs], ps[:], sinr2[:, hs], ALU.mult, [pb, sinb], [tA[1]])
            t1 = tmpA[:, 0, :].rearrange("p (g k) -> p g k", k=128)
            t2 = tmpA[:, 1, :].rearrange("p (g k) -> p g k", k=128)
            cpy(Hp[:, :, 0:1], carry[:, l, g0:g0 + 8].unsqueeze(2), [carb], [tBb[1]])
            tt(Hp[:, :, 1:128], t1[:, :, 0:127], t2[:, :, 0:127], ALU.subtract, [tA[0], tA[1]], [tBb[1]])
            tt(carry[:, l, g0:g0 + 8].unsqueeze(2), t1[:, :, 127:128], t2[:, :, 127:128], ALU.subtract, [tA[0], tA[1]], [carb])
            pY = [next_ps(hold=True) for _ in range(2)]
            for half in range(2):
                for g4 in range(4):
                    gi = half * 4 + g4
                    mm(pY[half][0][:, g4 * 128:(g4 + 1) * 128], wB[:, 0, gi, :], Ug[:, gi, :], True, False, [wBb, ugb], [pY[half][1]])
                    mm(pY[half][0][:, g4 * 128:(g4 + 1) * 128], wB[:, 1, gi, :], Hp[:, gi, :], False, True, [wBb, tBb[1]], [pY[half][1]])
            for half in range(2):
                act_op(tmpB[:, 2, half * 512:(half + 1) * 512], pY[half][0][:], AF.Gelu_apprx_tanh, [pY[half][1]], [tBb[2]])
                release_ps(pY[half][1])
            cv = cat[:, gb, :].rearrange("p (k s) -> p s k", s=8)
            for half in range(2):
                ps, pb = next_ps()
                for s4 in range(4):
                    s = half * 4 + s4
                    for gi in range(8):
                        mm(ps[:, s4 * 128:(s4 + 1) * 128], ew[:, s, 112 - 16 * gi:240 - 16 * gi], Yg[:, gi, :], gi == 0, gi == 7,
                           [cb, tBb[2]], [pb])
                evac(half, cv[:, half * 4:(half + 1) * 4, :], ps[:].rearrange("p (s k) -> p s k", k=128), [pb], [catb[gb][0], catb[gb][1]])
        w = wglu_d[l].rearrange("(k p) n -> p k n", p=128)
        for cc in range(2):
            wc, wcb = wload(w[:, :, cc * 384:(cc + 1) * 384], [128, 6, 384])
            for j3 in range(3):
                j = cc * 3 + j3
                for st in range(NST):
                    ps, pb = next_ps()
                    for k in range(6):
                        mm(ps[:], wc[:, k, j3 * 128:(j3 + 1) * 128], cat[:, k, SL[st]], k == 0, k == 5, [wcb, catb[k][st]], [pb])
                    sg = tmpA[:, st, 0:512]
                    act_op(sg, ps[:], AF.Sigmoid, [pb], [tA[st]])
                    tt(uT[:, j, SL[st]], cat[:, j, SL[st]], sg, ALU.mult, [catb[j][st], tA[st]], [ub[j][st]])
        return [(uT[:, k, :], ub[k]) for k in range(6)] + [(cat[:, 6 + k, :], catb[6 + k]) for k in range(2)]

    def diff_attn(l, tb):
        j = l - 2
        rope_inplace(uT, ub)
        nkeys = (16 * tb + 16) * 128
        kvs = [(kvbuf, [hb[0], hb[1]]), (kvbuf2, [buf("kv2")])]

        def kv_views(hh):
            kb_, bb = kvs[hh % 2]
            return kb_[:, 0, :], kb_[:, 1, :].rearrange("p (b e) -> p b e", e=128), bb

        def load_kv(hh):
            KT, V, bb = kv_views(hh)
            cx.dma(SP, KT[:, 0:nkeys], kT_scr[hh * 128:(hh + 1) * 128, 0:nkeys], reads=[buf("kvscr")], writes=bb)
            cx.dma(SP, V[:, 0:nkeys // 128, :], v_scr[0:nkeys, hh * 128:(hh + 1) * 128].rearrange("(b p) e -> p b e", p=128),
                   reads=[buf("kvscr")], writes=bb)

        steps = []
        for hh in range(6):
            for st in range(NST):
                i = 2 * tb + st
                nkb = 8 * i + 8
                for kb in range(nkb):
                    for c in range(2):
                        steps.append((hh, st, i, nkb, kb, c))
        eaccs = [[tmpA[:, 3, 0:512], tmpA[:, 3, 512:1024]], [tmpA[:, 0, 512:1024], tmpA[:, 1, 512:1024]]]
        eabs = [[buf("eacc00"), buf("eacc01")], [buf("eacc10"), buf("eacc11")]]
        PRE = 3
        sps = {}
        nE = 0
        load_kv(0)

        def col0(i_, kb_):
            r_ = kb_ - 8 * i_
            return (r_ // 2) * 128 if r_ >= 0 else 0

        def prep_s_pair(n0):
            todo, toks = [], []
            for n_ in (n0, n0 + 1):
                hh_, st_, i_, nkb_, kb_, c_ = steps[n_]
                KT, V, bb = kv_views(hh_)
                pss, psb = next_ps()
                toks += cx.deps(bb + [ub[hh_][st_]], [psb])
                todo.append((n_, pss, psb, KT, bb, hh_, st_, col0(i_, kb_), kb_, c_))
            return todo, toks

        def emit_s_pair(todo):
            for n_, pss, psb, KT, bb, hh_, st_, c0, kb_, c_ in todo:
                mm(pss[:, c0:512], KT[c_ * 64:(c_ + 1) * 64, kb_ * 128:(kb_ + 1) * 128],
                   uT[c_ * 64:(c_ + 1) * 64, hh_, st_ * 512 + c0:(st_ + 1) * 512],
                   True, True, bb + [ub[hh_][st_]], [psb])
                sps[n_] = (pss, psb)

        def issue_s_pair(n0):
            todo, toks = prep_s_pair(n0)
            PE.wait(toks)
            emit_s_pair(todo)

        issue_s_pair(0)
        if len(steps) > 2:
            issue_s_pair(2)
        pO = None
        pending_ep = []
        for n_, (hh, st, i, nkb, kb, c) in enumerate(steps):
            blk = (hh * NST + st) % 2
            eacc, eab = eaccs[blk], eabs[blk]
            KT, V, kvb = kv_views(hh)
            if kb == 0 and c == 0:
                pO = [next_ps(hold=True) for _ in range(2)]
                if st == 0 and hh + 1 < 6:
                    load_kv(hh + 1)
            pss, psb = sps.pop(n_)
            c0 = col0(i, kb)
            e = tmpB[:, nE % 6, 0:512]
            eb = tBb[nE % 6]
            nE += 1
            act_op(e[:, c0:512], pss[:, c0:512], AF.Exp, [psb], [eb], scale=0.125)
            if kb >= 8 * i:
                tt(e[:, c0:c0 + 128], e[:, c0:c0 + 128], masks[:, kb - 8 * i, :], ALU.mult, [eb, cb], [eb])
            if c == 0:
                pv0 = (e, eb)
            else:
                e0_, eb0_ = pv0
                nxt_todo, nxt_toks = prep_s_pair(n_ + 3) if n_ + 3 < len(steps) else ([], [])
                PE.wait(cx.deps(kvb + [eb0_], [pO[0][1]]) + cx.deps(kvb + [eb], [pO[1][1]]) + nxt_toks)
                mm(pO[0][0][:, c0:512], V[:, kb, :], e0_[:, c0:512], kb == 0, kb == nkb - 1, kvb + [eb0_], [pO[0][1]])
                mm(pO[1][0][:, c0:512], V[:, kb, :], e[:, c0:512], kb == 0, kb == nkb - 1, kvb + [eb], [pO[1][1]])
                emit_s_pair(nxt_todo)
            if c == 0:
                if kb == 0:
                    cpy(eacc[c], e, [eb], [eab[c]])
                else:
                    tt(eacc[c][:, c0:512], eacc[c][:, c0:512], e[:, c0:512], ALU.add, [eb, eab[c]], [eab[c]])
            else:
                ec_, e_ = eacc[c][:, c0:512], e[:, c0:512]
                if kb == 0:
                    cx.op(POOL, lambda: nc.gpsimd.tensor_copy(out=ec_, in_=e_), reads=[eb], writes=[eab[c]])
                else:
                    cx.op(POOL, lambda: nc.gpsimd.tensor_tensor(out=ec_, in0=ec_, in1=e_, op=ALU.add), reads=[eb, eab[c]], writes=[eab[c]])
            if pending_ep:
                pending_ep.pop(0)()
            if not (kb == nkb - 1 and c == 1):
                continue
            o0, o1, r_ = tmpA[:, 0, 0:512], tmpA[:, 1, 0:512], tmpA[:, 2, 0:512]
            for c2 in range(2):
                act_op((o0, o1)[c2], pO[c2][0][:], AF.Identity, [pO[c2][1]], [buf(f"o{c2}")])
                release_ps(pO[c2][1])

            def stage1(c2, eacc=eacc, eab=eab):
                pl, plb = next_ps()
                mm(pl[:], ones_f, eacc[c2], True, True, [cb, eab[c2]], [plb])
                dve(lambda: nc.vector.reciprocal(out=r_, in_=pl[:]), [plb], [tA[2]])
                tt((o0, o1)[c2], (o0, o1)[c2], r_, ALU.mult, [buf(f"o{c2}"), tA[2]], [buf(f"o{c2}")])

            def stage2():
                stt(o0, o1, lam[:, j, 1:2], o0, ALU.mult, ALU.add, [buf("o0"), buf("o1"), buf("lam")], [buf("o0")])
                act_op(sq[:, 0, :], o0, AF.Square, [buf("o0")], [buf("sq")])

            def stage3():
                ps_, pb_ = next_ps()
                mm(ps_[:], ones_b, sq[:, 0, :], True, True, [buf("sq"), cb], [pb_])
                act_op(r_, ps_[:], AF.Sqrt, [pb_], [tA[2]], scale=1.0 / 128, bias=EPS)

            def stage4(hh=hh, st=st):
                dve(lambda: nc.vector.reciprocal(out=r_, in_=r_), [tA[2]], [tA[2]])
                stt(cat[:, hh, SL[st]], o0, lam[:, j, 2:3], r_, ALU.mult, ALU.mult, [buf("o0"), tA[2], buf("lam")], [catb[hh][st]])

            nop = lambda: None
            pending_ep.extend([nop, lambda f=stage1: f(0), nop, lambda f=stage1: f(1), nop, stage2, nop, stage3, nop, stage4])
        while pending_ep:
            pending_ep.pop(0)()
        return [(cat[:, k, :], catb[k]) for k in range(8)]

    def kv_tile(ta):
        t0 = ta * TT
        norm_x(13)
        w = wkv_d.rearrange("(k p) n -> p k n", p=128)
        n = 0
        for cc in range(3):
            wc, wcb = wload(w[:, :, cc * 256:(cc + 1) * 256], [128, 8, 256])
            for half in range(2):
                kc = 2 * cc + half
                for st in range(NST):
                    ps, pb = next_ps()
                    for k in range(8):
                        mm(ps[:], wc[:, k, half * 128:(half + 1) * 128], hT[:, k, SL[st]], k == 0, k == 7, [wcb, hb[st]], [pb])
                    evac(n, uT[:, kc, SL[st]], ps[:], [pb], [ub[kc][st]])
                    n += 1
        rope_inplace(uT, ub)
        allub = [ub[k][st] for k in range(6) for st in range(NST)]
        kv_toks.append(cx.dma(SP, kT_scr[:, t0:t0 + TT].rearrange("(k p) t -> p k t", p=128), uT, reads=allub, writes=[buf("kvscr")]))
        act_guard.append(kv_toks[-1])
        vt = cat.rearrange("p k t -> p (k t)")[:, 0:6144].rearrange("p (b e) -> p b e", e=768)
        allcat = [catb[k][st] for k in range(8) for st in range(NST)]
        for cc in range(2):
            wc, wcb = wload(w[:, :, 768 + cc * 384:768 + (cc + 1) * 384], [128, 8, 384])
            for tbk in range(8):
                ps, pb = next_ps()
                st = tbk // 4
                for k in range(8):
                    mm(ps[:, 0:384], hT[:, k, tbk * 128:(tbk + 1) * 128], wc[:, k, :], k == 0, k == 7, [wcb, hb[st]], [pb])
                evac(n, vt[:, tbk, cc * 384:(cc + 1) * 384], ps[:, 0:384], [pb], [catb[tbk][cc]])
                n += 1
        kv_toks.append(cx.dma(SP, v_scr[t0:t0 + TT, :].rearrange("(b p) e -> p b e", p=128), vt, reads=allcat, writes=[buf("kvscr")]))
        act_guard.append(kv_toks[-1])

    kv_toks = []
    scr_toks = []

    def select_store(ta):
        xv = xT[:].rearrange("p k (m two j) -> p k m two j", two=2, j=128)
        ov = tmpA[:].rearrange("p a (k2 m j) -> p (a k2) m j", k2=2, j=128)
        ts(ov, xv[:, :, :, 0, :], flags[:, 0:1], None, ALU.mult, None, xb + [cb], tA)
        stt(ov, xv[:, :, :, 1, :], flags[:, 1:2], ov, ALU.mult, ALU.add, xb + [cb] + tA, tA)
        kv_toks.append(cx.dma(SP, x1_scr[:, ta * 512:(ta + 1) * 512].rearrange("(k p) t -> p k t", p=128),
                              tmpA[:].rearrange("p a (k2 t) -> p (a k2) t", k2=2), reads=tA, writes=[buf("x1scr")]))

    memf = tmpA[:].rearrange("p a t -> p (a t)")[:, 0:2048].rearrange("p (k m) -> p k m", m=256)
    memn = tmpB[:, 0:2, :].rearrange("p a t -> p (a t)").rearrange("p (k m) -> p k m", m=256)
    mfb, mnb = buf("memf"), buf("memn")
    cx.dma(SP, memf, memT_d.rearrange("(k p) m -> p k m", p=128), writes=[mfb])
    rmsnorm(memf, mfb, gains[:, 12, :], memn, mnb, 256, 0)
    n = 0
    for l4 in range(4):
        w = wmkv_d[l4].rearrange("(k p) n -> p k n", p=128)
        wk, wkb = wload(w[:, :, 0:256], [128, 8, 256])
        wv, wvb = wload(w[:, :, 256:512], [128, 8, 256])
        for c in range(2):
            ps, pb = next_ps()
            for k in range(8):
                mm(ps[:, 0:256], wk[:, k, c * 128:(c + 1) * 128], memn[:, k, :], k == 0, k == 7, [wkb, mnb], [pb])
            evac(n, mkT[:, l4, c, :], ps[:, 0:256], [pb], [buf("mkv")])
            n += 1
        for mb in range(2):
            ps, pb = next_ps()
            for k in range(8):
                mm(ps[:, 0:256], memn[:, k, mb * 128:(mb + 1) * 128], wv[:, k, :], k == 0, k == 7, [wvb, mnb], [pb])
            evac(n, mvv[:, l4, mb, :], ps[:, 0:256], [pb], [buf("mkv")])
            n += 1
    lq = tmpA[:, 2, 0:512].rearrange("p (j a d) -> p j a d", j=2, a=4)
    lb = buf("lam")
    cx.dma(SP, lq, lqk_d, writes=[tA[2]])
    for j in range(2):
        for a_ in range(2):
            tt(tmpA[:, 3, 0:64], lq[:, j, 2 * a_, :], lq[:, j, 2 * a_ + 1, :], ALU.mult, [tA[2]], [tA[3]])
            dve(lambda: nc.vector.reduce_sum(out=lam[:, j, 2 + a_:3 + a_], in_=tmpA[:, 3, 0:64], axis=mybir.AxisListType.X), [tA[3]], [lb])
            act_op(lam[:, j, 2 + a_:3 + a_], lam[:, j, 2 + a_:3 + a_], AF.Exp, [lb], [lb])
        tt(lam[:, j, 0:1], lam[:, j, 2:3], lam[:, j, 3:4], ALU.subtract, [lb], [lb])
        ts(lam[:, j, 1:2], lam[:, j, 0:1], -1.0, -LAM_INIT[2 + j], ALU.mult, ALU.add, [lb], [lb])
        ts(lam[:, j, 2:3], subln[:, j:j + 1], 1.0 - LAM_INIT[2 + j], None, ALU.mult, None, [lb, cb], [lb])

    xflat = xT[:].rearrange("p k t -> p (k t)")
    pp = xflat[:, 0:1152].rearrange("p (a g) -> p a g", g=NG)
    BRI = xflat[:, 1152:2688].rearrange("p (a g c) -> p a g c", a=2, c=16)
    CRI = xflat[:, 2688:4224].rearrange("p (a g c) -> p a g c", a=2, c=16)
    trtf = xflat[:, 4224:7296]
    trt = trtf.rearrange("p (a g) -> p a g", g=NG)
    DCOL = xflat[:, 7296:7344]
    tmpAf = tmpA[:].rearrange("p a t -> p (a t)")
    BRAW = tmpAf[:, 1024:2560].rearrange("p (a g c) -> p a g c", a=2, c=16)
    actf = act[:, 0:16, :].rearrange("p k t -> p (k t)").bitcast(F32)
    hTf = hT[:].rearrange("p k t -> p (k t)").bitcast(F32)
    sqf = sq[:].rearrange("p k t -> p (k t)").bitcast(F32)
    ppb = buf("pp")
    hsb = buf("hTscr")
    acb = buf("actscr")
    sqb_ = buf("sq")
    for l in range(2):
        R, W = [ppb], [ppb]
        cx.dma(SP, pp[:, 0:3, :], s5a_d[:, l, :, :], writes=W)
        cx.dma(SP, BRAW, s5b_d[:, l], writes=tA)
        cx.dma(SP, CRI, s5c_d[:, l], writes=W)
        cx.dma(SP, DCOL, s5d_d[:, l, :], writes=W)
        LR, LI, LDT, DT, LRDT, TH, ANG, TF, SN, MG, LAMR, LAMI, NR, DEN, FR, FI, T1, T2, TH8 = [pp[:, i, :] for i in range(19)]
        TI = pp[:, 19, :].bitcast(I32)
        act_op(DT, LDT, AF.Exp, R, W)
        tt(LRDT, LR, DT, ALU.mult, R, W)
        tt(TH, LI, DT, ALU.mult, R, W)
        ts(TH8, TH, 8.0, None, ALU.mult, None, R, W)

        def trig(out, n_, psi):
            ts(ANG, TH, float(n_), psi, ALU.mult, ALU.add, R + [cb], W)
            sin_rr(SN, ANG, TF, TI, R, W)
            act_op(MG, LRDT, AF.Exp, R, W, scale=float(n_))
            tt(out, SN, MG, ALU.mult, R, W)

        trig(LAMR, 1, CC["halfpi"])
        trig(LAMI, 1, CC["zero"])
        ts(NR, LAMR, -1.0, None, ALU.add, None, R, W)
        tt(T1, LR, LR, ALU.mult, R, W)
        tt(T2, LI, LI, ALU.mult, R, W)
        tt(DEN, T1, T2, ALU.add, R, W)
        dve(lambda: nc.vector.reciprocal(out=DEN, in_=DEN), R, W)
        tt(T1, NR, LR, ALU.mult, R, W)
        tt(T2, LAMI, LI, ALU.mult, R, W)
        tt(FR, T1, T2, ALU.add, R, W)
        tt(FR, FR, DEN, ALU.mult, R, W)
        tt(T1, LAMI, LR, ALU.mult, R, W)
        tt(T2, NR, LI, ALU.mult, R, W)
        tt(FI, T1, T2, ALU.subtract, R, W)
        tt(FI, FI, DEN, ALU.mult, R, W)
        bc = lambda a_: a_.unsqueeze(2).to_broadcast([128, NG, 16])
        tb_ = tmpAf[:, 0:768].rearrange("p (g c) -> p g c", c=16)
        RA, WA = R + tA, W + tA
        tt(BRI[:, 0], BRAW[:, 0], bc(FR), ALU.mult, RA, WA)
        tt(tb_, BRAW[:, 1], bc(FI), ALU.mult, RA, WA)
        tt(BRI[:, 0], BRI[:, 0], tb_, ALU.subtract, RA, WA)
        tt(BRI[:, 1], BRAW[:, 1], bc(FR), ALU.mult, RA, WA)
        tt(tb_, BRAW[:, 0], bc(FI), ALU.mult, RA, WA)
        tt(BRI[:, 1], BRI[:, 1], tb_, ALU.add, RA, WA)
        NV = tabc[:, 0, :]
        PSV = tabc[:, 1, :]
        b3 = lambda a_: a_.unsqueeze(2).to_broadcast([128, 64, NG])
        g3 = lambda a_: a_.unsqueeze(1).to_broadcast([128, 64, NG])
        ANGA = actf[:, 0:3072]
        TFA = actf[:, 3072:6144]
        MGA = hTf[:, 0:3072]
        TIA = tmpAf[:, 0:3072].bitcast(I32)
        a3 = lambda a_: a_.rearrange("p (a g) -> p a g", g=NG)
        RT, WT = R + [acb, hsb, cb] + tA, W + [acb, hsb] + tA
        tt(a3(ANGA), g3(TH), b3(NV), ALU.mult, RT, WT)
        tt(a3(ANGA), a3(ANGA), b3(PSV), ALU.add, RT, WT)
        sin_rr(trtf, ANGA, TFA, TIA, RT, WT)
        tt(a3(MGA), g3(LRDT), b3(NV), ALU.mult, RT, WT)
        act_op(MGA, MGA, AF.Exp, RT, WT)
        tt(trtf, trtf, MGA, ALU.mult, RT, WT)
        act_op(r8[:, l, :], LRDT, AF.Exp, R, [buf("r8")], scale=8.0)
        dve(lambda: nc.vector.memset(carry[:, l, :], 0.0), [], [buf(f"carry{l}")])
        for gb in range(6):
            g0 = 8 * gb
            gs = slice(g0, g0 + 8)
            mats = [tmpA[:, i, :].rearrange("p (g s c) -> p g s c", s=8, c=16) for i in range(4)]
            scr = sqf[:, 1024:2048].rearrange("p (g s c) -> p g s c", s=8, c=16)
            srcs_ = [(BRI, 0), (BRI, 2), (CRI, 4), (CRI, 6)]
            for mi in range(4):
                dat, st0 = srcs_[mi]
                A0 = dat[:, 0, gs, :].unsqueeze(2).to_broadcast([128, 8, 8, 16])
                A1 = dat[:, 1, gs, :].unsqueeze(2).to_broadcast([128, 8, 8, 16])
                Q0 = trt[:, st0 * 8:(st0 + 1) * 8, gs].rearrange("p s g -> p g s").unsqueeze(3).to_broadcast([128, 8, 8, 16])
                Q1 = trt[:, (st0 + 1) * 8:(st0 + 2) * 8, gs].rearrange("p s g -> p g s").unsqueeze(3).to_broadcast([128, 8, 8, 16])
                tt(mats[mi], A0, Q0, ALU.mult, R, [tA[mi]])
                tt(scr, A1, Q1, ALU.mult, R, [sqb_])
                tt(mats[mi], mats[mi], scr, ALU.add, [tA[mi], sqb_], [tA[mi]])
            W4 = tmpB[:, 2:6, :].rearrange("p a (g m) -> p a g m", m=128)
            w4b = tBb[2:6]
            Zf, Z7f, Xf, X1f = [tmpA[:, i, :].rearrange("p (g m) -> p g m", m=128) for i in range(4)]
            Tm = sqf[:, 0:1024].rearrange("p (g m) -> p g m", m=128)
            for half in range(2):
                ps, pb = next_ps()
                for g4 in range(4):
                    gi = half * 4 + g4
                    mm(ps[:, g4 * 128:(g4 + 1) * 128], Zf[:, gi, :], Xf[:, gi, :], True, True, [tA[0], tA[2]], [pb])
                tt(Tm[:, half * 4:(half + 1) * 4, :], ps[:].rearrange("p (g m) -> p g m", m=128),
                   maskT_f.unsqueeze(1).to_broadcast([128, 4, 128]), ALU.mult, [pb, cb], [sqb_])
                for g4 in range(4):
                    gi = half * 4 + g4
                    stt(W4[:, 2, gi, :], ident_f, DCOL[:, g0 + gi:g0 + gi + 1], Tm[:, gi, :], ALU.mult, ALU.add, [cb, sqb_] + R, w4b)
                for kind, rhs_ in ((0, ident_f), (1, jmat_f)):
                    ps, pb = next_ps()
                    for g4 in range(4):
                        gi = half * 4 + g4
                        mm(ps[:, g4 * 128:(g4 + 1) * 128], Z7f[:, gi, :], rhs_, True, True, [tA[1], cb], [pb])
                    evac(kind, W4[:, kind, half * 4:(half + 1) * 4, :], ps[:].rearrange("p (g m) -> p g m", m=128), [pb], w4b)
            cpy(W4[:, 3], X1f, [tA[3]], w4b)
            scr_toks.append(cx.dma(SP, s5w_scr[l, :, :, g0:g0 + 8, :].rearrange("k p g m -> p k g m"), W4, reads=w4b, writes=[buf("s5scr")]))
            ang2 = tmpAf[:, 0:2048].rearrange("p (w g k) -> p w g k", w=2, k=128)
            tt(ang2[:, 1], jv_f.unsqueeze(1).to_broadcast([128, 8, 128]), TH8[:, gs].unsqueeze(2).to_broadcast([128, 8, 128]),
               ALU.mult, R + [cb] + tA, tA)
            ts(ang2[:, 0], ang2[:, 1], PI / 2, None, ALU.add, None, tA, tA)
            sin_rr(tmpAf[:, 2048:4096], tmpAf[:, 0:2048], hTf[:, 0:2048], hTf[:, 2048:4096].bitcast(I32), tA + [hsb], tA + [hsb])
            scr_toks.append(cx.dma(SP, rot_scr[l, :, :, g0:g0 + 8, :].rearrange("w p g k -> p w g k"),
                                   tmpAf[:, 2048:4096].rearrange("p (w g k) -> p w g k", w=2, k=128), reads=tA, writes=[buf("s5scr")]))
    POOL.wait(scr_toks)
    bar = [Tok(e.sem, e.count, e.key) for e in (PE, ACT, DVE) if e.count > 0]
    SP.wait(bar)

    out_toks = []
    xsrc = xT_d.rearrange("(k p) t -> p k t", p=128)
    for ta in range(na_tiles):
        for st in range(NST):
            cx.dma(SP, xT[:, :, SL[st]], xsrc[:, :, ta * TT + st * 512: ta * TT + (st + 1) * 512], writes=[xb[st]])
        rope_tables(pos_all_d[:, ta * TT:(ta + 1) * TT])
        for l in range(2):
            ffn(l, 0)
            if (f"ffn1_{l}", ta) in dbgs:
                out_toks.append(cx.dma(SP, dbgs[(f"ffn1_{l}", ta)], xT[:].rearrange("p k t -> p (k t)"), reads=xb))
            in_proj(l)
            mem_attn(l)
            srcs = s5_mixer(l)
            if (f"tokout_{l}", ta) in dbgs:
                out_toks.append(cx.dma(SP, dbgs[(f"tokout_{l}", ta)], act[:, 0:16, :].rearrange("p k t -> p (k t)"),
                                       reads=[b_ for s_ in srcs for b_ in s_[1]]))
            out_proj(l, srcs)
            ffn(l, 1)
            if (f"x_{l}", ta) in dbgs:
                out_toks.append(cx.dma(SP, dbgs[(f"x_{l}", ta)], xT[:].rearrange("p k t -> p (k t)"), reads=xb))
        kv_tile(ta)
        select_store(ta)

    SP.wait(kv_toks)
    x1src = x1_scr.rearrange("(k p) t -> p k t", p=128)
    for tb in range(nb_tiles):
        for st in range(NST):
            cx.dma(SP, xT[:, :, SL[st]], x1src[:, :, tb * TT + st * 512: tb * TT + (st + 1) * 512],
                   reads=[buf("x1scr")], writes=[xb[st]])
        rope_tables(pos_mine_d[:, tb * TT:(tb + 1) * TT])
        for l in range(2, 4):
            ffn(l, 0)
            in_proj(l)
            mem_attn(l)
            srcs = diff_attn(l, tb)
            if (f"tokout_{l}", tb) in dbgs:
                out_toks.append(cx.dma(SP, dbgs[(f"tokout_{l}", tb)], act[:, 0:16, :].rearrange("p k t -> p (k t)"),
                                       reads=[b_ for s_ in srcs for b_ in s_[1]]))
            out_proj(l, srcs)
            ffn(l, 1)
        for st in range(NST):
            of = tmpA[:].rearrange("p a (k2 t) -> p (a k2) t", k2=2)
            rmsnorm(xT[:, :, SL[st]], xb[st], gains[:, 14, :], of, list(tA), 512, st)
            out_toks.append(cx.dma(SP, outT_d[:, tb * TT + st * 512: tb * TT + (st + 1) * 512].rearrange("(k p) t -> p k t", p=128),
                                   of, reads=list(tA)))

    SP.wait(out_toks)


def _consts():
    cst = np.zeros((128, 16), np.float32)
    hp = PI / 2
    lo, hi = slice(0, 64), slice(64, 128)
    cst[lo, 0], cst[hi, 0] = 0 + hp, -hp + hp
    cst[lo, 1], cst[hi, 1] = hp + hp, 0 + hp
    cst[lo, 2], cst[hi, 2] = 0 + hp, hp + hp
    cst[lo, 3], cst[hi, 3] = hp + hp, PI + hp
    inv = (500000.0 ** (-np.arange(0, 16, 2, dtype=np.float32) / 16)).astype(np.float32)
    for base in (0, 64):
        cst[base:base + 8, 4] = inv
        cst[base + 8:base + 16, 4] = inv
        cst[base:base + 8, 5] = -inv
        cst[base + 8:base + 16, 5] = inv
    cst[:, 6] = hp
    ew = np.zeros((128, 8, 240), np.float32)
    for gi in range(8):
        for c in range(16):
            ew[16 * gi + c, gi, 112 + c] = 1.0
    cm = np.zeros((128, 5, 128), np.float32)
    cm[:, 0, :] = np.eye(128)
    for p in range(64):
        cm[64 + p, 1, p] = 1.0
        cm[p, 1, 64 + p] = -1.0
    sidx = np.arange(128) // 16
    cm[:, 2, :] = (sidx[None, :] >= sidx[:, None]).astype(np.float32)
    cm[:, 3, :] = (np.arange(128, dtype=np.float32) + 1.0)[None, :]
    cm[:, 4, :] = 1.0
    cmb = np.zeros((128, 2, 128), np.float32)
    cmb[:, 0, :] = 1.0
    for base in (0, 64):
        for j in range(8):
            cmb[base + 8 + j, 1, base + j] = 1.0
            cmb[base + j, 1, base + 8 + j] = 1.0
    tabc = np.zeros((128, 2, 64), np.float32)
    sets = [(lambda s_: -s_, 0), (lambda s_: -s_, 1), (lambda s_: 7 - s_, 0), (lambda s_: 7 - s_, 1),
            (lambda s_: s_, 2), (lambda s_: s_, 3), (lambda s_: s_ + 1, 2), (lambda s_: s_ + 1, 3)]
    for k_, (fn, pc) in enumerate(sets):
        for s_ in range(8):
            tabc[:, 0, k_ * 8 + s_] = fn(s_)
            tabc[:, 1, k_ * 8 + s_] = cst[:, pc]
    return cst, ew.astype(ml_dtypes.bfloat16), cm, cmb.astype(ml_dtypes.bfloat16), tabc


def _masks(h):
    m = np.zeros((128, 8, 128), np.float32)
    s = np.arange(128)[:, None]
    for r in range(8):
        jj = r // 2
        key = r * 128 + s
        qry = (2 * jj + h) * 128 + np.arange(128)[None, :]
        m[:, r, :] = (key <= qry)
    return m.astype(ml_dtypes.bfloat16)


def make_in_maps(inputs):
    f = lambda a: np.ascontiguousarray(np.asarray(a))
    cst, ew, cm, cmb, tabc = _consts()
    gains = np.stack([*inputs["ln_ffn1"], *inputs["ln_mix"], *inputs["ln_ffn2"],
                      inputs["ln_mem"], inputs["ln_kv"], inputs["ln_final"]], 0)
    gains = f(gains.reshape(15, 8, 128).transpose(2, 0, 1))
    rep = lambda a: np.concatenate([a, a], 0)
    s5a = np.stack([np.stack([inputs["s5_a_re"][l].T, inputs["s5_a_im"][l].T,
                              np.broadcast_to(inputs["s5_log_dt"][l][None, :], (64, NG))], 0) for l in range(2)], 0)
    s5a = f(rep(s5a.transpose(2, 0, 1, 3)))
    s5b = np.stack([np.stack([inputs["s5_b_re"][l], inputs["s5_b_im"][l]], 0) for l in range(2)], 0)
    s5b = f(rep(s5b.transpose(3, 0, 1, 2, 4)))
    s5c = np.stack([np.stack([inputs["s5_c_re"][l], inputs["s5_c_im"][l]], 0) for l in range(2)], 0)
    s5c = f(rep(s5c.transpose(4, 0, 1, 2, 3)))
    s5d = f(np.tile(inputs["s5_d"].transpose(2, 0, 1), (8, 1, 1)))
    lqk = np.stack([np.stack([inputs["diff_lq1"][j], inputs["diff_lk1"][j], inputs["diff_lq2"][j],
                              inputs["diff_lk2"][j]], 0) for j in range(2)], 0)
    lqk = f(np.broadcast_to(lqk[None], (128, 2, 4, 64)))
    subln = f(inputs["diff_subln"].T)
    shared = dict(gains=gains, s5a=s5a, s5b=s5b, s5c=s5c, s5d=s5d, lqk=lqk, subln=subln,
                  cst=cst, ew=ew, cm=cm, cmb=cmb, tabc=tabc)
    for k in ["ffn1_in", "ffn2_in", "ffn1_out", "ffn2_out", "w_mix_in", "w_mix_out", "w_mem_kv",
              "s5_w_glu", "w_kv_shared"]:
        shared[k] = f(inputs[k])
    in_maps = []
    for c in range(8):
        b, h = c // 2, c % 2
        m = dict(shared)
        m["xT"] = f(inputs["x"][b].T)
        m["memT"] = f(inputs["mem"][b].T)
        pos = np.asarray(inputs["positions"][b]).astype(np.int32)
        m["pos_all"] = f(np.broadcast_to(pos[None, :], (128, S)))
        mine = pos.reshape(16, 2, 128)[:, h, :].reshape(-1)
        m["pos_mine"] = f(np.broadcast_to(mine[None, :], (128, S // 2)))
        m["masks"] = _masks(h)
        fl = np.zeros((128, 2), np.float32)
        fl[:, 0] = 1 - h
        fl[:, 1] = h
        m["flags"] = fl
        in_maps.append(m)
    return in_maps


def kernel(**inputs):
    in_maps = make_in_maps(inputs)
    nc = build_program()
    res = run_bass_kernel_spmd(nc, in_maps, core_ids=list(range(8)))
    out = np.zeros((NB, S, D), np.float32)
    for c in range(8):
        b, h = c // 2, c % 2
        o = np.asarray(res.results[c]["outT"]).T.reshape(16, 128, D)
        out[b].reshape(16, 2, 128, D)[:, h] = o
    return out
```

```python
import math
from contextlib import ExitStack

import numpy as np
import ml_dtypes

import concourse.bass as bass
import concourse.mybir as mybir
from concourse.bass_utils import run_bass_kernel_spmd

F32 = mybir.dt.float32
BF16 = mybir.dt.bfloat16
I32 = mybir.dt.int32
ALU = mybir.AluOpType
AF = mybir.ActivationFunctionType

D = 1024
S = 4096
NB = 4
DFF = 2816
NFC = DFF // 128
TT = 1024
NST = TT // 512
KCH = TT // 8
NG = 48
EPS = 1e-6
PI = math.pi
C1 = 6.28125
C2 = 2.0 * PI - C1
SLOT = 3072
NSLOT = 6
LAM_INIT = [0.8 - 0.6 * math.exp(-0.3 * i) for i in range(4)]


class Tok:
    __slots__ = ("sem", "val", "key")

    def __init__(self, sem, val, key):
        self.sem, self.val, self.key = sem, val, key


class Buf:
    __slots__ = ("w", "r", "name")

    def __init__(self, name=""):
        self.w = None
        self.r = {}
        self.name = name


class Eng:
    def __init__(self, h, sem, key):
        self.h, self.sem, self.key = h, sem, key
        self.count = 0
        self.waited = {}
        self.pending = []

    def wait(self, toks):
        for t in toks:
            if t is None:
                continue
            if t.key == self.key and self.key == "pe":
                continue
            assert t.val is not None, "waiting on unresolved token"
            if self.waited.get(t.key, 0) < t.val:
                self.h.wait_ge(t.sem, t.val)
                self.waited[t.key] = t.val


class Ctx:
    def __init__(self, nc, es):
        self.nc = nc
        self.es = es
        mk = lambda n: es.enter_context(nc.semaphore(n))
        self.pe = Eng(nc.tensor, mk("s_pe"), "pe")
        self.act = Eng(nc.scalar, mk("s_act"), "act")
        self.dve = Eng(nc.vector, mk("s_dve"), "dve")
        self.pool = Eng(nc.gpsimd, mk("s_pool"), "pool")
        self.sp = Eng(nc.sync, mk("s_sp"), "sp")
        self.dsems = [(mk(f"s_d{i}"), f"d{i}") for i in range(24)]
        self.dcnt = [0] * len(self.dsems)
        self.dlast = [None] * len(self.dsems)
        self.dnext = 0

    def deps(self, reads, writes):
        toks = []
        for b in reads:
            if b.w is not None:
                toks.append(b.w)
        for b in writes:
            if b.w is not None:
                toks.append(b.w)
            toks.extend(b.r.values())
        return toks

    def op(self, eng, fn, reads=(), writes=(), signal=True):
        eng.wait(self.deps(reads, writes))
        ins = fn()
        if signal:
            eng.count += 1
            ins.then_inc(eng.sem, 1)
            tok = Tok(eng.sem, eng.count, eng.key)
            for p in eng.pending:
                p.val = eng.count
            eng.pending = []
        else:
            tok = Tok(eng.sem, None, eng.key)
            eng.pending.append(tok)
        for b in reads:
            b.r[eng.key] = tok
        for b in writes:
            b.w = tok
            b.r = {}
        return tok

    def dma(self, q, out, in_, reads=(), writes=(), semidx=None):
        if semidx is None:
            semidx = self.dnext
            self.dnext = (self.dnext + 1) % 16
        sem, key = self.dsems[semidx]
        q.wait([self.dlast[semidx]])
        q.wait(self.deps(reads, writes))
        self.dcnt[semidx] += 1
        q.h.dma_start(out=out, in_=in_).then_inc(sem, 16)
        tok = Tok(sem, 16 * self.dcnt[semidx], key)
        self.dlast[semidx] = tok
        for b in reads:
            b.r[key] = tok
        for b in writes:
            b.w = tok
            b.r = {}
        return tok


def build_program(na_tiles=4, nb_tiles=2, dbg=None):
    nc = bass.Bass("TRN2", target_bir_lowering=False)
    es = ExitStack()
    with es:
        _build(nc, es, na_tiles, nb_tiles, dbg)
    return nc


def _build(nc, es, na_tiles, nb_tiles, dbg):
    cx = Ctx(nc, es)
    PE, ACT, DVE, POOL, SP = cx.pe, cx.act, cx.dve, cx.pool, cx.sp

    def din(name, shape, dt=F32):
        return nc.dram_tensor(name, list(shape), dt, kind="ExternalInput").ap()

    def dscr(name, shape, dt):
        return nc.dram_tensor(name, list(shape), dt, kind="Internal").ap()

    def sb(name, shape, dt):
        return es.enter_context(nc.sbuf_tensor("sb_" + name, list(shape), dt))

    xT_d = din("xT", [D, S])
    memT_d = din("memT", [D, 256])
    pos_all_d = din("pos_all", [128, S], I32)
    pos_mine_d = din("pos_mine", [128, S // 2], I32)
    masks_d = din("masks", [128, 8, 128], BF16)
    flags_d = din("flags", [128, 2])
    gains_d = din("gains", [128, 15, 8])
    ffn_in_d = [din("ffn1_in", [4, D, 2 * DFF]), din("ffn2_in", [4, D, 2 * DFF])]
    ffn_out_d = [din("ffn1_out", [4, DFF, D]), din("ffn2_out", [4, DFF, D])]
    wmi_d = din("w_mix_in", [4, D, D])
    wmo_d = din("w_mix_out", [4, D, D])
    wmkv_d = din("w_mem_kv", [4, D, 512])
    wglu_d = din("s5_w_glu", [2, 768, 768])
    wkv_d = din("w_kv_shared", [D, 1536])
    s5a_d = din("s5a", [128, 2, 3, NG])
    s5b_d = din("s5b", [128, 2, 2, NG, 16])
    s5c_d = din("s5c", [128, 2, 2, NG, 16])
    s5d_d = din("s5d", [128, 2, NG])
    lqk_d = din("lqk", [128, 2, 4, 64])
    subln_d = din("subln", [128, 2])
    cst_d = din("cst", [128, 16])
    ew_d = din("ew", [128, 8, 240], BF16)
    cm_d = din("cm", [128, 5, 128])
    cmb_d = din("cmb", [128, 2, 128], BF16)
    tabc_d = din("tabc", [128, 2, 64])
    outT_d = nc.dram_tensor("outT", [D, S // 2], F32, kind="ExternalOutput").ap()
    dbgs = {}
    for d_ in (dbg or []):
        dbgs[(d_[0], d_[1])] = nc.dram_tensor(f"dbg_{d_[0]}_{d_[1]}", [128, d_[2]], BF16 if d_[0].startswith("tokout") else F32, kind="ExternalOutput").ap()

    kT_scr = dscr("kT_scr", [768, S], BF16)
    v_scr = dscr("v_scr", [S, 768], BF16)
    x1_scr = dscr("x1_scr", [D, S // 2], F32)
    s5w_scr = dscr("s5w_scr", [2, 4, 128, NG, 128], BF16)
    rot_scr = dscr("rot_scr", [2, 2, 128, NG, KCH], F32)

    xT = sb("xT", [128, 8, TT], F32)
    hT = sb("hT", [128, 8, TT], BF16)
    act = sb("act", [128, NFC, TT], BF16)
    sq = sb("sq", [128, 8, 512], BF16)
    sq2 = sb("sq2", [128, 8, 512], BF16)
    rstd = sb("rstd", [128, 2, 512], F32)
    uT = act[:, 0:6, :]
    mqT = act[:, 6:8, :]
    cat = act[:, 8:16, :]
    tmpB = act[:, 16:22, :]
    ring = sb("ring", [128, NSLOT, SLOT], BF16)
    ew = sb("ew", [128, 8, 240], BF16)
    cm = sb("cm", [128, 5, 128], F32)
    cmb = sb("cmb", [128, 2, 128], BF16)
    cst = sb("cst", [128, 16], F32)
    tabc = sb("tabc", [128, 2, 64], F32)
    gains = sb("gains", [128, 15, 8], F32)
    flags = sb("flags", [128, 2], F32)
    masks = sb("masks", [128, 8, 128], BF16)
    ropeT = sb("ropeT", [128, 2, TT], F32)
    mkT = sb("mkT", [128, 4, 2, 256], BF16)
    mvv = sb("mvv", [128, 4, 2, 256], BF16)
    carry = sb("carry", [128, 2, NG], F32)
    r8 = sb("r8", [128, 2, NG], F32)
    lam = sb("lam", [128, 2, 4], F32)
    subln = sb("subln", [128, 2], F32)
    tmpA = sb("tmpA", [128, 4, 1024], F32)
    kvbuf = hT[:].rearrange("p k t -> p (k t)").rearrange("p (a n) -> p a n", a=2)
    kvbuf2 = sb("kvbuf2", [128, 2, 4096], BF16)

    ones_b = cmb[:, 0, :]
    perm_b = cmb[:, 1, :]
    ident_f = cm[:, 0, :]
    jmat_f = cm[:, 1, :]
    maskT_f = cm[:, 2, :]
    jv_f = cm[:, 3, :]
    ones_f = cm[:, 4, :]

    psum = [es.enter_context(nc.psum_tensor(f"ps{i}", [128, 512], F32)) for i in range(8)]
    ps_b = [Buf(f"ps{i}") for i in range(8)]
    ps_state = {"n": 0, "held": set()}

    def next_ps(hold=False):
        for _ in range(17):
            i = ps_state["n"]
            ps_state["n"] = (i + 1) % 8
            if i not in ps_state["held"] and not (ps_b[i].w is not None and not ps_b[i].r):
                break
        else:
            raise RuntimeError("no free PSUM bank")
        if hold:
            ps_state["held"].add(i)
        return psum[i], ps_b[i]

    def release_ps(pb):
        ps_state["held"].discard(ps_b.index(pb))

    B = {}

    def buf(name):
        if name not in B:
            B[name] = Buf(name)
        return B[name]

    slot_b = [Buf(f"slot{i}") for i in range(NSLOT)]
    wstate = {"n": 0}

    def wload(src_ap, shape, f32=False):
        n = wstate["n"]
        wstate["n"] = n + 1
        i = n % NSLOT
        nel = 1
        for s_ in shape[1:]:
            nel *= s_
        if f32:
            assert 2 * nel <= SLOT
            dst = ring[:, i, 0:2 * nel].bitcast(F32)
        else:
            assert nel <= SLOT, (shape, nel)
            dst = ring[:, i, 0:nel]
        if len(shape) == 3:
            dst = dst.rearrange("p (a b) -> p a b", b=shape[2])
        elif len(shape) == 4:
            dst = dst.rearrange("p (a b c) -> p a b c", b=shape[2], c=shape[3])
        cx.dma(POOL, dst, src_ap, writes=[slot_b[i]], semidx=16 + i)
        return dst, slot_b[i]

    def mm(out, lhsT, rhs, start, stop, reads, writes):
        return cx.op(PE, lambda: nc.tensor.matmul(out, lhsT, rhs, start=start, stop=stop),
                     reads=reads, writes=writes, signal=stop)

    def act_op(out, in_, func, reads, writes, **kw):
        return cx.op(ACT, lambda: nc.scalar.activation(out=out, in_=in_, func=func, **kw),
                     reads=reads, writes=writes)

    def dve(fn, reads, writes):
        return cx.op(DVE, fn, reads=reads, writes=writes)

    def tt(out, a, b, op, reads, writes):
        return dve(lambda: nc.vector.tensor_tensor(out=out, in0=a, in1=b, op=op), reads, writes)

    def ts(out, a, s1, s2, op0, op1, reads, writes):
        if s2 is None:
            return dve(lambda: nc.vector.tensor_scalar(out=out, in0=a, scalar1=s1, scalar2=None, op0=op0), reads, writes)
        return dve(lambda: nc.vector.tensor_scalar(out=out, in0=a, scalar1=s1, scalar2=s2, op0=op0, op1=op1), reads, writes)

    def stt(out, a, sc, b, op0, op1, reads, writes):
        return dve(lambda: nc.vector.scalar_tensor_tensor(out=out, in0=a, scalar=sc, in1=b, op0=op0, op1=op1), reads, writes)

    def cpy(out, a, reads, writes):
        return dve(lambda: nc.vector.tensor_copy(out=out, in_=a), reads, writes)

    def evac(i, out, a, reads, writes):
        if i % 2 == 0:
            return act_op(out, a, AF.Identity, reads, writes)
        return cpy(out, a, reads, writes)

    cb = buf("consts")
    for dst, src in [(ew, ew_d), (cm, cm_d), (cmb, cmb_d), (cst, cst_d), (gains, gains_d),
                     (flags, flags_d), (masks, masks_d), (subln, subln_d), (tabc, tabc_d)]:
        cx.dma(SP, dst[:], src, writes=[cb])
    CC = {n: cst[:, i:i + 1] for i, n in enumerate(
        ["psiQ1", "psiQ2", "psiP1", "psiP2", "invc", "invs", "halfpi", "zero"])}
    tA = [buf(f"tmpA{i}") for i in range(4)]
    tBb = [buf(f"tmpB{i}") for i in range(6)]

    def sin_rr(out, ang, tmpf, tmpi, rb, wb):
        ts(tmpi, ang, 1.0 / (2 * PI), None, ALU.mult, None, rb, wb)
        cpy(tmpf, tmpi, rb, wb)
        stt(ang, tmpf, -C1, ang, ALU.mult, ALU.add, rb, wb)
        stt(ang, tmpf, -C2, ang, ALU.mult, ALU.add, rb, wb)
        ts(ang, ang, PI, -PI, ALU.min, ALU.max, rb, wb)
        act_op(out, ang, AF.Sin, rb, wb)

    def rmsnorm(src, src_b, gcol, dst, dst_b, n, par=0):
        sqb = buf("sq")
        act_op(sq[:, :, 0:n], src, AF.Square, [src_b], [sqb])
        ps, pb = next_ps()
        for k in range(8):
            mm(ps[:, 0:n], ones_b, sq[:, k, 0:n], k == 0, k == 7, [sqb, cb], [pb])
        rb = buf(f"rstd{par}")
        r = rstd[:, par, 0:n]
        act_op(r, ps[:, 0:n], AF.Sqrt, [pb], [rb], scale=1.0 / D, bias=EPS)
        dve(lambda: nc.vector.reciprocal(out=r, in_=r), [rb], [rb])
        for k in range(8):
            stt(dst[:, k, :], src[:, k, :], gcol[:, k:k + 1], r, ALU.mult, ALU.mult, [src_b, rb, cb],
                dst_b if isinstance(dst_b, list) else [dst_b])

    xb = [buf(f"x{st}") for st in range(NST)]
    hb = [buf(f"h{st}") for st in range(NST)]
    SL = [slice(st * 512, (st + 1) * 512) for st in range(NST)]

    def norm_x(gidx):
        gcol = gains[:, gidx, :]
        sqbs = [buf("sq"), buf("sq2")]
        sqs = [sq, sq2]
        pss = []
        for st in range(NST):
            act_op(sqs[st][:, :, :], xT[:, :, SL[st]], AF.Square, [xb[st]], [sqbs[st]])
        for st in range(NST):
            ps, pb = next_ps()
            for k in range(8):
                mm(ps[:], ones_b, sqs[st][:, k, :], k == 0, k == 7, [sqbs[st], cb], [pb])
            pss.append((ps, pb))
        for st in range(NST):
            act_op(rstd[:, st, :], pss[st][0][:], AF.Sqrt, [pss[st][1]], [buf(f"rstd{st}")], scale=1.0 / D, bias=EPS)
        for st in range(NST):
            r = rstd[:, st, :]
            dve(lambda: nc.vector.reciprocal(out=r, in_=r), [buf(f"rstd{st}")], [buf(f"rstd{st}")])
        for st in range(NST):
            for k in range(8):
                stt(hT[:, k, SL[st]], xT[:, k, SL[st]], gcol[:, k:k + 1], rstd[:, st, :], ALU.mult, ALU.mult,
                    [xb[st], buf(f"rstd{st}"), cb], [hb[st]])

    act_guard = []

    def ffn(layer, which):
        norm_x((0 if which == 0 else 8) + layer)
        w_in = ffn_in_d[which][layer].rearrange("(k p) n -> p k n", p=128)
        w_out = ffn_out_d[which][layer].rearrange("(f p) n -> p f n", p=128)
        actb = [[buf(f"act{j}_{st}") for st in range(NST)] for j in range(NFC)]
        if act_guard:
            DVE.wait(act_guard)
            del act_guard[:]
        for j in range(NFC):
            wg, wgb = wload(w_in[:, :, j * 128:(j + 1) * 128], [128, 8, 128])
            wu, wub = wload(w_in[:, :, DFF + j * 128: DFF + (j + 1) * 128], [128, 8, 128])
            for st in range(NST):
                sl = SL[st]
                pg, pgb = next_ps()
                for k in range(8):
                    mm(pg[:], wg[:, k, :], hT[:, k, sl], k == 0, k == 7, [wgb, hb[st]], [pgb])
                pu, pub = next_ps()
                for k in range(8):
                    mm(pu[:], wu[:, k, :], hT[:, k, sl], k == 0, k == 7, [wub, hb[st]], [pub])
                sg = tmpA[:, st, 0:512]
                act_op(sg, pg[:], AF.Silu, [pgb], [tA[st]])
                tt(act[:, j, sl], sg, pu[:], ALU.mult, [tA[st], pub], [actb[j][st]])
        for dch in range(8):
            wo, wob = wload(w_out[:, :, dch * 128:(dch + 1) * 128], [128, NFC, 128])
            for st in range(NST):
                sl = SL[st]
                po, pob = next_ps()
                for j in range(NFC):
                    mm(po[:], wo[:, j, :], act[:, j, sl], j == 0, j == NFC - 1, [wob, actb[j][st]], [pob])
                stt(xT[:, dch, sl], po[:], 0.5, xT[:, dch, sl], ALU.mult, ALU.add, [pob, xb[st]], [xb[st]])

    ub = [[buf(f"u{k}_{st}") for st in range(NST)] for k in range(6)]
    mqb = [[buf(f"mq{k}_{st}") for st in range(NST)] for k in range(2)]
    catb = [[buf(f"cat{k}_{st}") for st in range(NST)] for k in range(8)]

    def in_proj(layer):
        norm_x(4 + layer)
        w = wmi_d[layer].rearrange("(k p) n -> p k n", p=128)
        n = 0
        for cc in range(4):
            wc, wcb = wload(w[:, :, cc * 256:(cc + 1) * 256], [128, 8, 256])
            for half in range(2):
                oc = 2 * cc + half
                for st in range(NST):
                    ps, pb = next_ps()
                    for k in range(8):
                        mm(ps[:], wc[:, k, half * 128:(half + 1) * 128], hT[:, k, SL[st]], k == 0, k == 7, [wcb, hb[st]], [pb])
                    if oc < 6:
                        evac(n, uT[:, oc, SL[st]], ps[:], [pb], [ub[oc][st]])
                    else:
                        evac(n, mqT[:, oc - 6, SL[st]], ps[:], [pb], [mqb[oc - 6][st]])
                    n += 1

    def mem_attn(layer):
        steps = [(st, c, hd2, mb) for st in range(NST) for c in range(2) for hd2 in range(2) for mb in range(2)]
        PRE = 3
        sps = {}

        def issue_s(n_):
            st_, c_, hd2_, mb_ = steps[n_]
            base = hd2_ * 64
            pss, psb = next_ps()
            mm(pss[:], mkT[base:base + 64, layer, c_, mb_ * 128:(mb_ + 1) * 128], mqT[base:base + 64, c_, SL[st_]],
               True, True, [buf("mkv"), mqb[c_][st_]], [psb])
            sps[n_] = (pss, psb)

        for n_ in range(PRE):
            issue_s(n_)
        po = pl = None
        for n_, (st, c, hd2, mb) in enumerate(steps):
            if hd2 == 0 and mb == 0:
                po, pob = next_ps(hold=True)
                pl, plb = next_ps(hold=True)
            if n_ + PRE < len(steps):
                issue_s(n_ + PRE)
            base = hd2 * 64
            hd = 2 * c + hd2
            pss, psb = sps.pop(n_)
            e = tmpB[:, n_ % 6, 0:512]
            eb = tBb[n_ % 6]
            act_op(e, pss[:], AF.Exp, [psb], [eb], scale=0.125)
            mm(po[base:base + 64, :], mvv[:, layer, mb, hd * 64:(hd + 1) * 64], e, mb == 0, mb == 1, [buf("mkv"), eb], [pob])
            mm(pl[base:base + 64, :], ones_b[:, 0:64], e, mb == 0, mb == 1, [cb, eb], [plb])
            if hd2 == 1 and mb == 1:
                rl = tmpA[:, 2, 0:512]
                dve(lambda: nc.vector.reciprocal(out=rl, in_=pl[:]), [plb], [tA[2]])
                tt(cat[:, 6 + c, SL[st]], po[:], rl, ALU.mult, [pob, tA[2]], [catb[6 + c][st]])
                release_ps(pob)
                release_ps(plb)

    def out_proj(layer, srcs):
        w = wmo_d[layer].rearrange("(k p) n -> p k n", p=128)
        for cc in range(4):
            wc, wcb = wload(w[:, :, cc * 256:(cc + 1) * 256], [128, 8, 256])
            for half in range(2):
                dch = 2 * cc + half
                for st in range(NST):
                    ps, pb = next_ps()
                    for k in range(8):
                        mm(ps[:], wc[:, k, half * 128:(half + 1) * 128], srcs[k][0][:, SL[st]], k == 0, k == 7,
                           [wcb, srcs[k][1][st]], [pb])
                    tt(xT[:, dch, SL[st]], ps[:], xT[:, dch, SL[st]], ALU.add, [pb, xb[st]], [xb[st]])

    def rope_tables(pos_src):
        rb_ = buf("rope")
        allA = tA
        posi = tmpA[:, 0, :].bitcast(I32)
        cx.dma(SP, posi, pos_src, writes=[tA[0]])
        cpy(tmpA[:, 1, :], posi, [tA[0]], [tA[1]])
        for which, icol, pcol in ((0, "invc", "halfpi"), (1, "invs", "zero")):
            ts(tmpA[:, 2, :], tmpA[:, 1, :], CC[icol], CC[pcol], ALU.mult, ALU.add, [tA[1], cb], [tA[2]])
            sin_rr(ropeT[:, which, :], tmpA[:, 2, :], tmpA[:, 3, :], tmpA[:, 0, :].bitcast(I32), [tA[2], tA[3], tA[0]], [tA[2], tA[3], tA[0], rb_])

    def rope_inplace(T6, bufs6):
        rb_ = buf("rope")
        n = 0
        for ch in range(6):
            for st in range(NST):
                ps, pb = next_ps()
                mm(ps[:], perm_b, T6[:, ch, SL[st]], True, True, [cb, bufs6[ch][st]], [pb])
                i0, i1 = (n % 2) * 2, (n % 2) * 2 + 1
                n += 1
                tt(tmpA[:, i0, 0:512], ps[:], ropeT[:, 1, SL[st]], ALU.mult, [pb, rb_], [tA[i0]])
                tt(tmpA[:, i1, 0:512], T6[:, ch, SL[st]], ropeT[:, 0, SL[st]], ALU.mult, [bufs6[ch][st], rb_], [tA[i1]])
                tt(T6[:, ch, SL[st]], tmpA[:, i0, 0:512], tmpA[:, i1, 0:512], ALU.add, [tA[i0], tA[i1]], [bufs6[ch][st]])

    def s5_mixer(l):
        carb = buf(f"carry{l}")

        def stage_a(gb):
            g0 = 8 * gb
            par = gb % 2
            wP, wPb = wload(s5w_scr[l, 0:2, :, g0:g0 + 8, :].rearrange("k p g m -> p k g m"), [128, 2, 8, 128])
            Ug = tmpB[:, 3 * par, :].rearrange("p (g k) -> p g k", k=128)
            ugb = tBb[3 * par]
            uv = uT[:, gb, :].rearrange("p (k s) -> p s k", s=8)
            for half in range(2):
                ps, pb = next_ps()
                for g4 in range(4):
                    gi = half * 4 + g4
                    for s in range(8):
                        mm(ps[:, g4 * 128:(g4 + 1) * 128], ew[:, gi, 112 - 16 * s:240 - 16 * s], uv[:, s, :], s == 0, s == 7,
                           [cb, ub[gb][0], ub[gb][1]], [pb])
                evac(half, tmpB[:, 3 * par, half * 512:(half + 1) * 512], ps[:], [pb], [ugb])
            pS = [next_ps(hold=True) for _ in range(2)]
            pSp = [next_ps(hold=True) for _ in range(2)]
            for half in range(2):
                for g4 in range(4):
                    gi = half * 4 + g4
                    mm(pS[half][0][:, g4 * 128:(g4 + 1) * 128], wP[:, 0, gi, :], Ug[:, gi, :], True, True, [wPb, ugb], [pS[half][1]])
                    mm(pSp[half][0][:, g4 * 128:(g4 + 1) * 128], wP[:, 1, gi, :], Ug[:, gi, :], True, True, [wPb, ugb], [pSp[half][1]])
            return Ug, ugb, pS, pSp

        nxt = stage_a(0)
        for gb in range(6):
            g0 = 8 * gb
            Ug, ugb, pS, pSp = nxt
            wB, wBb = wload(s5w_scr[l, 2:4, :, g0:g0 + 8, :].rearrange("k p g m -> p k g m"), [128, 2, 8, 128])
            cosr, cosb = wload(rot_scr[l, 0, :, g0:g0 + 8, :], [128, 8, 128], f32=True)
            sinr, sinb = wload(rot_scr[l, 1, :, g0:g0 + 8, :], [128, 8, 128], f32=True)
            cosr2 = cosr.rearrange("p g k -> p (g k)")
            sinr2 = sinr.rearrange("p g k -> p (g k)")
            Hp = tmpB[:, 1, :].rearrange("p (g k) -> p g k", k=128)
            Yg = tmpB[:, 2, :].rearrange("p (g k) -> p g k", k=128)
            for half in range(2):
                hs = slice(half * 512, (half + 1) * 512)
                tt(tmpA[:, 0, hs], pS[half][0][:], cosr2[:, hs], ALU.mult, [pS[half][1], cosb], [tA[0]])
                tt(tmpA[:, 1, hs], pSp[half][0][:], sinr2[:, hs], ALU.mult, [pSp[half][1], sinb], [tA[1]])
                release_ps(pS[half][1])
                release_ps(pSp[half][1])
            tt(tmpA[:, 2, :], tmpA[:, 0, :], tmpA[:, 1, :], ALU.add, [tA[0], tA[1]], [tA[2]])
            if gb + 1 < 6:
                nxt = stage_a(gb + 1)
            G = tmpA[:, 3, :].rearrange("p (g k) -> p g k", k=128)
            Sr = tmpA[:, 2, :].rearrange("p (g k) -> p g k", k=128)
            for gi in range(8):
                g = g0 + gi
                dve(lambda: nc.vector.tensor_tensor_scan(out=G[:, gi, :], data0=r8[:, l, g:g + 1].to_broadcast([128, 128]),
                                                          data1=Sr[:, gi, :], initial=carry[:, l, g:g + 1],
                                                          op0=ALU.mult, op1=ALU.add),
                    [tA[2], carb, buf("r8")], [tA[3]])
            for half in range(2):
                hs = slice(half * 512, (half + 1) * 512)
                ps, pb = next_ps()
                mm(ps[:], jmat_f, tmpA[:, 3, hs], True, True, [cb, tA[3]], [pb])
                tt(tmpA[:, 0, hs], tmpA[:, 3, hs], cosr2[:, hs], ALU.mult, [tA[3], cosb], [tA[0]])
                tt(tmpA[:, 1, hs], ps[:], sinr2[:, hs], ALU.mult, [pb, sinb], [tA[1]])
            t1 = tmpA[:, 0, :].rearrange("p (g k) -> p g k", k=128)
            t2 = tmpA[:, 1, :].rearrange("p (g k) -> p g k", k=128)
            cpy(Hp[:, :, 0:1], carry[:, l, g0:g0 + 8].unsqueeze(2), [carb], [tBb[1]])
            tt(Hp[:, :, 1:128], t1[:, :, 0:127], t2[:, :, 0:127], ALU.subtract, [tA[0], tA[1]], [tBb[1]])
            tt(carry[:, l, g0:g0 + 8].unsqueeze(2), t1[:, :, 127:128], t2[:, :, 127:128], ALU.subtract, [tA[0], tA[1]], [carb])
            pY = [next_ps(hold=True) for _ in range(2)]
            for half in range(2):
                for g4 in range(4):
                    gi = half * 4 + g4
                    mm(pY[half][0][:, g4 * 128:(g4 + 1) * 128], wB[:, 0, gi, :], Ug[:, gi, :], True, False, [wBb, ugb], [pY[half][1]])
                    mm(pY[half][0][:, g4 * 128:(g4 + 1) * 128], wB[:, 1, gi, :], Hp[:, gi, :], False, True, [wBb, tBb[1]], [pY[half][1]])
            for half in range(2):
                act_op(tmpB[:, 2, half * 512:(half + 1) * 512], pY[half][0][:], AF.Gelu_apprx_tanh, [pY[half][1]], [tBb[2]])
                release_ps(pY[half][1])
            cv = cat[:, gb, :].rearrange("p (k s) -> p s k", s=8)
            for half in range(2):
                ps, pb = next_ps()
                for s4 in range(4):
                    s = half * 4 + s4
                    for gi in range(8):
                        mm(ps[:, s4 * 128:(s4 + 1) * 128], ew[:, s, 112 - 16 * gi:240 - 16 * gi], Yg[:, gi, :], gi == 0, gi == 7,
                           [cb, tBb[2]], [pb])
                evac(half, cv[:, half * 4:(half + 1) * 4, :], ps[:].rearrange("p (s k) -> p s k", k=128), [pb], [catb[gb][0], catb[gb][1]])
        w = wglu_d[l].rearrange("(k p) n -> p k n", p=128)
        for cc in range(2):
            wc, wcb = wload(w[:, :, cc * 384:(cc + 1) * 384], [128, 6, 384])
            for j3 in range(3):
                j = cc * 3 + j3
                for st in range(NST):
                    ps, pb = next_ps()
                    for k in range(6):
                        mm(ps[:], wc[:, k, j3 * 128:(j3 + 1) * 128], cat[:, k, SL[st]], k == 0, k == 5, [wcb, catb[k][st]], [pb])
                    sg = tmpA[:, st, 0:512]
                    act_op(sg, ps[:], AF.Sigmoid, [pb], [tA[st]])
                    tt(uT[:, j, SL[st]], cat[:, j, SL[st]], sg, ALU.mult, [catb[j][st], tA[st]], [ub[j][st]])
        return [(uT[:, k, :], ub[k]) for k in range(6)] + [(cat[:, 6 + k, :], catb[6 + k]) for k in range(2)]

    def diff_attn(l, tb):
        j = l - 2
        rope_inplace(uT, ub)
        own = [t for b_ in tA for t in ([b_.w] + list(b_.r.values())) if t is not None and t.val is not None]
        for e_ in (ACT, DVE, POOL):
            e_.wait(own)
        nkeys = (16 * tb + 16) * 128
        kvs = [(kvbuf, [hb[0], hb[1]]), (kvbuf2, [buf("kv2")])]

        def kv_views(hh):
            kb_, bb = kvs[hh % 2]
            return kb_[:, 0, :], kb_[:, 1, :].rearrange("p (b e) -> p b e", e=128), bb

        def load_kv(hh):
            KT, V, bb = kv_views(hh)
            cx.dma(SP, KT[:, 0:nkeys], kT_scr[hh * 128:(hh + 1) * 128, 0:nkeys], reads=[buf("kvscr")], writes=bb)
            cx.dma(SP, V[:, 0:nkeys // 128, :], v_scr[0:nkeys, hh * 128:(hh + 1) * 128].rearrange("(b p) e -> p b e", p=128),
                   reads=[buf("kvscr")], writes=bb)

        steps = []
        for hh in range(6):
            for st in range(NST):
                i = 2 * tb + st
                nkb = 8 * i + 8
                for kb in range(nkb):
                    for c in range(2):
                        steps.append((hh, st, i, nkb, kb, c))
        eaccs = [[tmpA[:, 3, 0:512], tmpA[:, 3, 512:1024]], [tmpA[:, 0, 512:1024], tmpA[:, 1, 512:1024]]]
        eabs = [[buf("eacc00"), buf("eacc01")], [buf("eacc10"), buf("eacc11")]]
        PRE = 3
        sps = {}
        nE = 0
        load_kv(0)

        def col0(i_, kb_):
            r_ = kb_ - 8 * i_
            return (r_ // 2) * 128 if r_ >= 0 else 0

        def issue_s_pair(n0):
            todo = []
            toks = []
            for n_ in (n0, n0 + 1):
                hh_, st_, i_, nkb_, kb_, c_ = steps[n_]
                KT, V, bb = kv_views(hh_)
                pss, psb = next_ps()
                toks += cx.deps(bb + [ub[hh_][st_]], [psb])
                todo.append((n_, pss, psb, KT, bb, hh_, st_, col0(i_, kb_), kb_, c_))
            PE.wait(toks)
            for n_, pss, psb, KT, bb, hh_, st_, c0, kb_, c_ in todo:
                mm(pss[:, c0:512], KT[c_ * 64:(c_ + 1) * 64, kb_ * 128:(kb_ + 1) * 128],
                   uT[c_ * 64:(c_ + 1) * 64, hh_, st_ * 512 + c0:(st_ + 1) * 512],
                   True, True, bb + [ub[hh_][st_]], [psb])
                sps[n_] = (pss, psb)

        issue_s_pair(0)
        pO = None
        pending_ep = []
        for n_, (hh, st, i, nkb, kb, c) in enumerate(steps):
            blk = (hh * NST + st) % 2
            eacc, eab = eaccs[blk], eabs[blk]
            KT, V, kvb = kv_views(hh)
            if kb == 0 and c == 0:
                pO = [next_ps(hold=True) for _ in range(2)]
                if st == 0 and hh + 1 < 6:
                    load_kv(hh + 1)
            if c == 0 and n_ + 2 < len(steps):
                issue_s_pair(n_ + 2)
            pss, psb = sps.pop(n_)
            c0 = col0(i, kb)
            e = tmpB[:, nE % 6, 0:512]
            eb = tBb[nE % 6]
            nE += 1
            act_op(e[:, c0:512], pss[:, c0:512], AF.Exp, [psb], [eb], scale=0.125)
            if kb >= 8 * i:
                tt(e[:, c0:c0 + 128], e[:, c0:c0 + 128], masks[:, kb - 8 * i, :], ALU.mult, [eb, cb], [eb])
            if c == 0:
                pv0 = (e, eb)
            else:
                e0_, eb0_ = pv0
                PE.wait(cx.deps(kvb + [eb0_], [pO[0][1]]) + cx.deps(kvb + [eb], [pO[1][1]]))
                mm(pO[0][0][:, c0:512], V[:, kb, :], e0_[:, c0:512], kb == 0, kb == nkb - 1, kvb + [eb0_], [pO[0][1]])
                mm(pO[1][0][:, c0:512], V[:, kb, :], e[:, c0:512], kb == 0, kb == nkb - 1, kvb + [eb], [pO[1][1]])
            if c == 0:
                if kb == 0:
                    cpy(eacc[c], e, [eb], [eab[c]])
                else:
                    tt(eacc[c][:, c0:512], eacc[c][:, c0:512], e[:, c0:512], ALU.add, [eb, eab[c]], [eab[c]])
            else:
                ec_, e_ = eacc[c][:, c0:512], e[:, c0:512]
                if kb == 0:
                    cx.op(POOL, lambda: nc.gpsimd.tensor_copy(out=ec_, in_=e_), reads=[eb], writes=[eab[c]])
                else:
                    cx.op(POOL, lambda: nc.gpsimd.tensor_tensor(out=ec_, in0=ec_, in1=e_, op=ALU.add), reads=[eb, eab[c]], writes=[eab[c]])
            if pending_ep:
                pending_ep.pop(0)()
            if not (kb == nkb - 1 and c == 1):
                continue
            o0, o1, r_ = tmpA[:, 0, 0:512], tmpA[:, 1, 0:512], tmpA[:, 2, 0:512]
            for c2 in range(2):
                act_op((o0, o1)[c2], pO[c2][0][:], AF.Identity, [pO[c2][1]], [buf(f"o{c2}")])
                release_ps(pO[c2][1])

            def stage1(c2, eacc=eacc, eab=eab):
                pl, plb = next_ps()
                mm(pl[:], ones_f, eacc[c2], True, True, [cb, eab[c2]], [plb])
                dve(lambda: nc.vector.reciprocal(out=r_, in_=pl[:]), [plb], [tA[2]])
                tt((o0, o1)[c2], (o0, o1)[c2], r_, ALU.mult, [buf(f"o{c2}"), tA[2]], [buf(f"o{c2}")])

            def stage2():
                stt(o0, o1, lam[:, j, 1:2], o0, ALU.mult, ALU.add, [buf("o0"), buf("o1"), buf("lam")], [buf("o0")])
                act_op(sq[:, 0, :], o0, AF.Square, [buf("o0")], [buf("sq")])

            def stage3():
                ps_, pb_ = next_ps()
                mm(ps_[:], ones_b, sq[:, 0, :], True, True, [buf("sq"), cb], [pb_])
                act_op(r_, ps_[:], AF.Sqrt, [pb_], [tA[2]], scale=1.0 / 128, bias=EPS)

            def stage4(hh=hh, st=st):
                dve(lambda: nc.vector.reciprocal(out=r_, in_=r_), [tA[2]], [tA[2]])
                stt(cat[:, hh, SL[st]], o0, lam[:, j, 2:3], r_, ALU.mult, ALU.mult, [buf("o0"), tA[2], buf("lam")], [catb[hh][st]])

            nop = lambda: None
            pending_ep.extend([nop, lambda f=stage1: f(0), nop, lambda f=stage1: f(1), nop, stage2, nop, stage3, nop, stage4])
        while pending_ep:
            pending_ep.pop(0)()
        for b_ in [buf("o0"), buf("o1")] + eabs[0] + eabs[1]:
            for t in [b_.w] + list(b_.r.values()):
                if t is None or t.val is None:
                    continue
                for k_ in range(4):
                    cur = tA[k_].r.get(t.key)
                    if cur is None or cur.val is None or cur.val < t.val:
                        tA[k_].r[t.key] = t
        return [(cat[:, k, :], catb[k]) for k in range(8)]

    def kv_tile(ta):
        t0 = ta * TT
        norm_x(13)
        w = wkv_d.rearrange("(k p) n -> p k n", p=128)
        n = 0
        for cc in range(3):
            wc, wcb = wload(w[:, :, cc * 256:(cc + 1) * 256], [128, 8, 256])
            for half in range(2):
                kc = 2 * cc + half
                for st in range(NST):
                    ps, pb = next_ps()
                    for k in range(8):
                        mm(ps[:], wc[:, k, half * 128:(half + 1) * 128], hT[:, k, SL[st]], k == 0, k == 7, [wcb, hb[st]], [pb])
                    evac(n, uT[:, kc, SL[st]], ps[:], [pb], [ub[kc][st]])
                    n += 1
        rope_inplace(uT, ub)
        allub = [ub[k][st] for k in range(6) for st in range(NST)]
        kv_toks.append(cx.dma(SP, kT_scr[:, t0:t0 + TT].rearrange("(k p) t -> p k t", p=128), uT, reads=allub, writes=[buf("kvscr")]))
        act_guard.append(kv_toks[-1])
        vt = cat.rearrange("p k t -> p (k t)")[:, 0:6144].rearrange("p (b e) -> p b e", e=768)
        allcat = [catb[k][st] for k in range(8) for st in range(NST)]
        for cc in range(2):
            wc, wcb = wload(w[:, :, 768 + cc * 384:768 + (cc + 1) * 384], [128, 8, 384])
            for tbk in range(8):
                ps, pb = next_ps()
                st = tbk // 4
                for k in range(8):
                    mm(ps[:, 0:384], hT[:, k, tbk * 128:(tbk + 1) * 128], wc[:, k, :], k == 0, k == 7, [wcb, hb[st]], [pb])
                evac(n, vt[:, tbk, cc * 384:(cc + 1) * 384], ps[:, 0:384], [pb], [catb[tbk][cc]])
                n += 1
        kv_toks.append(cx.dma(SP, v_scr[t0:t0 + TT, :].rearrange("(b p) e -> p b e", p=128), vt, reads=allcat, writes=[buf("kvscr")]))
        act_guard.append(kv_toks[-1])

    kv_toks = []
    scr_toks = []

    def select_store(ta):
        xv = xT[:].rearrange("p k (m two j) -> p k m two j", two=2, j=128)
        ov = tmpA[:].rearrange("p a (k2 m j) -> p (a k2) m j", k2=2, j=128)
        ts(ov, xv[:, :, :, 0, :], flags[:, 0:1], None, ALU.mult, None, xb + [cb], tA)
        stt(ov, xv[:, :, :, 1, :], flags[:, 1:2], ov, ALU.mult, ALU.add, xb + [cb] + tA, tA)
        kv_toks.append(cx.dma(SP, x1_scr[:, ta * 512:(ta + 1) * 512].rearrange("(k p) t -> p k t", p=128),
                              tmpA[:].rearrange("p a (k2 t) -> p (a k2) t", k2=2), reads=tA, writes=[buf("x1scr")]))

    memf = tmpA[:].rearrange("p a t -> p (a t)")[:, 0:2048].rearrange("p (k m) -> p k m", m=256)
    memn = tmpB[:, 0:2, :].rearrange("p a t -> p (a t)").rearrange("p (k m) -> p k m", m=256)
    mfb, mnb = buf("memf"), buf("memn")
    cx.dma(SP, memf, memT_d.rearrange("(k p) m -> p k m", p=128), writes=[mfb])
    rmsnorm(memf, mfb, gains[:, 12, :], memn, mnb, 256, 0)
    n = 0
    for l4 in range(4):
        w = wmkv_d[l4].rearrange("(k p) n -> p k n", p=128)
        wk, wkb = wload(w[:, :, 0:256], [128, 8, 256])
        wv, wvb = wload(w[:, :, 256:512], [128, 8, 256])
        for c in range(2):
            ps, pb = next_ps()
            for k in range(8):
                mm(ps[:, 0:256], wk[:, k, c * 128:(c + 1) * 128], memn[:, k, :], k == 0, k == 7, [wkb, mnb], [pb])
            evac(n, mkT[:, l4, c, :], ps[:, 0:256], [pb], [buf("mkv")])
            n += 1
        for mb in range(2):
            ps, pb = next_ps()
            for k in range(8):
                mm(ps[:, 0:256], memn[:, k, mb * 128:(mb + 1) * 128], wv[:, k, :], k == 0, k == 7, [wvb, mnb], [pb])
            evac(n, mvv[:, l4, mb, :], ps[:, 0:256], [pb], [buf("mkv")])
            n += 1
    lq = tmpA[:, 2, 0:512].rearrange("p (j a d) -> p j a d", j=2, a=4)
    lb = buf("lam")
    cx.dma(SP, lq, lqk_d, writes=[tA[2]])
    for j in range(2):
        for a_ in range(2):
            tt(tmpA[:, 3, 0:64], lq[:, j, 2 * a_, :], lq[:, j, 2 * a_ + 1, :], ALU.mult, [tA[2]], [tA[3]])
            dve(lambda: nc.vector.reduce_sum(out=lam[:, j, 2 + a_:3 + a_], in_=tmpA[:, 3, 0:64], axis=mybir.AxisListType.X), [tA[3]], [lb])
            act_op(lam[:, j, 2 + a_:3 + a_], lam[:, j, 2 + a_:3 + a_], AF.Exp, [lb], [lb])
        tt(lam[:, j, 0:1], lam[:, j, 2:3], lam[:, j, 3:4], ALU.subtract, [lb], [lb])
        ts(lam[:, j, 1:2], lam[:, j, 0:1], -1.0, -LAM_INIT[2 + j], ALU.mult, ALU.add, [lb], [lb])
        ts(lam[:, j, 2:3], subln[:, j:j + 1], 1.0 - LAM_INIT[2 + j], None, ALU.mult, None, [lb, cb], [lb])

    xflat = xT[:].rearrange("p k t -> p (k t)")
    pp = xflat[:, 0:1152].rearrange("p (a g) -> p a g", g=NG)
    BRI = xflat[:, 1152:2688].rearrange("p (a g c) -> p a g c", a=2, c=16)
    CRI = xflat[:, 2688:4224].rearrange("p (a g c) -> p a g c", a=2, c=16)
    trtf = xflat[:, 4224:7296]
    trt = trtf.rearrange("p (a g) -> p a g", g=NG)
    DCOL = xflat[:, 7296:7344]
    tmpAf = tmpA[:].rearrange("p a t -> p (a t)")
    BRAW = tmpAf[:, 1024:2560].rearrange("p (a g c) -> p a g c", a=2, c=16)
    actf = act[:, 0:16, :].rearrange("p k t -> p (k t)").bitcast(F32)
    hTf = hT[:].rearrange("p k t -> p (k t)").bitcast(F32)
    sqf = sq[:].rearrange("p k t -> p (k t)").bitcast(F32)
    ppb = buf("pp")
    hsb = buf("hTscr")
    acb = buf("actscr")
    sqb_ = buf("sq")
    for l in range(2):
        R, W = [ppb], [ppb]
        cx.dma(SP, pp[:, 0:3, :], s5a_d[:, l, :, :], writes=W)
        cx.dma(SP, BRAW, s5b_d[:, l], writes=tA)
        cx.dma(SP, CRI, s5c_d[:, l], writes=W)
        cx.dma(SP, DCOL, s5d_d[:, l, :], writes=W)
        LR, LI, LDT, DT, LRDT, TH, ANG, TF, SN, MG, LAMR, LAMI, NR, DEN, FR, FI, T1, T2, TH8 = [pp[:, i, :] for i in range(19)]
        TI = pp[:, 19, :].bitcast(I32)
        act_op(DT, LDT, AF.Exp, R, W)
        tt(LRDT, LR, DT, ALU.mult, R, W)
        tt(TH, LI, DT, ALU.mult, R, W)
        ts(TH8, TH, 8.0, None, ALU.mult, None, R, W)

        def trig(out, n_, psi):
            ts(ANG, TH, float(n_), psi, ALU.mult, ALU.add, R + [cb], W)
            sin_rr(SN, ANG, TF, TI, R, W)
            act_op(MG, LRDT, AF.Exp, R, W, scale=float(n_))
            tt(out, SN, MG, ALU.mult, R, W)

        trig(LAMR, 1, CC["halfpi"])
        trig(LAMI, 1, CC["zero"])
        ts(NR, LAMR, -1.0, None, ALU.add, None, R, W)
        tt(T1, LR, LR, ALU.mult, R, W)
        tt(T2, LI, LI, ALU.mult, R, W)
        tt(DEN, T1, T2, ALU.add, R, W)
        dve(lambda: nc.vector.reciprocal(out=DEN, in_=DEN), R, W)
        tt(T1, NR, LR, ALU.mult, R, W)
        tt(T2, LAMI, LI, ALU.mult, R, W)
        tt(FR, T1, T2, ALU.add, R, W)
        tt(FR, FR, DEN, ALU.mult, R, W)
        tt(T1, LAMI, LR, ALU.mult, R, W)
        tt(T2, NR, LI, ALU.mult, R, W)
        tt(FI, T1, T2, ALU.subtract, R, W)
        tt(FI, FI, DEN, ALU.mult, R, W)
        bc = lambda a_: a_.unsqueeze(2).to_broadcast([128, NG, 16])
        tb_ = tmpAf[:, 0:768].rearrange("p (g c) -> p g c", c=16)
        RA, WA = R + tA, W + tA
        tt(BRI[:, 0], BRAW[:, 0], bc(FR), ALU.mult, RA, WA)
        tt(tb_, BRAW[:, 1], bc(FI), ALU.mult, RA, WA)
        tt(BRI[:, 0], BRI[:, 0], tb_, ALU.subtract, RA, WA)
        tt(BRI[:, 1], BRAW[:, 1], bc(FR), ALU.mult, RA, WA)
        tt(tb_, BRAW[:, 0], bc(FI), ALU.mult, RA, WA)
        tt(BRI[:, 1], BRI[:, 1], tb_, ALU.add, RA, WA)
        NV = tabc[:, 0, :]
        PSV = tabc[:, 1, :]
        b3 = lambda a_: a_.unsqueeze(2).to_broadcast([128, 64, NG])
        g3 = lambda a_: a_.unsqueeze(1).to_broadcast([128, 64, NG])
        ANGA = actf[:, 0:3072]
        TFA = actf[:, 3072:6144]
        MGA = hTf[:, 0:3072]
        TIA = tmpAf[:, 0:3072].bitcast(I32)
        a3 = lambda a_: a_.rearrange("p (a g) -> p a g", g=NG)
        RT, WT = R + [acb, hsb, cb] + tA, W + [acb, hsb] + tA
        tt(a3(ANGA), g3(TH), b3(NV), ALU.mult, RT, WT)
        tt(a3(ANGA), a3(ANGA), b3(PSV), ALU.add, RT, WT)
        sin_rr(trtf, ANGA, TFA, TIA, RT, WT)
        tt(a3(MGA), g3(LRDT), b3(NV), ALU.mult, RT, WT)
        act_op(MGA, MGA, AF.Exp, RT, WT)
        tt(trtf, trtf, MGA, ALU.mult, RT, WT)
        act_op(r8[:, l, :], LRDT, AF.Exp, R, [buf("r8")], scale=8.0)
        dve(lambda: nc.vector.memset(carry[:, l, :], 0.0), [], [buf(f"carry{l}")])
        for gb in range(6):
            g0 = 8 * gb
            gs = slice(g0, g0 + 8)
            mats = [tmpA[:, i, :].rearrange("p (g s c) -> p g s c", s=8, c=16) for i in range(4)]
            scr = sqf[:, 1024:2048].rearrange("p (g s c) -> p g s c", s=8, c=16)
            srcs_ = [(BRI, 0), (BRI, 2), (CRI, 4), (CRI, 6)]
            for mi in range(4):
                dat, st0 = srcs_[mi]
                A0 = dat[:, 0, gs, :].unsqueeze(2).to_broadcast([128, 8, 8, 16])
                A1 = dat[:, 1, gs, :].unsqueeze(2).to_broadcast([128, 8, 8, 16])
                Q0 = trt[:, st0 * 8:(st0 + 1) * 8, gs].rearrange("p s g -> p g s").unsqueeze(3).to_broadcast([128, 8, 8, 16])
                Q1 = trt[:, (st0 + 1) * 8:(st0 + 2) * 8, gs].rearrange("p s g -> p g s").unsqueeze(3).to_broadcast([128, 8, 8, 16])
                tt(mats[mi], A0, Q0, ALU.mult, R, [tA[mi]])
                tt(scr, A1, Q1, ALU.mult, R, [sqb_])
                tt(mats[mi], mats[mi], scr, ALU.add, [tA[mi], sqb_], [tA[mi]])
            W4 = tmpB[:, 2:6, :].rearrange("p a (g m) -> p a g m", m=128)
            w4b = tBb[2:6]
            Zf, Z7f, Xf, X1f = [tmpA[:, i, :].rearrange("p (g m) -> p g m", m=128) for i in range(4)]
            Tm = sqf[:, 0:1024].rearrange("p (g m) -> p g m", m=128)
            for half in range(2):
                ps, pb = next_ps()
                for g4 in range(4):
                    gi = half * 4 + g4
                    mm(ps[:, g4 * 128:(g4 + 1) * 128], Zf[:, gi, :], Xf[:, gi, :], True, True, [tA[0], tA[2]], [pb])
                tt(Tm[:, half * 4:(half + 1) * 4, :], ps[:].rearrange("p (g m) -> p g m", m=128),
                   maskT_f.unsqueeze(1).to_broadcast([128, 4, 128]), ALU.mult, [pb, cb], [sqb_])
                for g4 in range(4):
                    gi = half * 4 + g4
                    stt(W4[:, 2, gi, :], ident_f, DCOL[:, g0 + gi:g0 + gi + 1], Tm[:, gi, :], ALU.mult, ALU.add, [cb, sqb_] + R, w4b)
                for kind, rhs_ in ((0, ident_f), (1, jmat_f)):
                    ps, pb = next_ps()
                    for g4 in range(4):
                        gi = half * 4 + g4
                        mm(ps[:, g4 * 128:(g4 + 1) * 128], Z7f[:, gi, :], rhs_, True, True, [tA[1], cb], [pb])
                    evac(kind, W4[:, kind, half * 4:(half + 1) * 4, :], ps[:].rearrange("p (g m) -> p g m", m=128), [pb], w4b)
            cpy(W4[:, 3], X1f, [tA[3]], w4b)
            scr_toks.append(cx.dma(SP, s5w_scr[l, :, :, g0:g0 + 8, :].rearrange("k p g m -> p k g m"), W4, reads=w4b, writes=[buf("s5scr")]))
            ang2 = tmpAf[:, 0:2048].rearrange("p (w g k) -> p w g k", w=2, k=128)
            tt(ang2[:, 1], jv_f.unsqueeze(1).to_broadcast([128, 8, 128]), TH8[:, gs].unsqueeze(2).to_broadcast([128, 8, 128]),
               ALU.mult, R + [cb] + tA, tA)
            ts(ang2[:, 0], ang2[:, 1], PI / 2, None, ALU.add, None, tA, tA)
            sin_rr(tmpAf[:, 2048:4096], tmpAf[:, 0:2048], hTf[:, 0:2048], hTf[:, 2048:4096].bitcast(I32), tA + [hsb], tA + [hsb])
            scr_toks.append(cx.dma(SP, rot_scr[l, :, :, g0:g0 + 8, :].rearrange("w p g k -> p w g k"),
                                   tmpAf[:, 2048:4096].rearrange("p (w g k) -> p w g k", w=2, k=128), reads=tA, writes=[buf("s5scr")]))
    POOL.wait(scr_toks)
    bar = [Tok(e.sem, e.count, e.key) for e in (PE, ACT, DVE) if e.count > 0]
    SP.wait(bar)

    out_toks = []
    xsrc = xT_d.rearrange("(k p) t -> p k t", p=128)
    for ta in range(na_tiles):
        for st in range(NST):
            cx.dma(SP, xT[:, :, SL[st]], xsrc[:, :, ta * TT + st * 512: ta * TT + (st + 1) * 512], writes=[xb[st]])
        rope_tables(pos_all_d[:, ta * TT:(ta + 1) * TT])
        for l in range(2):
            ffn(l, 0)
            if (f"ffn1_{l}", ta) in dbgs:
                out_toks.append(cx.dma(SP, dbgs[(f"ffn1_{l}", ta)], xT[:].rearrange("p k t -> p (k t)"), reads=xb))
            in_proj(l)
            mem_attn(l)
            srcs = s5_mixer(l)
            if (f"tokout_{l}", ta) in dbgs:
                out_toks.append(cx.dma(SP, dbgs[(f"tokout_{l}", ta)], act[:, 0:16, :].rearrange("p k t -> p (k t)"),
                                       reads=[b_ for s_ in srcs for b_ in s_[1]]))
            out_proj(l, srcs)
            ffn(l, 1)
            if (f"x_{l}", ta) in dbgs:
                out_toks.append(cx.dma(SP, dbgs[(f"x_{l}", ta)], xT[:].rearrange("p k t -> p (k t)"), reads=xb))
        kv_tile(ta)
        select_store(ta)

    SP.wait(kv_toks)
    x1src = x1_scr.rearrange("(k p) t -> p k t", p=128)
    for tb in range(nb_tiles):
        for st in range(NST):
            cx.dma(SP, xT[:, :, SL[st]], x1src[:, :, tb * TT + st * 512: tb * TT + (st + 1) * 512],
                   reads=[buf("x1scr")], writes=[xb[st]])
        rope_tables(pos_mine_d[:, tb * TT:(tb + 1) * TT])
        for l in range(2, 4):
            ffn(l, 0)
            in_proj(l)
            mem_attn(l)
            srcs = diff_attn(l, tb)
            if (f"tokout_{l}", tb) in dbgs:
                out_toks.append(cx.dma(SP, dbgs[(f"tokout_{l}", tb)], act[:, 0:16, :].rearrange("p k t -> p (k t)"),
                                       reads=[b_ for s_ in srcs for b_ in s_[1]]))
            out_proj(l, srcs)
            ffn(l, 1)
        for st in range(NST):
            of = tmpA[:].rearrange("p a (k2 t) -> p (a k2) t", k2=2)
            rmsnorm(xT[:, :, SL[st]], xb[st], gains[:, 14, :], of, list(tA), 512, st)
            out_toks.append(cx.dma(SP, outT_d[:, tb * TT + st * 512: tb * TT + (st + 1) * 512].rearrange("(k p) t -> p k t", p=128),
                                   of, reads=list(tA)))

    SP.wait(out_toks)


def _consts():
    cst = np.zeros((128, 16), np.float32)
    hp = PI / 2
    lo, hi = slice(0, 64), slice(64, 128)
    cst[lo, 0], cst[hi, 0] = 0 + hp, -hp + hp
    cst[lo, 1], cst[hi, 1] = hp + hp, 0 + hp
    cst[lo, 2], cst[hi, 2] = 0 + hp, hp + hp
    cst[lo, 3], cst[hi, 3] = hp + hp, PI + hp
    inv = (500000.0 ** (-np.arange(0, 16, 2, dtype=np.float32) / 16)).astype(np.float32)
    for base in (0, 64):
        cst[base:base + 8, 4] = inv
        cst[base + 8:base + 16, 4] = inv
        cst[base:base + 8, 5] = -inv
        cst[base + 8:base + 16, 5] = inv
    cst[:, 6] = hp
    ew = np.zeros((128, 8, 240), np.float32)
    for gi in range(8):
        for c in range(16):
            ew[16 * gi + c, gi, 112 + c] = 1.0
    cm = np.zeros((128, 5, 128), np.float32)
    cm[:, 0, :] = np.eye(128)
    for p in range(64):
        cm[64 + p, 1, p] = 1.0
        cm[p, 1, 64 + p] = -1.0
    sidx = np.arange(128) // 16
    cm[:, 2, :] = (sidx[None, :] >= sidx[:, None]).astype(np.float32)
    cm[:, 3, :] = (np.arange(128, dtype=np.float32) + 1.0)[None, :]
    cm[:, 4, :] = 1.0
    cmb = np.zeros((128, 2, 128), np.float32)
    cmb[:, 0, :] = 1.0
    for base in (0, 64):
        for j in range(8):
            cmb[base + 8 + j, 1, base + j] = 1.0
            cmb[base + j, 1, base + 8 + j] = 1.0
    tabc = np.zeros((128, 2, 64), np.float32)
    sets = [(lambda s_: -s_, 0), (lambda s_: -s_, 1), (lambda s_: 7 - s_, 0), (lambda s_: 7 - s_, 1),
            (lambda s_: s_, 2), (lambda s_: s_, 3), (lambda s_: s_ + 1, 2), (lambda s_: s_ + 1, 3)]
    for k_, (fn, pc) in enumerate(sets):
        for s_ in range(8):
            tabc[:, 0, k_ * 8 + s_] = fn(s_)
            tabc[:, 1, k_ * 8 + s_] = cst[:, pc]
    return cst, ew.astype(ml_dtypes.bfloat16), cm, cmb.astype(ml_dtypes.bfloat16), tabc


def _masks(h):
    m = np.zeros((128, 8, 128), np.float32)
    s = np.arange(128)[:, None]
    for r in range(8):
        jj = r // 2
        key = r * 128 + s
        qry = (2 * jj + h) * 128 + np.arange(128)[None, :]
        m[:, r, :] = (key <= qry)
    return m.astype(ml_dtypes.bfloat16)


def make_in_maps(inputs):
    f = lambda a: np.ascontiguousarray(np.asarray(a))
    cst, ew, cm, cmb, tabc = _consts()
    gains = np.stack([*inputs["ln_ffn1"], *inputs["ln_mix"], *inputs["ln_ffn2"],
                      inputs["ln_mem"], inputs["ln_kv"], inputs["ln_final"]], 0)
    gains = f(gains.reshape(15, 8, 128).transpose(2, 0, 1))
    rep = lambda a: np.concatenate([a, a], 0)
    s5a = np.stack([np.stack([inputs["s5_a_re"][l].T, inputs["s5_a_im"][l].T,
                              np.broadcast_to(inputs["s5_log_dt"][l][None, :], (64, NG))], 0) for l in range(2)], 0)
    s5a = f(rep(s5a.transpose(2, 0, 1, 3)))
    s5b = np.stack([np.stack([inputs["s5_b_re"][l], inputs["s5_b_im"][l]], 0) for l in range(2)], 0)
    s5b = f(rep(s5b.transpose(3, 0, 1, 2, 4)))
    s5c = np.stack([np.stack([inputs["s5_c_re"][l], inputs["s5_c_im"][l]], 0) for l in range(2)], 0)
    s5c = f(rep(s5c.transpose(4, 0, 1, 2, 3)))
    s5d = f(np.tile(inputs["s5_d"].transpose(2, 0, 1), (8, 1, 1)))
    lqk = np.stack([np.stack([inputs["diff_lq1"][j], inputs["diff_lk1"][j], inputs["diff_lq2"][j],
                              inputs["diff_lk2"][j]], 0) for j in range(2)], 0)
    lqk = f(np.broadcast_to(lqk[None], (128, 2, 4, 64)))
    subln = f(inputs["diff_subln"].T)
    shared = dict(gains=gains, s5a=s5a, s5b=s5b, s5c=s5c, s5d=s5d, lqk=lqk, subln=subln,
                  cst=cst, ew=ew, cm=cm, cmb=cmb, tabc=tabc)
    for k in ["ffn1_in", "ffn2_in", "ffn1_out", "ffn2_out", "w_mix_in", "w_mix_out", "w_mem_kv",
              "s5_w_glu", "w_kv_shared"]:
        shared[k] = f(inputs[k])
    in_maps = []
    for c in range(8):
        b, h = c // 2, c % 2
        m = dict(shared)
        m["xT"] = f(inputs["x"][b].T)
        m["memT"] = f(inputs["mem"][b].T)
        pos = np.asarray(inputs["positions"][b]).astype(np.int32)
        m["pos_all"] = f(np.broadcast_to(pos[None, :], (128, S)))
        mine = pos.reshape(16, 2, 128)[:, h, :].reshape(-1)
        m["pos_mine"] = f(np.broadcast_to(mine[None, :], (128, S // 2)))
        m["masks"] = _masks(h)
        fl = np.zeros((128, 2), np.float32)
        fl[:, 0] = 1 - h
        fl[:, 1] = h
        m["flags"] = fl
        in_maps.append(m)
    return in_maps


def kernel(**inputs):
    in_maps = make_in_maps(inputs)
    nc = build_program()
    res = run_bass_kernel_spmd(nc, in_maps, core_ids=list(range(8)))
    out = np.zeros((NB, S, D), np.float32)
    for c in range(8):
        b, h = c // 2, c % 2
        o = np.asarray(res.results[c]["outT"]).T.reshape(16, 128, D)
        out[b].reshape(16, 2, 128, D)[:, h] = o
    return out
```
